# Optimizing a Trainium2 kernel written in Bass

```python
import jax, jax.numpy as jnp
from jax import lax
import numpy as np

D_MODEL = 1024
BATCH = 8
SEQ = 2048
DEPTH = 4
DEC_BATCH = 128
DEC_SEQ = 4
PAST_LEN = 16384
PAGE_SIZE = 128

D_MIX = D_MODEL
D_POOL = D_MIX // 4
POOL_WINDOWS = (2, 4, 8, 16)
POOL_GROUPS = len(POOL_WINDOWS)
POOL_GC = D_POOL // POOL_GROUPS
POOL_PAST = max(POOL_WINDOWS) - 1
D_SCONV = (D_MIX - D_POOL) // 2
D_CCONV = D_MIX - D_POOL - D_SCONV
SCONV_K = 3
CCONV_K = 31
FFN_K = 3
D_FF = ((8 * D_MODEL // 3 + 127) // 128) * 128
D_IN = D_POOL + 3 * D_SCONV + 2 * D_CCONV
EPS = 1e-6

kernel_name = "hybrid_pool_shortconv_conformer_decoder_step"


def rms_norm(x, g):
    xf = x.astype(jnp.float32)
    y = xf * lax.rsqrt(jnp.mean(xf * xf, axis=-1, keepdims=True) + EPS)
    return (y * g.astype(jnp.float32)).astype(x.dtype)


def layer_norm(x, g, b):
    xf = x.astype(jnp.float32)
    mu = jnp.mean(xf, axis=-1, keepdims=True)
    var = jnp.mean(jnp.square(xf - mu), axis=-1, keepdims=True)
    y = (xf - mu) * lax.rsqrt(var + EPS) * g.astype(jnp.float32) + b.astype(jnp.float32)
    return y.astype(x.dtype)


def causal_dwconv(z, past, w):
    ext = jnp.concatenate([past.astype(z.dtype), z], axis=1)
    y = lax.conv_general_dilated(ext, w.astype(z.dtype)[:, None, :], (1,), 'VALID',
                                 dimension_numbers=('NWC', 'WIO', 'NWC'),
                                 feature_group_count=z.shape[-1])
    return y, ext[:, -(w.shape[0] - 1):]


def pool_mixer(u, past, pos0, pool_w, pool_scale):
    bsz, L, _ = u.shape
    ext = jnp.concatenate([past.astype(u.dtype), u], axis=1).astype(jnp.float32)
    csum = jnp.concatenate([jnp.zeros_like(ext[:, :1]), jnp.cumsum(ext, axis=1)], axis=1)
    P = POOL_PAST
    pos = pos0 + jnp.arange(L)
    uf = u.astype(jnp.float32)
    outs = []
    for g, w in enumerate(POOL_WINDOWS):
        sl = slice(g * POOL_GC, (g + 1) * POOL_GC)
        win = csum[:, P + 1:P + 1 + L, sl] - csum[:, P + 1 - w:P + 1 - w + L, sl]
        cnt = jnp.minimum(pos + 1, w).astype(jnp.float32)[None, :, None]
        outs.append(win / cnt - uf[..., sl])
    pooled = jnp.stack(outs, axis=2)
    mixed = jnp.einsum('blgc,gcd->blgd', pooled, pool_w.astype(jnp.float32)).reshape(bsz, L, D_POOL)
    out = mixed * pool_scale.astype(jnp.float32)
    return out.astype(u.dtype), ext[:, -P:].astype(u.dtype)


def layer(x, c, past_pool, past_sconv, past_cconv, past_ffn, pos0,
          w_ada, b_ada, g_mix_pre, g_mix_post, g_ffn_pre, g_ffn_post,
          w_in, pool_w, pool_scale, sconv_w, cconv_w, cconv_b, cln_g, cln_b,
          w_out, w_up, ffn_conv_w, w_down):
    mod = (jax.nn.silu(c) @ w_ada + b_ada)[:, None, :]
    sh1, sc1, gt1, sh2, sc2, gt2 = jnp.split(mod, 6, axis=-1)

    h = rms_norm(x, g_mix_pre) * (1 + sc1) + sh1
    proj = h @ w_in
    o = np.cumsum([D_POOL, D_SCONV, D_SCONV, D_SCONV, D_CCONV])
    u_pool, h_b, bg, cg, a_c, b_c = jnp.split(proj, list(o), axis=-1)
    y_a, new_pool = pool_mixer(u_pool, past_pool, pos0, pool_w, pool_scale)
    z_b = cg * h_b
    conv_b, new_sconv = causal_dwconv(z_b, past_sconv, sconv_w)
    y_b = bg * conv_b
    glu = a_c * jax.nn.sigmoid(b_c)
    conv_c, new_cconv = causal_dwconv(glu, past_cconv, cconv_w)
    y_c = jax.nn.silu(layer_norm(conv_c + cconv_b, cln_g, cln_b))
    mix = jnp.concatenate([y_a, y_b, y_c], axis=-1) @ w_out
    x = x + gt1 * rms_norm(mix, g_mix_post)

    h = rms_norm(x, g_ffn_pre) * (1 + sc2) + sh2
    up = h @ w_up
    up_c, new_ffn = causal_dwconv(up, past_ffn, ffn_conv_w)
    gate, val = jnp.split(up_c, 2, axis=-1)
    ff = (jax.nn.silu(gate) * val) @ w_down
    x = x + gt2 * rms_norm(ff, g_ffn_post)
    return x, new_pool, new_sconv, new_cconv, new_ffn


def setup_inputs(seed: int = 0) -> dict:
    key = jax.random.key(seed)
    ks = iter(jax.random.split(key, 32))

    def nrm(shape, s):
        return jax.random.normal(next(ks), shape, jnp.float32) * s

    return {
        "x_prompt": nrm((BATCH, SEQ, D_MODEL), 1.0),
        "x_sample": nrm((DEC_BATCH, DEC_SEQ, D_MODEL), 1.0),
        "c_prompt": nrm((BATCH, D_MODEL), 1.0),
        "c_sample": nrm((DEC_BATCH, D_MODEL), 1.0),
        "state_pool": nrm((DEPTH, DEC_BATCH, POOL_PAST, D_POOL), 1.0),
        "state_sconv": nrm((DEPTH, DEC_BATCH, SCONV_K - 1, D_SCONV), 1.0),
        "state_cconv": nrm((DEPTH, DEC_BATCH, CCONV_K - 1, D_CCONV), 0.5),
        "state_ffn": nrm((DEPTH, DEC_BATCH, FFN_K - 1, 2 * D_FF), 1.0),
        "w_ada": nrm((DEPTH, D_MODEL, 6 * D_MODEL), 0.5 * D_MODEL ** -0.5),
        "b_ada": nrm((DEPTH, 6 * D_MODEL), 0.01),
        "g_mix_pre": 1.0 + nrm((DEPTH, D_MODEL), 0.05),
        "g_mix_post": 1.0 + nrm((DEPTH, D_MODEL), 0.05),
        "g_ffn_pre": 1.0 + nrm((DEPTH, D_MODEL), 0.05),
        "g_ffn_post": 1.0 + nrm((DEPTH, D_MODEL), 0.05),
        "w_in": nrm((DEPTH, D_MODEL, D_IN), D_MODEL ** -0.5),
        "pool_w": nrm((DEPTH, POOL_GROUPS, POOL_GC, POOL_GC), POOL_GC ** -0.5),
        "pool_scale": 1.0 + nrm((DEPTH, D_POOL), 0.1),
        "sconv_w": nrm((DEPTH, SCONV_K, D_SCONV), SCONV_K ** -0.5),
        "cconv_w": nrm((DEPTH, CCONV_K, D_CCONV), CCONV_K ** -0.5),
        "cconv_b": nrm((DEPTH, D_CCONV), 0.01),
        "cln_g": 1.0 + nrm((DEPTH, D_CCONV), 0.05),
        "cln_b": nrm((DEPTH, D_CCONV), 0.01),
        "w_out": nrm((DEPTH, D_MIX, D_MODEL), D_MIX ** -0.5),
        "w_up": nrm((DEPTH, D_MODEL, 2 * D_FF), D_MODEL ** -0.5),
        "ffn_conv_w": nrm((DEPTH, FFN_K, 2 * D_FF), FFN_K ** -0.5),
        "w_down": nrm((DEPTH, D_FF, D_MODEL), D_FF ** -0.5),
    }


def reference(x_prompt, x_sample, c_prompt, c_sample, state_pool, state_sconv, state_cconv, state_ffn,
              w_ada, b_ada, g_mix_pre, g_mix_post, g_ffn_pre, g_ffn_post,
              w_in, pool_w, pool_scale, sconv_w, cconv_w, cconv_b, cln_g, cln_b,
              w_out, w_up, ffn_conv_w, w_down):
    xp, xs = x_prompt, x_sample
    dt = x_prompt.dtype
    zp_pool = jnp.zeros((BATCH, POOL_PAST, D_POOL), dt)
    zp_sconv = jnp.zeros((BATCH, SCONV_K - 1, D_SCONV), dt)
    zp_cconv = jnp.zeros((BATCH, CCONV_K - 1, D_CCONV), dt)
    zp_ffn = jnp.zeros((BATCH, FFN_K - 1, 2 * D_FF), dt)
    np_pool, np_sconv, np_cconv, np_ffn = [], [], [], []
    ns_pool, ns_sconv, ns_cconv, ns_ffn = [], [], [], []
    for l in range(DEPTH):
        params = (w_ada[l], b_ada[l], g_mix_pre[l], g_mix_post[l], g_ffn_pre[l], g_ffn_post[l],
                  w_in[l], pool_w[l], pool_scale[l], sconv_w[l], cconv_w[l], cconv_b[l],
                  cln_g[l], cln_b[l], w_out[l], w_up[l], ffn_conv_w[l], w_down[l])
        xp, a, b, c, d = layer(xp, c_prompt, zp_pool, zp_sconv, zp_cconv, zp_ffn, 0, *params)
        np_pool.append(a); np_sconv.append(b); np_cconv.append(c); np_ffn.append(d)
        xs, a, b, c, d = layer(xs, c_sample, state_pool[l], state_sconv[l], state_cconv[l],
                               state_ffn[l], PAST_LEN, *params)
        ns_pool.append(a); ns_sconv.append(b); ns_cconv.append(c); ns_ffn.append(d)
    return (xp, xs,
            jnp.stack(np_pool), jnp.stack(np_sconv), jnp.stack(np_cconv), jnp.stack(np_ffn),
            jnp.stack(ns_pool), jnp.stack(ns_sconv), jnp.stack(ns_cconv), jnp.stack(ns_ffn))
```

```python
import numpy as np
import concourse.bass as bass
import concourse.mybir as mybir
from concourse.bass_utils import run_bass_kernel_spmd

F32 = mybir.dt.float32
BF16 = mybir.dt.bfloat16
AF = mybir.ActivationFunctionType
ALU = mybir.AluOpType

NCORES = 8
D = 1024
KC = 8
DEPTH = 4
SEQ = 2048
NSEQ_S = 16
NS = 64
DP, DS, DC, DFF = 256, 384, 384, 2816
DIN = 2176
NFC = 44
NPAIR = 22
EPS = 1e-6
NPC = 512
NPIECE = SEQ // NPC
NT = NPC // 512
NCOL = NPC + NS
HP, HS, HC, HF = 15, 2, 30, 2

PO = {}
_o = 0
for _n, _r in (("b_ada", 48), ("g_mix_pre", 8), ("g_mix_post", 8), ("g_ffn_pre", 8), ("g_ffn_post", 8),
               ("pool_scale", 2), ("sconv_w", 9), ("cconv_w", 93), ("cconv_b", 3), ("cln_g", 3),
               ("cln_b", 3), ("ffn_conv_w", 132)):
    PO[_n] = _o
    _o += _r
NPT = _o
DBG_SKIP = False


class _Stop(Exception):
    pass


class Buf:
    __slots__ = ("name", "space", "lo", "hi", "w", "r", "ov")

    def __init__(self, name, space, lo, hi):
        self.name, self.space, self.lo, self.hi = name, space, lo, hi
        self.w = None
        self.r = {}
        self.ov = None


class V:
    def __init__(self, ap, bufs):
        self.ap = ap
        self.bufs = list(bufs)

    def __getitem__(self, key):
        return V(self.ap[key], self.bufs)

    def re(self, s, **kw):
        return V(self.ap.rearrange(s, **kw), self.bufs)

    def bc(self, shape):
        return V(self.ap.to_broadcast(shape), self.bufs)

    def us(self, axis):
        return V(self.ap.unsqueeze(axis), self.bufs)


class Sem:
    def __init__(self, h, key):
        self.h, self.key, self.total = h, key, 0


class Eng:
    def __init__(self, nc, name, h, is_pe=False):
        self.name, self.h, self.is_pe = name, h, is_pe
        self.sem = Sem(nc.alloc_semaphore("s_" + name), "s_" + name)
        self.count = 0
        self.seen = {}
        self.dsems = []
        self.di = 0


class KB:
    def __init__(self):
        nc = bass.Bass("TRN2", target_bir_lowering=False)
        self.nc = nc
        self.pe = Eng(nc, "pe", nc.tensor, True)
        self.act = Eng(nc, "act", nc.scalar)
        self.dve = Eng(nc, "dve", nc.vector)
        self.pool = Eng(nc, "pool", nc.gpsimd)
        self.sp = Eng(nc, "sp", nc.sync)
        for e, n in ((self.sp, 14), (self.pool, 12)):
            for i in range(n):
                e.dsems.append(Sem(nc.alloc_semaphore(f"d_{e.name}{i}"), f"d_{e.name}{i}"))
        self.bufs = {"sb": [], "ps": []}
        self.nwords = 53100
        self.arena = nc.alloc_sbuf_tensor("arena", [128, self.nwords], F32)
        self.psum = nc.alloc_psum_tensor("psum", [128, 4096], F32)
        self.top = 0
        self.out_toks = []
        self.opn = 0
        self.stop_n = 0
        self.marks = {}

    def mkbuf(self, name, space, lo, hi):
        b = Buf(name, space, lo, hi)
        lst = self.bufs[space]
        b.ov = [b]
        for o in lst:
            if o.lo < hi and lo < o.hi:
                o.ov.append(b)
                b.ov.append(o)
        lst.append(b)
        return b

    def alloc(self, words):
        lo = self.top
        self.top += words
        assert self.top <= self.nwords, f"SBUF arena overflow {self.top}"
        return lo

    def sb(self, name, lo, words, dt=F32, shape=None):
        b = self.mkbuf(name, "sb", lo, lo + words)
        ap = self.arena[:, lo:lo + words]
        if dt != F32:
            ap = ap.bitcast(dt)
        return V(ap, [b])

    def sbn(self, name, words, dt=F32):
        return self.sb(name, self.alloc(words), words, dt)

    def ps(self, name, lo, n):
        b = self.mkbuf(name, "ps", lo, lo + n)
        return V(self.psum[:, lo:lo + n], [b])

    def _deps(self, eng, reads, writes):
        toks = {}
        for b in reads:
            for ob in b.ov:
                if ob.w is not None:
                    s, v = ob.w
                    if toks.get(s.key, (None, 0))[1] < v:
                        toks[s.key] = (s, v)
        for b in writes:
            for ob in b.ov:
                if ob.w is not None:
                    s, v = ob.w
                    if toks.get(s.key, (None, 0))[1] < v:
                        toks[s.key] = (s, v)
                for k, (s, v) in ob.r.items():
                    if toks.get(k, (None, 0))[1] < v:
                        toks[k] = (s, v)
        for k, (s, v) in toks.items():
            if eng.is_pe and s is eng.sem:
                continue
            if eng.seen.get(k, 0) >= v:
                continue
            eng.h.wait_ge(s.h, v)
            eng.seen[k] = v

    def _commit(self, tok, reads, writes):
        s, v = tok
        ws = set(id(b) for b in writes)
        for b in writes:
            b.w = tok
            b.r = {}
        for b in reads:
            if id(b) in ws:
                continue
            if b.r.get(s.key, (None, 0))[1] < v:
                b.r[s.key] = (s, v)

    def emit(self, eng, fn, reads, writes):
        self.opn += 1
        if self.stop_n and self.opn == self.stop_n:
            raise _Stop()
        rb = [b for v in reads for b in v.bufs]
        wb = [b for v in writes for b in v.bufs]
        self._deps(eng, rb, wb)
        ins = fn()
        eng.count += 1
        ins.then_inc(eng.sem.h, 1)
        tok = (eng.sem, eng.count)
        self._commit(tok, rb, wb)
        return tok

    def dma(self, eng, out, in_, reads=(), writes=(), is_out=False):
        self.opn += 1
        if self.stop_n and self.opn == self.stop_n:
            raise _Stop()
        rb = [b for v in reads for b in v.bufs]
        wb = [b for v in writes for b in v.bufs]
        s = eng.dsems[eng.di % len(eng.dsems)]
        eng.di += 1
        if s.total > 0 and eng.seen.get(s.key, 0) < s.total:
            eng.h.wait_ge(s.h, s.total)
            eng.seen[s.key] = s.total
        self._deps(eng, rb, wb)
        ins = eng.h.dma_start(out=out, in_=in_)
        ins.then_inc(s.h, 16)
        s.total += 16
        tok = (s, s.total)
        self._commit(tok, rb, wb)
        if is_out:
            self.out_toks.append(tok)
        return tok

    def A(self, out, in_, func, bias=None, scale=None, eng=None):
        rd = [in_] + [x for x in (bias, scale) if isinstance(x, V)]
        kw = {}
        if bias is not None:
            kw["bias"] = bias.ap if isinstance(bias, V) else bias
        if scale is not None:
            kw["scale"] = scale.ap if isinstance(scale, V) else scale
        return self.emit(self.act, lambda: self.nc.scalar.activation(out=out.ap, in_=in_.ap, func=func, **kw),
                         rd, [out])

    def TS(self, out, in0, s1, s2, op0, op1=None, eng=None):
        eng = eng or self.dve
        rd = [in0] + [x for x in (s1, s2) if isinstance(x, V)]
        a1 = s1.ap if isinstance(s1, V) else s1
        a2 = s2.ap if isinstance(s2, V) else s2
        kw = {}
        if op1 is not None:
            kw["op1"] = op1
        return self.emit(eng, lambda: eng.h.tensor_scalar(out=out.ap, in0=in0.ap, scalar1=a1, scalar2=a2,
                                                          op0=op0, **kw), rd, [out])

    def TT(self, out, in0, in1, op, eng=None):
        eng = eng or self.dve
        return self.emit(eng, lambda: eng.h.tensor_tensor(out=out.ap, in0=in0.ap, in1=in1.ap, op=op),
                         [in0, in1], [out])

    def STT(self, out, in0, sc, in1, op0, op1, eng=None):
        eng = eng or self.dve
        rd = [in0, in1] + ([sc] if isinstance(sc, V) else [])
        a = sc.ap if isinstance(sc, V) else sc
        return self.emit(eng, lambda: eng.h.scalar_tensor_tensor(out=out.ap, in0=in0.ap, scalar=a, in1=in1.ap,
                                                                 op0=op0, op1=op1), rd, [out])

    def RC(self, out, in_):
        return self.emit(self.dve, lambda: self.nc.vector.reciprocal(out=out.ap, in_=in_.ap), [in_], [out])

    def CP(self, out, in_, eng=None):
        eng = eng or self.dve
        return self.emit(eng, lambda: eng.h.tensor_copy(out=out.ap, in_=in_.ap), [in_], [out])

    def MS(self, out, val, eng=None):
        eng = eng or self.dve
        return self.emit(eng, lambda: eng.h.memset(out.ap, val), [], [out])

    def MM(self, fn, reads, writes):
        return self.emit(self.pe, fn, reads, writes)


def build(stop=None):
    kb = KB()

    def chk(tag):
        kb.marks[tag] = kb.opn
        if stop is not None and tag == stop:
            raise _Stop()
    nc = kb.nc
    pe, act, dve, pool, sp = kb.pe, kb.act, kb.dve, kb.pool, kb.sp

    def din(name, shape):
        return nc.dram_tensor(name, list(shape), F32, kind="ExternalInput").ap()

    def dout(name, shape):
        return nc.dram_tensor(name, list(shape), F32, kind="ExternalOutput").ap()

    xp_d = din("xp", [SEQ, D]); xs_d = din("xs", [NS, D]); cc_d = din("cc", [17, D])
    stp_d = din("st_pool", [DEPTH, 16 * HP, DP]); sts_d = din("st_sconv", [DEPTH, 16 * HS, DS])
    stc_d = din("st_cconv", [DEPTH, 16 * HC, DC]); stf_d = din("st_ffn", [DEPTH, 16 * HF, 2 * DFF])
    wada_d = din("w_ada", [DEPTH, D, 6 * D]); pt_d = din("ptab", [DEPTH, NPT, 128])
    win_d = din("w_in", [DEPTH, D, DIN]); pw_d = din("pool_w", [DEPTH, 4, 64, 64])
    wout_d = din("w_out", [DEPTH, D, D]); wup_d = din("w_up", [DEPTH, D, 2 * DFF])
    wdn_d = din("w_down", [DEPTH, DFF, D])
    dgt_d = din("dgt", [DEPTH, 3, 128, 31 * 128])
    yp_d = dout("yp", [SEQ, D]); ys_d = dout("ys", [NS, D])
    opp_d = dout("o_pool_p", [DEPTH, HP, DP]); osp_d = dout("o_sconv_p", [DEPTH, HS, DS])
    ocp_d = dout("o_cconv_p", [DEPTH, HC, DC]); ofp_d = dout("o_ffn_p", [DEPTH, HF, 2 * DFF])
    ops_d = dout("o_pool_s", [DEPTH, 16, HP, DP]); oss_d = dout("o_sconv_s", [DEPTH, 16, HS, DS])
    ocs_d = dout("o_cconv_s", [DEPTH, 16, HC, DC]); ofs_d = dout("o_ffn_s", [DEPTH, 16, HF, 2 * DFF])

    wada_d = [wada_d[l] for l in range(DEPTH)]; win_d = [win_d[l] for l in range(DEPTH)]
    wout_d = [wout_d[l] for l in range(DEPTH)]; wup_d = [wup_d[l] for l in range(DEPTH)]; wdn_d = [wdn_d[l] for l in range(DEPTH)]

    W_X = KC * NCOL
    x_lo = kb.alloc(W_X)
    hc_lo = kb.alloc(W_X)
    a_lo = kb.alloc(NPAIR * NCOL // 2)
    A_WORDS = NPAIR * NCOL // 2

    def chunked(name, lo, nch, dt):
        wpc = NCOL if dt == F32 else NCOL // 2
        full = kb.arena[:, lo:lo + nch * wpc]
        if dt != F32:
            full = full.bitcast(dt)
        full = full.rearrange("p (c n) -> p c n", c=nch)
        res = {"full": full, "p": [], "s": [], "bp": [], "bs": []}
        pw = NPC if dt == F32 else NPC // 2
        for c in range(nch):
            bp = kb.mkbuf(f"{name}{c}p", "sb", lo + c * wpc, lo + c * wpc + pw)
            bs = kb.mkbuf(f"{name}{c}s", "sb", lo + c * wpc + pw, lo + (c + 1) * wpc)
            res["p"].append(V(full[:, c, 0:NPC], [bp]))
            res["s"].append(V(full[:, c, NPC:NCOL], [bs]))
            res["bp"].append(bp); res["bs"].append(bs)
        res["sall"] = V(full[:, :, NPC:NCOL], res["bs"])
        return res

    X = chunked("x", x_lo, KC, F32)
    H = chunked("h", hc_lo, KC, BF16)
    CAT = chunked("cat", hc_lo + W_X // 2, KC, BF16)
    FF = chunked("ff", hc_lo, KC, F32)
    MX = chunked("mx", a_lo, KC, F32)
    AA = chunked("a", a_lo, NPAIR, BF16)

    WSLOT = 1536
    NWS = 5
    wring = [kb.sbn(f"wr{i}", WSLOT, BF16) for i in range(NWS)]
    wr_i = [0]
    whalf = [[kb.sb(f"wrh{i}_{g}", w_.bufs[0].lo + g * 512, 512, BF16) for g in range(2)] for i, w_ in enumerate(wring)]

    PT = kb.sbn("pt", DEPTH * NPT)
    PTv = V(PT.ap.rearrange("p (l n) -> p l n", l=DEPTH), PT.bufs)
    DVA = [kb.sbn(f"dv{l}", 6 * KC * 17) for l in range(DEPTH)]
    ident_f = kb.sbn("ident_f", 128); ident_b = kb.sbn("ident_b", 64, BF16)
    ones_b = kb.sbn("ones_b", 64, BF16); ones_f = kb.sbn("ones_f", 128)
    invcnt = kb.sbn("invcnt", 32)
    PW = kb.sbn("pw", DEPTH * 2 * 64, BF16)
    PWv = V(PW.ap.rearrange("p (l c n) -> p l c n", l=DEPTH, c=2), PW.bufs)
    CTt = kb.sbn("ct", 80, BF16)
    CTv = V(CTt.ap[:, 0:KC * 17].rearrange("p (k s) -> p k s", k=KC), CTt.bufs)
    RSp = kb.sbn("rsp", NPC); RSs = kb.sbn("rss", NS)
    SQ = [kb.sbn(f"sq{i}", NCOL // 2, BF16) for i in range(3)]
    TMP = [kb.sbn(f"tmp{i}", NCOL) for i in range(4)]
    sq_i = [0]; tmp_i = [0]

    def nsq():
        sq_i[0] += 1
        return SQ[sq_i[0] % len(SQ)]

    def ntmp():
        tmp_i[0] += 1
        return TMP[tmp_i[0] % len(TMP)]

    HALO_P = [kb.sbn(f"hp{l}", 2 * HP) for l in range(DEPTH)]
    HALO_S = [kb.sbn(f"hs{l}", 3 * HS) for l in range(DEPTH)]
    HALO_C = [kb.sbn(f"hcv{l}", 3 * HC // 2, BF16) for l in range(DEPTH)]
    HALO_F = [kb.sbn(f"hf{l}", NFC * HF // 2, BF16) for l in range(DEPTH)]

    def padded(name, lo, nch, Hh, dt):
        cols = Hh + NPC + 16 * (Hh + 4)
        colsw = cols if dt == F32 else (cols + 1) // 2
        res = {"cols": cols, "H": Hh, "pf": [], "sf": [], "words": nch * colsw}
        for c in range(nch):
            l0 = lo + c * colsw
            ap = kb.arena[:, l0:l0 + colsw]
            if dt != F32:
                ap = ap.bitcast(dt)
            pcw = (Hh + NPC) if dt == F32 else (Hh + NPC) // 2
            bp = kb.mkbuf(f"{name}{c}p", "sb", l0, l0 + pcw)
            bs = kb.mkbuf(f"{name}{c}s", "sb", l0 + pcw, l0 + colsw)
            res["pf"].append(V(ap[:, 0:Hh + NPC], [bp]))
            res["sf"].append(V(ap[:, Hh + NPC:Hh + NPC + 16 * (Hh + 4)].rearrange("p (b j) -> p b j", j=Hh + 4), [bs]))
        return res

    MODT = kb.sbn("modt", 48 * 17 + 16)
    STG = kb.sbn("stg", 1536)
    OST = kb.sbn("ost", 1536)
    IOX = [kb.sb(f"iox{i}", a_lo + i * 1024, 1024) for i in range(2)]
    m_lo = kb.top
    mlo = m_lo
    UP = padded("up", mlo, 1, HP, F32); mlo += UP["words"]
    W1 = padded("w1", mlo, 1, HP, F32); mlo += W1["words"]
    W2 = padded("w2", mlo, 1, HP, F32); mlo += W2["words"]
    PL = kb.sb("pl", mlo, NCOL // 2, BF16); mlo += NCOL // 2
    ZB = padded("zb", mlo, 2, HS, F32); mlo += ZB["words"]
    GL = padded("gl", mlo, 3, HC, BF16); mlo += GL["words"]
    VB = kb.sb("vb", mlo, 3 * NCOL); mlo += 3 * NCOL
    VQ = kb.sb("vq", mlo, 3 * NCOL); mlo += 3 * NCOL
    LNM = kb.sb("lnm", mlo, 512); mlo += 512
    LNR = kb.sb("lnr", mlo, 512); mlo += 512
    GT = kb.sb("gt", mlo, 3 * 96); mlo += 3 * 96
    TAILS = kb.sb("tails", mlo, 3 * 96); mlo += 3 * 96
    DGS = []
    for i in range(2):
        DGS.append(kb.sb(f"dg{i}", mlo, 31 * 64, BF16)); mlo += 31 * 64
    m_hi = mlo
    DGS.append(kb.sb("dg2", a_lo + 2048, 31 * 64, BF16))
    flo = m_lo
    UPBw = (HF + NPC + 16 * (HF + 4)) // 2
    UPB = []
    NUPB, NACC = 6, 8
    for i in range(NUPB):
        UPB.append(padded(f"upb{i}", flo, 1, HF, BF16)); flo += UPBw
    ACC = []
    for i in range(NACC):
        ACC.append(kb.sb(f"acc{i}", flo, NCOL)); flo += NCOL
    PTMP = []
    for i in range(2):
        PTMP.append(kb.sb(f"ptmp{i}", flo, NCOL)); flo += NCOL
    FST = []
    for i in range(2):
        FST.append(kb.sb(f"fst{i}", flo, 11 * 34)); flo += 11 * 34
    UPS = kb.sb("ups", flo, NFC * 16 * HF // 2, BF16); flo += NFC * 16 * HF // 2
    kb.top = max(m_hi, flo)
    assert kb.top <= kb.nwords, f"SBUF overflow {kb.top}"
    print("SBUF words used", kb.top, "of", kb.nwords)

    RB = 8 - NT - 2
    nslots = RB // NT
    ring = [kb.ps(f"ring{i}", i * NT * 512, NT * 512) for i in range(nslots)]
    if NT == 1:
        ring.append(kb.ps("ring_b6", 6 * 512, 512))
        nslots += 1
    ring_i = [0]
    STATP = kb.ps("statp", RB * 512, NT * 512)
    STATS = kb.ps("stats", 7 * 512, 64)

    def rslot():
        ring_i[0] += 1
        return ring[ring_i[0] % nslots]

    def sslot():
        return rslot()

    def wslot():
        wr_i[0] += 1
        return wring[wr_i[0] % NWS]

    wplan = []

    def P_mod(l):
        for blk in range(16):
            wplan.append((wada_d[l], D, [(blk * 384, 384, 0)], 384))

    def P_mix(l):
        for (c0, n) in ((0, 256), (256, 384), (256 + 768, 384), (256 + 384, 384), (1408, 384), (1792, 384)):
            wplan.append((win_d[l], D, [(c0, n, 0)], n))
        for mo in range(KC):
            wplan.append((wout_d[l], D, [(mo * 128, 128, 0)], 128))

    def P_ffn(l, modl=None):
        for j in range(NPAIR):
            wplan.append((wup_d[l], D, [(j * 128, 128, "pair")], 256))
            if modl is not None and j < 16:
                wplan.append((wada_d[modl], D, [(j * 384, 384, 0)], 384))
        for mo in range(KC):
            wplan.append((wdn_d[l], DFF, [(mo * 128, 128, 0)], 128))

    P_mod(0)
    for q_ in range(NPIECE):
        for l_ in range(DEPTH):
            P_mix(l_)
            P_ffn(l_, (l_ + 1) if (q_ == 0 and l_ + 1 < DEPTH) else None)
    wst = {"issued": 0, "taken": 0, "views": {}}
    WLIVE = 3

    def w_issue(j):
        dram2d, nrows, parts, tcols = wplan[j]
        nkc = nrows // 128
        slot = wring[j % NWS]
        view = V(slot.ap[:, 0:nkc * tcols].rearrange("p (k c) -> p k c", k=nkc), slot.bufs)
        for (c0, ncols, coff) in parts:
            if coff == "pair":
                hv = []
                for g in range(2):
                    hb_ = whalf[j % NWS][g]
                    hview = V(hb_.ap.rearrange("p (k c) -> p k c", k=nkc), hb_.bufs)
                    src = dram2d[0:nrows, g * DFF + c0:g * DFF + c0 + ncols].rearrange("(k p) c -> p k c", p=128)
                    kb.dma(pool, hview.ap, src, writes=[hview])
                    hv.append(hview)
                view = hv
                continue
            src = dram2d[0:nrows, c0:c0 + ncols].rearrange("(k p) c -> p k c", p=128)
            kb.dma(pool, view.ap[:, :, coff:coff + ncols], src, writes=[view])
        wst["views"][j] = view

    def take_w(dram2d, c0, wl=WLIVE):
        i = wst["taken"]
        assert wplan[i][0] is dram2d and wplan[i][2][0][0] == c0, (i, c0, wplan[i][2])
        lim = min(len(wplan) - 1, i + NWS - wl)
        while wst["issued"] <= lim:
            w_issue(wst["issued"])
            wst["issued"] += 1
        wst["taken"] += 1
        return wst["views"].pop(i)

    iot = V(STG.ap[:, 0:128].bitcast(mybir.dt.int32), STG.bufs)
    kb.emit(pool, lambda: nc.gpsimd.iota(iot.ap, pattern=[[1, 128]], base=0, channel_multiplier=-1), [], [iot])
    kb.CP(ident_f, iot)
    kb.TS(ident_f, ident_f, 0.0, None, ALU.is_equal)
    kb.CP(ident_b, ident_f)
    kb.MS(ones_b, 1.0)
    kb.MS(ones_f, 1.0)
    icv = V(invcnt.ap[:, 0:30].rearrange("p (c t) -> p c t", c=2), invcnt.bufs)
    for ch in range(2):
        for half in range(2):
            w = (2, 4, 8, 16)[2 * ch + half]
            ps_ = slice(64 * half, 64 * half + 64)
            kb.MS(icv[ps_, ch, :], 1.0 / w)
            for t in range(w - 1):
                kb.MS(icv[ps_, ch, t:t + 1], 1.0 / (t + 1))
    for l in range(DEPTH):
        r = 0
        while r < NPT:
            n = min(128, NPT - r)
            st = V(STG.ap[0:n, 0:128], STG.bufs)
            kb.dma(sp, st.ap, pt_d[l, r:r + n, :], writes=[st])
            sl = rslot()
            kb.MM(lambda st=st, sl=sl, n=n: nc.tensor.transpose(out=sl.ap[:, 0:n], in_=st.ap, identity=ident_f.ap[0:n, 0:n]),
                  [st, ident_f], [sl])
            kb.A(PTv[:, l, r:r + n], sl[:, 0:n], AF.Copy)
            r += n
    kb.MS(PW, 0.0)
    for l in range(DEPTH):
        for g in range(4):
            ch, half = g // 2, g % 2
            dst = PWv[64 * half:64 * half + 64, l, ch, 64 * half:64 * half + 64]
            kb.dma(pool, dst.ap, pw_d[l, g, :, :], writes=[PWv])
    cst = V(STG.ap[0:17, 0:1024], STG.bufs)
    kb.dma(sp, cst.ap, cc_d[:, :], writes=[cst])
    kb.A(cst, cst, AF.Silu)
    sl = rslot()
    kb.MM(lambda: [nc.tensor.transpose(out=sl.ap[:, k * 17:(k + 1) * 17], in_=cst.ap[:, k * 128:(k + 1) * 128],
                                       identity=ident_f.ap[0:17, 0:17]) for k in range(KC)][-1],
          [cst, ident_f], [sl])
    kb.A(CTv, sl[:, 0:KC * 17].re("p (k s) -> p k s", k=KC), AF.Copy)

    def pcol(l, name, idx):
        o = PO[name] + idx
        return PTv[:, l, o:o + 1]

    def mod_block(l, blk):
        modv = V(MODT.ap[:, 0:48 * 17].rearrange("p (c s) -> p c s", c=48), MODT.bufs)
        wv = take_w(wada_d[l], blk * 384, wl=1)
        sl = rslot()

        def fn(wv=wv, sl=sl):
            last = None
            for j in range(3):
                for k in range(KC):
                    last = nc.tensor.matmul(sl.ap[:, j * 17:(j + 1) * 17], lhsT=wv.ap[:, k, j * 128:(j + 1) * 128],
                                            rhs=CTv.ap[:, k, :], start=(k == 0), stop=(k == KC - 1))
            return last
        kb.MM(fn, [wv, CTv], [sl])
        o = PO["b_ada"] + blk * 3
        kb.TT(modv[:, blk * 3:blk * 3 + 3, :], sl[:, 0:51].re("p (c s) -> p c s", c=3),
              PTv[:, l, o:o + 3].us(2).bc([128, 3, 17]), ALU.add)

    def mod_finish(l):
        modv = V(MODT.ap[:, 0:48 * 17].rearrange("p (c s) -> p c s", c=48), MODT.bufs)
        dv = V(DVA[l].ap.rearrange("p (w c s) -> p w c s", w=6, c=KC), DVA[l].bufs)
        for sub, (gpre, gpost) in enumerate((("g_mix_pre", "g_mix_post"), ("g_ffn_pre", "g_ffn_post"))):
            m0 = sub * 24
            gp = PTv[:, l, PO[gpre]:PO[gpre] + 8].us(2).bc([128, 8, 17])
            gq = PTv[:, l, PO[gpost]:PO[gpost] + 8].us(2).bc([128, 8, 17])
            kb.TS(dv[:, 3 * sub + 0], modv[:, m0 + 8:m0 + 16, :], 1.0, 32.0, ALU.add, ALU.mult)
            kb.TT(dv[:, 3 * sub + 0], dv[:, 3 * sub + 0], gp, ALU.mult)
            kb.CP(dv[:, 3 * sub + 1], modv[:, m0:m0 + 8, :])
            kb.TS(dv[:, 3 * sub + 2], modv[:, m0 + 16:m0 + 24, :], 32.0, None, ALU.mult)
            kb.TT(dv[:, 3 * sub + 2], dv[:, 3 * sub + 2], gq, ALU.mult)

    def compute_mod(l):
        for blk in range(16):
            mod_block(l, blk)
        mod_finish(l)

    def dvv(l):
        return V(DVA[l].ap.rearrange("p (w c s) -> p w c s", w=6, c=KC), DVA[l].bufs)

    def mm_chunk(lhs_fn, nk, rhs_p, rhs_s, segs, reads):
        outs = {}
        slp = rslot()
        outs["p"] = slp
        wr = [slp]
        if "s" in segs:
            sls = sslot()
            outs["s"] = sls
            wr.append(sls)

        def fn():
            last = None
            for k in range(nk):
                lh = lhs_fn(k)
                for t in range(NT):
                    last = nc.tensor.matmul(slp.ap[:, t * 512:(t + 1) * 512], lhsT=lh, rhs=rhs_p(k, t),
                                            start=(k == 0), stop=(k == nk - 1))
            if "s" in segs and not DBG_SKIP:
                for k in range(nk):
                    last = nc.tensor.matmul(sls.ap[:, 0:NS], lhsT=lhs_fn(k), rhs=rhs_s(k),
                                            start=(k == 0), stop=(k == nk - 1))
            return last
        kb.MM(fn, reads, wr)
        return outs

    def rms_stats(src, segs, sq_from_psum=None):
        for seg in segs:
            for k in range(KC):
                q = nsq()
                n = NPC if seg == "p" else NS
                qv = q[:, 0:n]
                kb.A(qv, src[seg][k], AF.Square)
                st = STATP if seg == "p" else STATS

                def fn(qv=qv, st=st, n=n, k=k):
                    last = None
                    for t in range(max(1, n // 512)):
                        w = min(512, n)
                        last = nc.tensor.matmul(st.ap[:, t * 512:t * 512 + w], lhsT=ones_b.ap, rhs=qv.ap[:, t * 512:t * 512 + w],
                                                start=(k == 0), stop=(k == KC - 1))
                    return last
                kb.MM(fn, [qv, ones_b], [st])

    def rstd_from_stats(segs):
        for seg in segs:
            st, rs = (STATP, RSp) if seg == "p" else (STATS, RSs)
            kb.A(rs, st, AF.Sqrt, bias=float(D * EPS))
            kb.RC(rs, rs)

    def prenorm(l, sub, segs):
        dv = dvv(l)
        rms_stats(X, segs)
        rstd_from_stats(segs)
        for k in range(KC):
            t = ntmp()
            kb.TT(t[:, 0:NPC], X["p"][k], RSp, ALU.mult)
            kb.A(H["p"][k], t[:, 0:NPC], AF.Identity, bias=dv[:, 3 * sub + 1, k, 0:1], scale=dv[:, 3 * sub + 0, k, 0:1])
        if "s" in segs:
            t = ntmp()
            tv = t[:, 0:KC * NS].re("p (k n) -> p k n", k=KC) if KC * NS <= NCOL else None
            kb.TT(tv, X["sall"], RSs.us(1).bc([128, KC, NS]), ALU.mult)
            t4 = tv.re("p k (b j) -> p k b j", j=4)
            kb.TT(t4, t4, dv[:, 3 * sub + 0, :, 1:17].us(3).bc([128, KC, 16, 4]), ALU.mult)
            kb.TT(H["sall"].re("p k (b j) -> p k b j", j=4), t4,
                  dv[:, 3 * sub + 1, :, 1:17].us(3).bc([128, KC, 16, 4]), ALU.add)

    def postnorm(l, sub, SRC, segs):
        dv = dvv(l)
        rstd_from_stats(segs)
        for k in range(KC):
            t = ntmp()
            kb.TT(t[:, 0:NPC], SRC["p"][k], RSp, ALU.mult)
            kb.TT(X["p"][k], X["p"][k], t[:, 0:NPC], ALU.add)
        if "s" in segs:
            t = ntmp()
            tv = t[:, 0:KC * NS].re("p (k n) -> p k n", k=KC)
            kb.TT(tv, SRC["sall"], RSs.us(1).bc([128, KC, NS]), ALU.mult)
            t4 = tv.re("p k (b j) -> p k b j", j=4)
            kb.TT(t4, t4, dv[:, 3 * sub + 2, :, 1:17].us(3).bc([128, KC, 16, 4]), ALU.mult)
            kb.TT(X["sall"], X["sall"], tv, ALU.add)

    def out_proj(l, sub, wd, nk, rhs_src, DST, segs):
        pend = None
        for mo in range(KC):
            wv = take_w(wd, mo * 128, wl=1)
            outs = mm_chunk(lambda k, wv=wv: wv.ap[:, k, :], nk,
                            lambda k, t: rhs_src["p"][k].ap[:, t * 512:(t + 1) * 512],
                            lambda k: rhs_src["s"][k].ap, segs,
                            [wv] + [rhs_src[s][k] for s in segs for k in range(nk)])
            if pend is not None:
                pend()
            qs = {}
            for seg in segs:
                if seg == "p":
                    kb.A(DST[seg][mo], outs[seg], AF.Identity, scale=dvv(l)[:, 3 * sub + 2, mo, 0:1])
                else:
                    kb.A(DST[seg][mo], outs[seg][:, 0:NS], AF.Copy)
                q = nsq()
                n = NPC if seg == "p" else NS
                qs[seg] = q[:, 0:n]
                kb.A(qs[seg], outs[seg] if seg == "p" else outs[seg][:, 0:NS], AF.Square)

            def mk(qs=qs, mo=mo):
                for seg in segs:
                    st = STATP if seg == "p" else STATS
                    n = NPC if seg == "p" else NS
                    qv = qs[seg]

                    def fn(qv=qv, st=st, n=n):
                        last = None
                        for t in range(max(1, n // 512)):
                            w = min(512, n)
                            last = nc.tensor.matmul(st.ap[:, t * 512:t * 512 + w], lhsT=ones_b.ap,
                                                    rhs=qv.ap[:, t * 512:t * 512 + w], start=(mo == 0), stop=(mo == KC - 1))
                        return last
                    kb.MM(fn, [qv, ones_b], [st])
            pend = mk
        pend()
        postnorm(l, sub, DST, segs)

    def store_rows(srcs, ncol, dsts):
        n = len(srcs)
        sl = rslot()
        kb.MM(lambda: [nc.tensor.transpose(out=sl.ap[0:ncol, i * 128:(i + 1) * 128], in_=srcs[i].ap, identity=ident_f.ap)
                       for i in range(n)][-1], list(srcs) + [ident_f], [sl])
        ov = V(OST.ap[0:ncol, 0:n * 128], OST.bufs)
        kb.A(ov, sl[0:ncol, 0:n * 128], AF.Copy)
        for (r0, nr, dfn) in dsts:
            kb.dma(sp, dfn(), OST.ap[r0:r0 + nr, 0:n * 128], reads=[ov], is_out=True)

    def build_diag(l, c, bi):
        dgb = DGS[bi]
        dgv = V(dgb.ap.rearrange("p (k n) -> p k n", k=31), dgb.bufs)
        kb.dma(pool, dgv.ap, dgt_d[l, c].rearrange("p (k n) -> p k n", k=31), writes=[dgv])
        return dgv

    def mix(l, q, segs):
        last = (q == NPIECE - 1)
        first = (q == 0)
        dgs_built = [build_diag(l, 0, 0), build_diag(l, 1, 1), build_diag(l, 2, 2)]
        prenorm(l, 0, segs)
        chk(f'mixP{q}_{l}')
        hreads = [H[s][k] for s in segs for k in range(KC)]

        def win_chunk(wv, c0):
            return mm_chunk(lambda k: wv.ap[:, k, c0:c0 + 128], KC,
                            lambda k, t: H["p"][k].ap[:, t * 512:(t + 1) * 512],
                            lambda k: H["s"][k].ap, segs, [wv] + hreads)

        if last:
            for tI in range(2):
                st = V(STG.ap[0:120, 0:256], STG.bufs)
                kb.dma(sp, st.ap, stp_d[l, tI * 120:(tI + 1) * 120, :], writes=[st])
                sl = rslot()
                kb.MM(lambda st=st, sl=sl: [nc.tensor.transpose(out=sl.ap[:, c * 120:(c + 1) * 120], in_=st.ap[:, c * 128:(c + 1) * 128],
                                                                 identity=ident_f.ap[0:120, 0:120]) for c in range(2)][-1],
                      [st, ident_f], [sl])
                for c in range(2):
                    dstv = UPSP[c][:, tI * 8:(tI + 1) * 8, :]
                    kb.A(dstv, sl[:, c * 120:(c + 1) * 120].re("p (b r) -> p b r", r=HP), AF.Copy)
            chk(f'mixS1{q}_{l}')
            st = V(STG.ap[0:32, 0:384], STG.bufs)
            kb.dma(sp, st.ap, sts_d[l, :, :], writes=[st])
            sl = rslot()
            kb.MM(lambda: [nc.tensor.transpose(out=sl.ap[:, c * 32:(c + 1) * 32], in_=st.ap[:, c * 128:(c + 1) * 128],
                                               identity=ident_f.ap[0:32, 0:32]) for c in range(3)][-1], [st, ident_f], [sl])
            for c in range(3):
                kb.A(ZSP[c], sl[:, c * 32:(c + 1) * 32].re("p (b r) -> p b r", r=HS), AF.Copy)
            chk(f'mixS2{q}_{l}')
            for c in range(3):
                sl = rslot()
                sts_ = []
                for tI in range(4):
                    st = V(STG.ap[0:120, tI * 384:(tI + 1) * 384], STG.bufs)
                    if c == 0:
                        kb.dma(sp, st.ap, stc_d[l, tI * 120:(tI + 1) * 120, :], writes=[st])
                    sts_.append(st)
                kb.MM(lambda sl=sl, c=c, sts_=sts_: [nc.tensor.transpose(out=sl.ap[:, tI * 120:(tI + 1) * 120],
                                                                          in_=sts_[tI].ap[:, c * 128:(c + 1) * 128],
                                                                          identity=ident_f.ap[0:120, 0:120]) for tI in range(4)][-1],
                      sts_ + [ident_f], [sl])
                chk(f'mixS3{q}_{l}_{c}')
                kb.CP(GL["sf"][c][:, :, 0:HC], sl[:, 0:480].re("p (b r) -> p b r", r=HC))
                chk(f'mixS4{q}_{l}_{c}')

        chk(f'mixA{q}_{l}')
        wv = take_w(win_d[l], 0)
        for ch in range(2):
            outs = win_chunk(wv, ch * 128)
            E = UP["pf"][0]
            Es = UP["sf"][0]
            if first:
                kb.MS(E[:, 0:HP], 0.0)
            else:
                kb.CP(E[:, 0:HP], V(HALO_P[l].ap[:, ch * HP:(ch + 1) * HP], HALO_P[l].bufs))
            kb.A(E[:, HP:HP + NPC], outs["p"], AF.Copy)
            if not last:
                kb.CP(V(HALO_P[l].ap[:, ch * HP:(ch + 1) * HP], HALO_P[l].bufs), E[:, NPC:NPC + HP])
            if "s" in segs:
                kb.CP(Es[:, :, 0:HP], UPSP[ch])
                kb.A(Es[:, :, HP:HP + 4], outs["s"][:, 0:NS].re("p (b j) -> p b j", j=4), AF.Copy)
            L = HP + NPC
            views = [(E, W1["pf"][0], W2["pf"][0], L, None)]
            if "s" in segs:
                views.append((Es, W1["sf"][0], W2["sf"][0], HP + 4, 1))
            for (e, w1, w2, Ln, is3) in views:
                def sl_(v, a, b):
                    return v[:, :, a:b] if is3 else v[:, a:b]
                kb.TT(sl_(w1, 1, Ln), sl_(e, 1, Ln), sl_(e, 0, Ln - 1), ALU.add)
                kb.TT(sl_(w2, 3, Ln), sl_(w1, 3, Ln), sl_(w1, 1, Ln - 2), ALU.add)
                if ch == 1 and not (is3 and DBG_SKIP):
                    kb.TT(sl_(w1, 7, Ln), sl_(w2, 7, Ln), sl_(w2, 3, Ln - 4), ALU.add)
                    kb.TT(sl_(w2, 15, Ln), sl_(w1, 15, Ln), sl_(w1, 7, Ln - 8), ALU.add)
                for half, wsrc in ((0, w1), (1, w2)):
                    wdw = (2, 4, 8, 16)[2 * ch + half]
                    pr = slice(64 * half, 64 * half + 64)
                    if is3:
                        o_ = PL[pr, NPC:NCOL].re("p (b j) -> p b j", j=4)
                        kb.STT(o_, wsrc[pr, :, HP:HP + 4], 1.0 / wdw, e[pr, :, HP:HP + 4], ALU.mult, ALU.subtract)
                    else:
                        kb.STT(PL[pr, 0:NPC], wsrc[pr, HP:Ln], 1.0 / wdw, e[pr, HP:Ln], ALU.mult, ALU.subtract)
                        if first:
                            t = ntmp()
                            kb.TT(t[pr, 0:HP], wsrc[pr, HP:2 * HP], icv[pr, ch, :], ALU.mult)
                            kb.TT(PL[pr, 0:HP], t[pr, 0:HP], e[pr, HP:2 * HP], ALU.subtract)
            chk(f'mixB1{q}_{l}_{ch}')
            if last:
                t = TAILS
                kb.CP(t[:, ch * 96:ch * 96 + HP], E[:, NPC:NPC + HP])
                kb.CP(t[:, ch * 96 + HP:ch * 96 + HP + NS].re("p (j b) -> p j b", j=4), Es[:, :, HP:HP + 4].re("p b j -> p j b"))
                pool_tails.append(t[:, ch * 96:ch * 96 + HP + NS])
            chk(f'mixB2{q}_{l}_{ch}')
            o2 = mm_chunk(lambda k: PWv.ap[:, l, ch, :], 1,
                          lambda k, t: PL.ap[:, t * 512:(t + 1) * 512], lambda k: PL.ap[:, NPC:NCOL], segs, [PWv, PL])
            for seg in segs:
                kb.A(CAT[seg][ch], o2[seg] if seg == "p" else o2[seg][:, 0:NS], AF.Identity, scale=pcol(l, "pool_scale", ch))
            chk(f'mixB5{q}_{l}_{ch}')
        chk(f'mixB3{q}_{l}')
        if last:
            store_rows(pool_tails, HP + NS,
                       [(0, HP, lambda: opp_d[l, :, :])] +
                       [(HP + 16 * j, 16, (lambda j=j: ops_d[l, :, 11 + j, :])) for j in range(4)])
            pool_tails.clear()
            chk(f'mixB4{q}_{l}')
            kb.dma(sp, ops_d[l, :, 0:11, :], stp_d[l].rearrange("(b r) c -> b r c", r=HP)[:, 4:HP, :], is_out=True)

        chk(f'mixB{q}_{l}')
        whb = take_w(win_d[l], 256)
        wcg = take_w(win_d[l], 256 + 768)
        wbg = take_w(win_d[l], 256 + 384)
        for c in range(3):
            zi = c % 2
            Z, Zs = ZB["pf"][zi], ZB["sf"][zi]
            o_hb = win_chunk(whb, c * 128)
            hb = ntmp()
            kb.A(hb[:, 0:NPC], o_hb["p"], AF.Copy)
            if "s" in segs:
                kb.A(hb[:, NPC:NCOL], o_hb["s"][:, 0:NS], AF.Copy)
            o_cg = win_chunk(wcg, c * 128)
            if first:
                kb.MS(Z[:, 0:HS], 0.0)
            else:
                kb.CP(Z[:, 0:HS], V(HALO_S[l].ap[:, c * HS:(c + 1) * HS], HALO_S[l].bufs))
            kb.TT(Z[:, HS:HS + NPC], o_cg["p"], hb[:, 0:NPC], ALU.mult)
            if not last:
                kb.CP(V(HALO_S[l].ap[:, c * HS:(c + 1) * HS], HALO_S[l].bufs), Z[:, NPC:NPC + HS])
            if "s" in segs:
                kb.CP(Zs[:, :, 0:HS], ZSP[c])
                kb.TT(Zs[:, :, HS:HS + 4], o_cg["s"][:, 0:NS].re("p (b j) -> p b j", j=4),
                      hb[:, NPC:NCOL].re("p (b j) -> p b j", j=4), ALU.mult)
            ca = ntmp()
            for (zv, cav, is3) in [(Z, ca[:, 0:NPC], False)] + ([(Zs, ca[:, NPC:NCOL].re("p (b j) -> p b j", j=4), True)] if "s" in segs else []):
                n = 4 if is3 else NPC
                for kk in range(3):
                    src = zv[:, :, kk:kk + n] if is3 else zv[:, kk:kk + n]
                    wk = pcol(l, "sconv_w", kk * 3 + c)
                    if kk == 0:
                        kb.TS(cav, src, wk, None, ALU.mult)
                    else:
                        kb.STT(cav, src, wk, cav, ALU.mult, ALU.add)
            if last:
                t = TAILS
                kb.CP(t[:, c * 96:c * 96 + HS], Z[:, NPC:NPC + HS])
                kb.CP(t[:, c * 96 + HS:c * 96 + HS + 32].re("p (j b) -> p j b", j=2), Zs[:, :, HS + 2:HS + 4].re("p b j -> p j b"))
                sconv_tails.append(t[:, c * 96:c * 96 + HS + 32])
            o_bg = win_chunk(wbg, c * 128)
            kb.TT(CAT["p"][2 + c], o_bg["p"], ca[:, 0:NPC], ALU.mult)
            if "s" in segs:
                kb.TT(CAT["s"][2 + c], o_bg["s"][:, 0:NS], ca[:, NPC:NCOL], ALU.mult)
        if last:
            store_rows(sconv_tails, HS + 32,
                       [(0, HS, lambda: osp_d[l, :, :])] +
                       [(HS + 16 * j, 16, (lambda j=j: oss_d[l, :, j, :])) for j in range(2)])
            sconv_tails.clear()

        chk(f'mixC{q}_{l}')
        wac = take_w(win_d[l], 1408)
        wbc = take_w(win_d[l], 1792)
        gtv = V(GT.ap[:, 0:288].rearrange("p (c n) -> p c n", c=3), GT.bufs)
        for c in range(3):
            G, Gs = GL["pf"][c], GL["sf"][c]
            o_b = win_chunk(wbc, c * 128)
            sg = ntmp()
            kb.A(sg[:, 0:NPC], o_b["p"], AF.Sigmoid)
            if "s" in segs:
                kb.A(sg[:, NPC:NCOL], o_b["s"][:, 0:NS], AF.Sigmoid)
            o_a = win_chunk(wac, c * 128)
            if first:
                kb.MS(G[:, 0:HC], 0.0)
            else:
                kb.CP(G[:, 0:HC], V(HALO_C[l].ap[:, c * HC:(c + 1) * HC], HALO_C[l].bufs))
            kb.TT(G[:, HC:HC + NPC], o_a["p"], sg[:, 0:NPC], ALU.mult)
            if not last:
                kb.CP(V(HALO_C[l].ap[:, c * HC:(c + 1) * HC], HALO_C[l].bufs), G[:, NPC:NPC + HC])
            if "s" in segs:
                kb.TT(Gs[:, :, HC:HC + 4], o_a["s"][:, 0:NS].re("p (b j) -> p b j", j=4),
                      sg[:, NPC:NCOL].re("p (b j) -> p b j", j=4), ALU.mult)
            if last:
                kb.TT(gtv[:, c, 0:HC], o_a["p"][:, NPC - HC:NPC], sg[:, NPC - HC:NPC], ALU.mult)
                kb.TT(gtv[:, c, HC:HC + NS].re("p (j b) -> p j b", j=4), o_a["s"][:, 0:NS].re("p (b j) -> p j b", j=4),
                      sg[:, NPC:NCOL].re("p (b j) -> p j b", j=4), ALU.mult)
        if last:
            store_rows([gtv[:, c, 0:HC + NS] for c in range(3)], HC + NS,
                       [(0, HC, lambda: ocp_d[l, :, :])] +
                       [(HC + 16 * j, 16, (lambda j=j: ocs_d[l, :, 26 + j, :])) for j in range(4)])
            kb.dma(sp, ocs_d[l, :, 0:26, :], stc_d[l].rearrange("(b r) c -> b r c", r=HC)[:, 4:HC, :], is_out=True)
        chk(f'mixD{q}_{l}')
        units = [("p", t) for t in range(NT)] + ([("s", 0)] if "s" in segs else [])
        NVB = NPC + NS
        vbv = V(VB.ap.rearrange("p (c n) -> p c n", c=3), VB.bufs)
        vqv = V(VQ.ap.rearrange("p (c n) -> p c n", c=3), VQ.bufs)
        for c in range(3):
            dgv = dgs_built[c]
            G, Gs = GL["pf"][c], GL["sf"][c]
            for (seg, t) in units:
                n = 512 if seg == "p" else NS
                o0 = t * 512 if seg == "p" else NPC
                if seg == "p":
                    sl = rslot()
                    kb.MM(lambda sl=sl, G=G, t=t, dgv=dgv: [nc.tensor.matmul(sl.ap[:, 0:512], lhsT=dgv.ap[:, kk, :],
                                                                     rhs=G.ap[:, kk + t * 512:kk + t * 512 + 512],
                                                                     start=(kk == 0), stop=(kk == 30)) for kk in range(31)][-1],
                          [dgv, G], [sl])
                    src = sl[:, 0:512]
                else:
                    sl = sslot()
                    kb.MM(lambda sl=sl, Gs=Gs, dgv=dgv: [nc.tensor.matmul(sl.ap[:, 0:NS], lhsT=dgv.ap[:, kk, :],
                                                                  rhs=Gs.ap[:, :, kk:kk + 4],
                                                                  start=(kk == 0), stop=(kk == 30)) for kk in range(31)][-1],
                          [dgv, Gs], [sl])
                    src = sl[:, 0:NS]
                kb.A(vbv[:, c, o0:o0 + n], src, AF.Identity, bias=pcol(l, "cconv_b", c))
                kb.A(vqv[:, c, o0:o0 + n], vbv[:, c, o0:o0 + n], AF.Square)
        for (seg, t) in units:
            n = 512 if seg == "p" else NS
            o0 = t * 512 if seg == "p" else NPC
            vbu = vbv[:, :, o0:o0 + n]
            vqu = vqv[:, :, o0:o0 + n]
            s1 = rslot()
            s2 = rslot()
            for (sv, srcv) in ((s1, vbu), (s2, vqu)):
                kb.MM(lambda sv=sv, srcv=srcv, n=n: [nc.tensor.matmul(sv.ap[:, 0:n], lhsT=ones_f.ap, rhs=srcv.ap[:, c, :],
                                                                      start=(c == 0), stop=(c == 2)) for c in range(3)][-1],
                      [srcv, ones_f], [sv])
            kb.TS(LNM[:, 0:n], s1[:, 0:n], 1.0 / DC, None, ALU.mult)
            t_ = ntmp()
            kb.TT(t_[:, 0:n], LNM[:, 0:n], LNM[:, 0:n], ALU.mult)
            kb.STT(LNR[:, 0:n], s2[:, 0:n], 1.0 / DC, t_[:, 0:n], ALU.mult, ALU.subtract)
            kb.A(LNR[:, 0:n], LNR[:, 0:n], AF.Sqrt, bias=float(EPS))
            kb.RC(LNR[:, 0:n], LNR[:, 0:n])
            for c in range(3):
                kb.TT(vbu[:, c], vbu[:, c], LNM[:, 0:n], ALU.subtract)
                kb.TT(vbu[:, c], vbu[:, c], LNR[:, 0:n], ALU.mult)
                dst = CAT["p"][5 + c][:, t * 512:t * 512 + 512] if seg == "p" else CAT["s"][5 + c]
                kb.A(dst, vbu[:, c], AF.Silu, bias=pcol(l, "cln_b", c), scale=pcol(l, "cln_g", c))

        chk(f'mixE{q}_{l}')
        out_proj(l, 0, wout_d[l], KC, CAT, MX, segs)

    def ffn(l, q, segs):
        last = (q == NPIECE - 1)
        first = (q == 0)
        prenorm(l, 1, segs)
        hreads = [H[s][k] for s in segs for k in range(KC)]
        upsv = V(UPS.ap.rearrange("p (c b r) -> p c b r", c=NFC, b=16), UPS.bufs)
        fstvs = [V(f_.ap.rearrange("p (c n) -> p c n", c=11), f_.bufs) for f_ in FST]
        hfv = V(HALO_F[l].ap.rearrange("p (c r) -> p c r", c=NFC), HALO_F[l].bufs)
        if last:
            for rd in range(4):
                st = V(STG.ap[0:32, 0:1408], STG.bufs)
                kb.dma(sp, st.ap, stf_d[l, :, rd * 1408:(rd + 1) * 1408], writes=[st])
                sl = rslot()
                kb.MM(lambda st=st, sl=sl: [nc.tensor.transpose(out=sl.ap[:, c * 32:(c + 1) * 32], in_=st.ap[:, c * 128:(c + 1) * 128],
                                                                 identity=ident_f.ap[0:32, 0:32]) for c in range(11)][-1],
                      [st, ident_f], [sl])
                kb.A(upsv[:, rd * 11:(rd + 1) * 11], sl[:, 0:352].re("p (c b r) -> p c b r", c=11, b=16), AF.Copy)
        ub_i = 0
        ac_i = 0
        pt_i = 0
        if first:
            for ub in UPB:
                kb.MS(ub["pf"][0][:, 0:HF], 0.0)
        for j in range(NPAIR):
            wv = take_w(wup_d[l], j * 128, wl=1)
            accs = []
            for gv in range(2):
                ci = j + gv * NPAIR
                outs = mm_chunk(lambda k, gv=gv: wv[gv].ap[:, k, :], KC,
                                lambda k, t: H["p"][k].ap[:, t * 512:(t + 1) * 512],
                                lambda k: H["s"][k].ap, segs, [wv[gv]] + hreads)
                ub = UPB[ub_i % NUPB]; ub_i += 1
                acc = ACC[ac_i % NACC]; ac_i += 1
                U, Us = ub["pf"][0], ub["sf"][0]
                if not first:
                    kb.A(U[:, 0:HF], hfv[:, ci, :], AF.Copy)
                kb.A(U[:, HF:HF + NPC], outs["p"], AF.Copy)
                kb.A(acc[:, 0:NPC], outs["p"], AF.Identity, scale=pcol(l, "ffn_conv_w", 2 * NFC + ci))
                if not last:
                    kb.A(hfv[:, ci, :], U[:, NPC:NPC + HF], AF.Copy)
                if "s" in segs:
                    o4 = outs["s"][:, 0:NS].re("p (b j) -> p b j", j=4)
                    kb.A(Us[:, :, 0:HF], upsv[:, ci], AF.Copy)
                    kb.A(Us[:, :, HF:HF + 4], o4, AF.Copy)
                    kb.A(acc[:, NPC:NCOL], outs["s"][:, 0:NS], AF.Identity, scale=pcol(l, "ffn_conv_w", 2 * NFC + ci))
                if last:
                    rc = ci % 11
                    fstv = fstvs[gv]
                    kb.A(fstv[:, rc, 0:HF], outs["p"][:, NPC - HF:NPC], AF.Copy)
                    kb.A(fstv[:, rc, HF:HF + 32].re("p (j b) -> p j b", j=2), o4[:, :, 2:4].re("p b j -> p j b"), AF.Copy)
                for kk in (1, 0):
                    wk = pcol(l, "ffn_conv_w", kk * NFC + ci)
                    if True:
                        kb.STT(acc[:, 0:NPC], U[:, kk:kk + NPC], wk, acc[:, 0:NPC], ALU.mult, ALU.add)
                        if "s" in segs:
                            a4 = acc[:, NPC:NCOL].re("p (b j) -> p b j", j=4)
                            kb.STT(a4, Us[:, :, kk:kk + 4], wk, a4, ALU.mult, ALU.add)
                    else:
                        pt_ = PTMP[pt_i % 2]; pt_i += 1
                        kb.TS(pt_[:, 0:NPC], U[:, kk:kk + NPC], wk, None, ALU.mult, eng=pool)
                        if "s" in segs:
                            kb.TS(pt_[:, NPC:NCOL].re("p (b j) -> p b j", j=4), Us[:, :, kk:kk + 4], wk, None, ALU.mult, eng=pool)
                        nn_ = NCOL if "s" in segs else NPC
                        kb.TT(acc[:, 0:nn_], acc[:, 0:nn_], pt_[:, 0:nn_], ALU.add, eng=pool)
                accs.append(acc)
                if last and ci % 11 == 10:
                    rd = ci // 11
                    for g0 in range(0, 11, 4):
                        ng = min(4, 11 - g0)
                        cs = (rd * 11 + g0) * 128
                        store_rows([fstvs[gv][:, g0 + i, 0:34] for i in range(ng)], 34,
                                   [(0, HF, (lambda cs=cs, ng=ng: ofp_d[l, :, cs:cs + ng * 128]))] +
                                   [(HF + 16 * jj, 16, (lambda jj=jj, cs=cs, ng=ng: ofs_d[l, :, jj, cs:cs + ng * 128])) for jj in range(2)])
            nn = NCOL if "s" in segs else NPC
            kb.A(accs[0][:, 0:nn], accs[0][:, 0:nn], AF.Silu)
            kb.TT(V(AA["full"][:, j, 0:nn], [AA["bp"][j], AA["bs"][j]]), accs[0][:, 0:nn], accs[1][:, 0:nn], ALU.mult)
            if q == 0 and l + 1 < DEPTH and j < 16:
                mod_block(l + 1, j)
                if j == 15:
                    mod_finish(l + 1)
        out_proj(l, 1, wdn_d[l], NPAIR, AA, FF, segs)

    UPSP = [kb.sbn(f"upsp{c}", 16 * HP) for c in range(2)]
    UPSP = [V(u.ap.rearrange("p (b r) -> p b r", r=HP), u.bufs) for u in UPSP]
    ZSP = [kb.sbn(f"zsp{c}", 16 * HS) for c in range(3)]
    ZSP = [V(u.ap.rearrange("p (b r) -> p b r", r=HS), u.bufs) for u in ZSP]
    pool_tails = []
    sconv_tails = []
    print("SBUF words used (final)", kb.top, "of", kb.nwords)

    try:
        chk('setup')
        compute_mod(0)
        chk('mod0')
        for q in range(NPIECE):
            last = (q == NPIECE - 1)
            segs = ["p", "s"] if last else ["p"]
            blocks = [(xp_d, q * NPC + tb * 128, 128, tb * 128) for tb in range(NPC // 128)]
            if last:
                blocks.append((xs_d, 0, NS, NPC))
            for bi, (src, r0, nr, c0) in enumerate(blocks):
                io = IOX[bi % 2]
                iov = V(io.ap[0:nr, :], io.bufs)
                kb.dma(sp, iov.ap, src[r0:r0 + nr, :], writes=[iov])
                sl = rslot() if NT == 2 else None
                if NT == 2:
                    kb.MM(lambda iov=iov, sl=sl, nr=nr: [nc.tensor.transpose(out=sl.ap[:, k * 128:k * 128 + nr], in_=iov.ap[:, k * 128:(k + 1) * 128],
                                                                             identity=ident_f.ap[0:nr, 0:nr]) for k in range(KC)][-1],
                          [iov, ident_f], [sl])
                    dst = V(X["full"][:, :, c0:c0 + nr], X["bp"] + X["bs"])
                    kb.A(dst, sl[:, 0:KC * 128].re("p (k n) -> p k n", k=KC)[:, :, 0:nr], AF.Copy)
                else:
                    for hk in range(2):
                        sl = rslot()
                        kb.MM(lambda iov=iov, sl=sl, nr=nr, hk=hk: [nc.tensor.transpose(out=sl.ap[:, k * 128:k * 128 + nr],
                                                                                        in_=iov.ap[:, (hk * 4 + k) * 128:(hk * 4 + k + 1) * 128],
                                                                                        identity=ident_f.ap[0:nr, 0:nr]) for k in range(4)][-1],
                              [iov, ident_f], [sl])
                        dst = V(X["full"][:, hk * 4:hk * 4 + 4, c0:c0 + nr], X["bp"] + X["bs"])
                        kb.A(dst, sl[:, 0:512].re("p (k n) -> p k n", k=4)[:, :, 0:nr], AF.Copy)
            chk(f'xload{q}')
            for l in range(DEPTH):
                mix(l, q, segs)
                chk(f'mix{q}_{l}')
                ffn(l, q, segs)
                chk(f'ffn{q}_{l}')
            for bi, (src, r0, nr, c0) in enumerate(blocks):
                io = IOX[bi % 2]
                iov = V(io.ap[0:nr, :], io.bufs)
                for hk in range(2):
                    sl = rslot()
                    xr = [X["p"][hk * 4 + k] if c0 < NPC else X["s"][hk * 4 + k] for k in range(4)]
                    kb.MM(lambda sl=sl, nr=nr, c0=c0, hk=hk: [nc.tensor.transpose(out=sl.ap[0:nr, k * 128:(k + 1) * 128],
                                                                                  in_=X["full"][:, hk * 4 + k, c0:c0 + nr],
                                                                                  identity=ident_f.ap) for k in range(4)][-1],
                          xr + [ident_f], [sl])
                    kb.A(iov[:, hk * 512:(hk + 1) * 512], sl[0:nr, 0:512], AF.Copy)
                dstd = (yp_d if src is xp_d else ys_d)[r0:r0 + nr, :]
                kb.dma(sp, dstd, iov.ap, reads=[iov], is_out=True)
    except _Stop:
        pass

    best = {}
    for (s, v) in kb.out_toks:
        if best.get(s.key, (None, 0))[1] < v:
            best[s.key] = (s, v)
    for k, (s, v) in best.items():
        nc.sync.wait_ge(s.h, v)
    build.marks = kb.marks
    return nc


_NC = None


def kernel(x_prompt, x_sample, c_prompt, c_sample, state_pool, state_sconv, state_cconv, state_ffn,
           w_ada, b_ada, g_mix_pre, g_mix_post, g_ffn_pre, g_ffn_post,
           w_in, pool_w, pool_scale, sconv_w, cconv_w, cconv_b, cln_g, cln_b,
           w_out, w_up, ffn_conv_w, w_down):
    global _NC
    f = lambda a: np.ascontiguousarray(np.asarray(a, dtype=np.float32))
    x_prompt, x_sample, c_prompt, c_sample = map(f, (x_prompt, x_sample, c_prompt, c_sample))
    state_pool, state_sconv, state_cconv, state_ffn = map(f, (state_pool, state_sconv, state_cconv, state_ffn))
    rows = []
    for l in range(DEPTH):
        parts = [f(b_ada)[l].reshape(48, 128), f(g_mix_pre)[l].reshape(8, 128), f(g_mix_post)[l].reshape(8, 128),
                 f(g_ffn_pre)[l].reshape(8, 128), f(g_ffn_post)[l].reshape(8, 128), f(pool_scale)[l].reshape(2, 128),
                 f(sconv_w)[l].reshape(9, 128), f(cconv_w)[l].reshape(93, 128), f(cconv_b)[l].reshape(3, 128),
                 f(cln_g)[l].reshape(3, 128), f(cln_b)[l].reshape(3, 128), f(ffn_conv_w)[l].reshape(132, 128)]
        rows.append(np.concatenate(parts, axis=0))
    ptab = np.ascontiguousarray(np.stack(rows, 0))
    cw = f(cconv_w)
    dgt = np.zeros((DEPTH, 3, 128, 31, 128), np.float32)
    ar = np.arange(128)
    for c in range(3):
        dgt[:, c, ar, :, ar] = cw[:, :, c * 128:(c + 1) * 128].transpose(2, 0, 1)
    dgt = dgt.reshape(DEPTH, 3, 128, 31 * 128)
    shared = {"w_ada": f(w_ada), "ptab": ptab, "dgt": dgt, "w_in": f(w_in), "pool_w": f(pool_w), "w_out": f(w_out),
              "w_up": f(w_up), "w_down": f(w_down)}
    in_maps = []
    for i in range(NCORES):
        sl = slice(16 * i, 16 * i + 16)
        m = dict(shared)
        m["xp"] = x_prompt[i]
        m["xs"] = np.ascontiguousarray(x_sample[sl].reshape(NS, D))
        m["cc"] = np.ascontiguousarray(np.concatenate([c_prompt[i:i + 1], c_sample[sl]], axis=0))
        m["st_pool"] = np.ascontiguousarray(state_pool[:, sl].reshape(DEPTH, 16 * HP, DP))
        m["st_sconv"] = np.ascontiguousarray(state_sconv[:, sl].reshape(DEPTH, 16 * HS, DS))
        m["st_cconv"] = np.ascontiguousarray(state_cconv[:, sl].reshape(DEPTH, 16 * HC, DC))
        m["st_ffn"] = np.ascontiguousarray(state_ffn[:, sl].reshape(DEPTH, 16 * HF, 2 * DFF))
        in_maps.append(m)
    if _NC is None:
        _NC = build()
    res = run_bass_kernel_spmd(_NC, in_maps, core_ids=list(range(NCORES)))
    R = res.results
    cat = lambda k, ax: np.concatenate([np.asarray(R[i][k]) for i in range(NCORES)], axis=ax)
    yp = np.stack([np.asarray(R[i]["yp"]) for i in range(NCORES)], 0)
    ys = cat("ys", 0).reshape(128, 4, D)
    outs = [yp, ys]
    for k in ("o_pool_p", "o_sconv_p", "o_cconv_p", "o_ffn_p"):
        outs.append(np.stack([np.asarray(R[i][k]) for i in range(NCORES)], 1))
    for k in ("o_pool_s", "o_sconv_s", "o_cconv_s", "o_ffn_s"):
        outs.append(cat(k, 1))
    return tuple(np.ascontiguousarray(o.astype(np.float32)) for o in outs)
```

```python
import numpy as np
import concourse.bass as bass
import concourse.mybir as mybir
from concourse.bass_utils import run_bass_kernel_spmd

F32 = mybir.dt.float32
BF16 = mybir.dt.bfloat16
AF = mybir.ActivationFunctionType
ALU = mybir.AluOpType

NCORES = 8
D = 1024
KC = 8
DEPTH = 4
SEQ = 2048
NSEQ_S = 16
NS = 64
DP, DS, DC, DFF = 256, 384, 384, 2816
DIN = 2176
NFC = 44
NPAIR = 22
EPS = 1e-6
NPC = 512
NPIECE = SEQ // NPC
NT = NPC // 512
NCOL = NPC + NS
HP, HS, HC, HF = 15, 2, 30, 2

PO = {}
_o = 0
for _n, _r in (("b_ada", 48), ("g_mix_pre", 8), ("g_mix_post", 8), ("g_ffn_pre", 8), ("g_ffn_post", 8),
               ("pool_scale", 2), ("sconv_w", 9), ("cconv_w", 93), ("cconv_b", 3), ("cln_g", 3),
               ("cln_b", 3), ("ffn_conv_w", 132)):
    PO[_n] = _o
    _o += _r
NPT = _o
DBG_SKIP = False


class _Stop(Exception):
    pass


class Buf:
    __slots__ = ("name", "space", "lo", "hi", "w", "r", "ov")

    def __init__(self, name, space, lo, hi):
        self.name, self.space, self.lo, self.hi = name, space, lo, hi
        self.w = None
        self.r = {}
        self.ov = None


class V:
    def __init__(self, ap, bufs):
        self.ap = ap
        self.bufs = list(bufs)

    def __getitem__(self, key):
        return V(self.ap[key], self.bufs)

    def re(self, s, **kw):
        return V(self.ap.rearrange(s, **kw), self.bufs)

    def bc(self, shape):
        return V(self.ap.to_broadcast(shape), self.bufs)

    def us(self, axis):
        return V(self.ap.unsqueeze(axis), self.bufs)


class Sem:
    def __init__(self, h, key):
        self.h, self.key, self.total = h, key, 0


class Eng:
    def __init__(self, nc, name, h, is_pe=False):
        self.name, self.h, self.is_pe = name, h, is_pe
        self.sem = Sem(nc.alloc_semaphore("s_" + name), "s_" + name)
        self.count = 0
        self.seen = {}
        self.dsems = []
        self.di = 0


class KB:
    def __init__(self):
        nc = bass.Bass("TRN2", target_bir_lowering=False)
        self.nc = nc
        self.pe = Eng(nc, "pe", nc.tensor, True)
        self.act = Eng(nc, "act", nc.scalar)
        self.dve = Eng(nc, "dve", nc.vector)
        self.pool = Eng(nc, "pool", nc.gpsimd)
        self.sp = Eng(nc, "sp", nc.sync)
        for e, n in ((self.sp, 14), (self.pool, 12)):
            for i in range(n):
                e.dsems.append(Sem(nc.alloc_semaphore(f"d_{e.name}{i}"), f"d_{e.name}{i}"))
        self.bufs = {"sb": [], "ps": []}
        self.nwords = 53100
        self.arena = nc.alloc_sbuf_tensor("arena", [128, self.nwords], F32)
        self.psum = nc.alloc_psum_tensor("psum", [128, 4096], F32)
        self.top = 0
        self.out_toks = []
        self.opn = 0
        self.stop_n = 0
        self.marks = {}

    def mkbuf(self, name, space, lo, hi):
        b = Buf(name, space, lo, hi)
        lst = self.bufs[space]
        b.ov = [b]
        for o in lst:
            if o.lo < hi and lo < o.hi:
                o.ov.append(b)
                b.ov.append(o)
        lst.append(b)
        return b

    def alloc(self, words):
        lo = self.top
        self.top += words
        assert self.top <= self.nwords, f"SBUF arena overflow {self.top}"
        return lo

    def sb(self, name, lo, words, dt=F32, shape=None):
        b = self.mkbuf(name, "sb", lo, lo + words)
        ap = self.arena[:, lo:lo + words]
        if dt != F32:
            ap = ap.bitcast(dt)
        return V(ap, [b])

    def sbn(self, name, words, dt=F32):
        return self.sb(name, self.alloc(words), words, dt)

    def ps(self, name, lo, n):
        b = self.mkbuf(name, "ps", lo, lo + n)
        return V(self.psum[:, lo:lo + n], [b])

    def _deps(self, eng, reads, writes):
        toks = {}
        for b in reads:
            for ob in b.ov:
                if ob.w is not None:
                    s, v = ob.w
                    if toks.get(s.key, (None, 0))[1] < v:
                        toks[s.key] = (s, v)
        for b in writes:
            for ob in b.ov:
                if ob.w is not None:
                    s, v = ob.w
                    if toks.get(s.key, (None, 0))[1] < v:
                        toks[s.key] = (s, v)
                for k, (s, v) in ob.r.items():
                    if toks.get(k, (None, 0))[1] < v:
                        toks[k] = (s, v)
        for k, (s, v) in toks.items():
            if eng.is_pe and s is eng.sem:
                continue
            if eng.seen.get(k, 0) >= v:
                continue
            eng.h.wait_ge(s.h, v)
            eng.seen[k] = v

    def _commit(self, tok, reads, writes):
        s, v = tok
        ws = set(id(b) for b in writes)
        for b in writes:
            b.w = tok
            b.r = {}
        for b in reads:
            if id(b) in ws:
                continue
            if b.r.get(s.key, (None, 0))[1] < v:
                b.r[s.key] = (s, v)

    def emit(self, eng, fn, reads, writes):
        self.opn += 1
        if self.stop_n and self.opn == self.stop_n:
            raise _Stop()
        rb = [b for v in reads for b in v.bufs]
        wb = [b for v in writes for b in v.bufs]
        self._deps(eng, rb, wb)
        ins = fn()
        eng.count += 1
        ins.then_inc(eng.sem.h, 1)
        tok = (eng.sem, eng.count)
        self._commit(tok, rb, wb)
        return tok

    def dma(self, eng, out, in_, reads=(), writes=(), is_out=False):
        self.opn += 1
        if self.stop_n and self.opn == self.stop_n:
            raise _Stop()
        rb = [b for v in reads for b in v.bufs]
        wb = [b for v in writes for b in v.bufs]
        s = eng.dsems[eng.di % len(eng.dsems)]
        eng.di += 1
        if s.total > 0 and eng.seen.get(s.key, 0) < s.total:
            eng.h.wait_ge(s.h, s.total)
            eng.seen[s.key] = s.total
        self._deps(eng, rb, wb)
        ins = eng.h.dma_start(out=out, in_=in_)
        ins.then_inc(s.h, 16)
        s.total += 16
        tok = (s, s.total)
        self._commit(tok, rb, wb)
        if is_out:
            self.out_toks.append(tok)
        return tok

    def A(self, out, in_, func, bias=None, scale=None, eng=None):
        rd = [in_] + [x for x in (bias, scale) if isinstance(x, V)]
        kw = {}
        if bias is not None:
            kw["bias"] = bias.ap if isinstance(bias, V) else bias
        if scale is not None:
            kw["scale"] = scale.ap if isinstance(scale, V) else scale
        return self.emit(self.act, lambda: self.nc.scalar.activation(out=out.ap, in_=in_.ap, func=func, **kw),
                         rd, [out])

    def TS(self, out, in0, s1, s2, op0, op1=None, eng=None):
        eng = eng or self.dve
        rd = [in0] + [x for x in (s1, s2) if isinstance(x, V)]
        a1 = s1.ap if isinstance(s1, V) else s1
        a2 = s2.ap if isinstance(s2, V) else s2
        kw = {}
        if op1 is not None:
            kw["op1"] = op1
        return self.emit(eng, lambda: eng.h.tensor_scalar(out=out.ap, in0=in0.ap, scalar1=a1, scalar2=a2,
                                                          op0=op0, **kw), rd, [out])

    def TT(self, out, in0, in1, op, eng=None):
        eng = eng or self.dve
        return self.emit(eng, lambda: eng.h.tensor_tensor(out=out.ap, in0=in0.ap, in1=in1.ap, op=op),
                         [in0, in1], [out])

    def STT(self, out, in0, sc, in1, op0, op1, eng=None):
        eng = eng or self.dve
        rd = [in0, in1] + ([sc] if isinstance(sc, V) else [])
        a = sc.ap if isinstance(sc, V) else sc
        return self.emit(eng, lambda: eng.h.scalar_tensor_tensor(out=out.ap, in0=in0.ap, scalar=a, in1=in1.ap,
                                                                 op0=op0, op1=op1), rd, [out])

    def RC(self, out, in_):
        return self.emit(self.dve, lambda: self.nc.vector.reciprocal(out=out.ap, in_=in_.ap), [in_], [out])

    def CP(self, out, in_, eng=None):
        eng = eng or self.dve
        return self.emit(eng, lambda: eng.h.tensor_copy(out=out.ap, in_=in_.ap), [in_], [out])

    def MS(self, out, val, eng=None):
        eng = eng or self.dve
        return self.emit(eng, lambda: eng.h.memset(out.ap, val), [], [out])

    def MM(self, fn, reads, writes):
        return self.emit(self.pe, fn, reads, writes)


def build(stop=None):
    kb = KB()

    def chk(tag):
        kb.marks[tag] = kb.opn
        if stop is not None and tag == stop:
            raise _Stop()
    nc = kb.nc
    pe, act, dve, pool, sp = kb.pe, kb.act, kb.dve, kb.pool, kb.sp

    def din(name, shape):
        return nc.dram_tensor(name, list(shape), F32, kind="ExternalInput").ap()

    def dout(name, shape):
        return nc.dram_tensor(name, list(shape), F32, kind="ExternalOutput").ap()

    xp_d = din("xp", [SEQ, D]); xs_d = din("xs", [NS, D]); cc_d = din("cc", [17, D])
    stp_d = din("st_pool", [DEPTH, 16 * HP, DP]); sts_d = din("st_sconv", [DEPTH, 16 * HS, DS])
    stc_d = din("st_cconv", [DEPTH, 16 * HC, DC]); stf_d = din("st_ffn", [DEPTH, 16 * HF, 2 * DFF])
    wada_d = din("w_ada", [DEPTH, D, 6 * D]); pt_d = din("ptab", [DEPTH, NPT, 128])
    win_d = din("w_in", [DEPTH, D, DIN]); pw_d = din("pool_w", [DEPTH, 4, 64, 64])
    wout_d = din("w_out", [DEPTH, D, D]); wup_d = din("w_up", [DEPTH, D, 2 * DFF])
    wdn_d = din("w_down", [DEPTH, DFF, D])
    yp_d = dout("yp", [SEQ, D]); ys_d = dout("ys", [NS, D])
    opp_d = dout("o_pool_p", [DEPTH, HP, DP]); osp_d = dout("o_sconv_p", [DEPTH, HS, DS])
    ocp_d = dout("o_cconv_p", [DEPTH, HC, DC]); ofp_d = dout("o_ffn_p", [DEPTH, HF, 2 * DFF])
    ops_d = dout("o_pool_s", [DEPTH, 16, HP, DP]); oss_d = dout("o_sconv_s", [DEPTH, 16, HS, DS])
    ocs_d = dout("o_cconv_s", [DEPTH, 16, HC, DC]); ofs_d = dout("o_ffn_s", [DEPTH, 16, HF, 2 * DFF])

    wada_d = [wada_d[l] for l in range(DEPTH)]; win_d = [win_d[l] for l in range(DEPTH)]
    wout_d = [wout_d[l] for l in range(DEPTH)]; wup_d = [wup_d[l] for l in range(DEPTH)]; wdn_d = [wdn_d[l] for l in range(DEPTH)]

    W_X = KC * NCOL
    x_lo = kb.alloc(W_X)
    hc_lo = kb.alloc(W_X)
    a_lo = kb.alloc(NPAIR * NCOL // 2)
    A_WORDS = NPAIR * NCOL // 2

    def chunked(name, lo, nch, dt):
        wpc = NCOL if dt == F32 else NCOL // 2
        full = kb.arena[:, lo:lo + nch * wpc]
        if dt != F32:
            full = full.bitcast(dt)
        full = full.rearrange("p (c n) -> p c n", c=nch)
        res = {"full": full, "p": [], "s": [], "bp": [], "bs": []}
        pw = NPC if dt == F32 else NPC // 2
        for c in range(nch):
            bp = kb.mkbuf(f"{name}{c}p", "sb", lo + c * wpc, lo + c * wpc + pw)
            bs = kb.mkbuf(f"{name}{c}s", "sb", lo + c * wpc + pw, lo + (c + 1) * wpc)
            res["p"].append(V(full[:, c, 0:NPC], [bp]))
            res["s"].append(V(full[:, c, NPC:NCOL], [bs]))
            res["bp"].append(bp); res["bs"].append(bs)
        res["sall"] = V(full[:, :, NPC:NCOL], res["bs"])
        return res

    X = chunked("x", x_lo, KC, F32)
    H = chunked("h", hc_lo, KC, BF16)
    CAT = chunked("cat", hc_lo + W_X // 2, KC, BF16)
    FF = chunked("ff", hc_lo, KC, F32)
    MX = chunked("mx", a_lo, KC, F32)
    AA = chunked("a", a_lo, NPAIR, BF16)

    WSLOT = 1536
    NWS = 5
    wring = [kb.sbn(f"wr{i}", WSLOT, BF16) for i in range(NWS)]
    wr_i = [0]
    whalf = [[kb.sb(f"wrh{i}_{g}", w_.bufs[0].lo + g * 512, 512, BF16) for g in range(2)] for i, w_ in enumerate(wring)]

    PT = kb.sbn("pt", DEPTH * NPT)
    PTv = V(PT.ap.rearrange("p (l n) -> p l n", l=DEPTH), PT.bufs)
    DVA = [kb.sbn(f"dv{l}", 6 * KC * 17) for l in range(DEPTH)]
    ident_f = kb.sbn("ident_f", 128); ident_b = kb.sbn("ident_b", 64, BF16)
    ones_b = kb.sbn("ones_b", 64, BF16); ones_f = kb.sbn("ones_f", 128)
    invcnt = kb.sbn("invcnt", 32)
    PW = kb.sbn("pw", DEPTH * 2 * 64, BF16)
    PWv = V(PW.ap.rearrange("p (l c n) -> p l c n", l=DEPTH, c=2), PW.bufs)
    CTt = kb.sbn("ct", 80, BF16)
    CTv = V(CTt.ap[:, 0:KC * 17].rearrange("p (k s) -> p k s", k=KC), CTt.bufs)
    RSp = kb.sbn("rsp", NPC); RSs = kb.sbn("rss", NS)
    SQ = [kb.sbn(f"sq{i}", NCOL // 2, BF16) for i in range(3)]
    TMP = [kb.sbn(f"tmp{i}", NCOL) for i in range(4)]
    sq_i = [0]; tmp_i = [0]

    def nsq():
        sq_i[0] += 1
        return SQ[sq_i[0] % len(SQ)]

    def ntmp():
        tmp_i[0] += 1
        return TMP[tmp_i[0] % len(TMP)]

    HALO_P = [kb.sbn(f"hp{l}", 2 * HP) for l in range(DEPTH)]
    HALO_S = [kb.sbn(f"hs{l}", 3 * HS) for l in range(DEPTH)]
    HALO_C = [kb.sbn(f"hcv{l}", 3 * HC // 2, BF16) for l in range(DEPTH)]
    HALO_F = [kb.sbn(f"hf{l}", NFC * HF // 2, BF16) for l in range(DEPTH)]

    def padded(name, lo, nch, Hh, dt):
        cols = Hh + NPC + 16 * (Hh + 4)
        colsw = cols if dt == F32 else (cols + 1) // 2
        res = {"cols": cols, "H": Hh, "pf": [], "sf": [], "words": nch * colsw}
        for c in range(nch):
            l0 = lo + c * colsw
            ap = kb.arena[:, l0:l0 + colsw]
            if dt != F32:
                ap = ap.bitcast(dt)
            pcw = (Hh + NPC) if dt == F32 else (Hh + NPC) // 2
            bp = kb.mkbuf(f"{name}{c}p", "sb", l0, l0 + pcw)
            bs = kb.mkbuf(f"{name}{c}s", "sb", l0 + pcw, l0 + colsw)
            res["pf"].append(V(ap[:, 0:Hh + NPC], [bp]))
            res["sf"].append(V(ap[:, Hh + NPC:Hh + NPC + 16 * (Hh + 4)].rearrange("p (b j) -> p b j", j=Hh + 4), [bs]))
        return res

    MODT = kb.sbn("modt", 48 * 17 + 16)
    STG = kb.sbn("stg", 1536)
    OST = kb.sbn("ost", 1536)
    IOX = [kb.sb(f"iox{i}", a_lo + i * 1024, 1024) for i in range(2)]
    m_lo = kb.top
    mlo = m_lo
    UP = padded("up", mlo, 1, HP, F32); mlo += UP["words"]
    W1 = padded("w1", mlo, 1, HP, F32); mlo += W1["words"]
    W2 = padded("w2", mlo, 1, HP, F32); mlo += W2["words"]
    PLS = []
    for i in range(2):
        PLS.append(kb.sb(f"pl{i}", mlo, NCOL // 2, BF16)); mlo += NCOL // 2
    ZB = padded("zb", mlo, 2, HS, F32); mlo += ZB["words"]
    GL = padded("gl", mlo, 3, HC, BF16); mlo += GL["words"]
    VB = kb.sb("vb", mlo, 3 * NCOL); mlo += 3 * NCOL
    VQ = kb.sb("vq", mlo, 3 * NCOL); mlo += 3 * NCOL
    LNM = kb.sb("lnm", mlo, 512); mlo += 512
    LNR = kb.sb("lnr", mlo, 512); mlo += 512
    GT = kb.sb("gt", mlo, 3 * 96); mlo += 3 * 96
    TAILS = kb.sb("tails", mlo, 3 * 96); mlo += 3 * 96
    DGS = []
    for i in range(2):
        DGS.append(kb.sb(f"dg{i}", mlo, 31 * 64, BF16)); mlo += 31 * 64
    m_hi = mlo
    flo = m_lo
    UPBw = (HF + NPC + 16 * (HF + 4)) // 2
    UPB = []
    NUPB, NACC = 6, 8
    for i in range(NUPB):
        UPB.append(padded(f"upb{i}", flo, 1, HF, BF16)); flo += UPBw
    ACC = []
    for i in range(NACC):
        ACC.append(kb.sb(f"acc{i}", flo, NCOL)); flo += NCOL
    PTMP = []
    for i in range(2):
        PTMP.append(kb.sb(f"ptmp{i}", flo, NCOL)); flo += NCOL
    FST = []
    for i in range(2):
        FST.append(kb.sb(f"fst{i}", flo, 11 * 34)); flo += 11 * 34
    UPS = kb.sb("ups", flo, NFC * 16 * HF // 2, BF16); flo += NFC * 16 * HF // 2
    kb.top = max(m_hi, flo)
    assert kb.top <= kb.nwords, f"SBUF overflow {kb.top}"
    print("SBUF words used", kb.top, "of", kb.nwords)

    RB = 8 - NT - 2
    nslots = RB // NT
    ring = [kb.ps(f"ring{i}", i * NT * 512, NT * 512) for i in range(nslots)]
    if NT == 1:
        ring.append(kb.ps("ring_b6", 6 * 512, 512))
        nslots += 1
    ring_i = [0]
    STATP = kb.ps("statp", RB * 512, NT * 512)
    STATS = kb.ps("stats", 7 * 512, 64)

    def rslot():
        ring_i[0] += 1
        return ring[ring_i[0] % nslots]

    def sslot():
        return rslot()

    def wslot():
        wr_i[0] += 1
        return wring[wr_i[0] % NWS]

    wplan = []

    def P_mod(l):
        for blk in range(16):
            wplan.append((wada_d[l], D, [(blk * 384, 384, 0)], 384))

    def P_mix(l):
        for (c0, n) in ((256, 384), (256 + 768, 384), (256 + 384, 384), (1408, 384), (1792, 384), (0, 256)):
            wplan.append((win_d[l], D, [(c0, n, 0)], n))
        for mo in range(KC):
            wplan.append((wout_d[l], D, [(mo * 128, 128, 0)], 128))

    def P_ffn(l, modl=None):
        for j in range(NPAIR):
            wplan.append((wup_d[l], D, [(j * 128, 128, "pair")], 256))
            if modl is not None and j < 16:
                wplan.append((wada_d[modl], D, [(j * 384, 384, 0)], 384))
        for mo in range(KC):
            wplan.append((wdn_d[l], DFF, [(mo * 128, 128, 0)], 128))

    P_mod(0)
    for q_ in range(NPIECE):
        for l_ in range(DEPTH):
            P_mix(l_)
            P_ffn(l_, (l_ + 1) if (q_ == 0 and l_ + 1 < DEPTH) else None)
    wst = {"issued": 0, "taken": 0, "views": {}}
    WLIVE = 3

    def w_issue(j):
        dram2d, nrows, parts, tcols = wplan[j]
        nkc = nrows // 128
        slot = wring[j % NWS]
        view = V(slot.ap[:, 0:nkc * tcols].rearrange("p (k c) -> p k c", k=nkc), slot.bufs)
        for (c0, ncols, coff) in parts:
            if coff == "pair":
                hv = []
                for g in range(2):
                    hb_ = whalf[j % NWS][g]
                    hview = V(hb_.ap.rearrange("p (k c) -> p k c", k=nkc), hb_.bufs)
                    src = dram2d[0:nrows, g * DFF + c0:g * DFF + c0 + ncols].rearrange("(k p) c -> p k c", p=128)
                    kb.dma(pool, hview.ap, src, writes=[hview])
                    hv.append(hview)
                view = hv
                continue
            src = dram2d[0:nrows, c0:c0 + ncols].rearrange("(k p) c -> p k c", p=128)
            kb.dma(pool, view.ap[:, :, coff:coff + ncols], src, writes=[view])
        wst["views"][j] = view

    def take_w(dram2d, c0, wl=WLIVE):
        i = wst["taken"]
        assert wplan[i][0] is dram2d and wplan[i][2][0][0] == c0, (i, c0, wplan[i][2])
        lim = min(len(wplan) - 1, i + NWS - wl)
        while wst["issued"] <= lim:
            w_issue(wst["issued"])
            wst["issued"] += 1
        wst["taken"] += 1
        return wst["views"].pop(i)

    iot = V(STG.ap[:, 0:128].bitcast(mybir.dt.int32), STG.bufs)
    kb.emit(pool, lambda: nc.gpsimd.iota(iot.ap, pattern=[[1, 128]], base=0, channel_multiplier=-1), [], [iot])
    kb.CP(ident_f, iot)
    kb.TS(ident_f, ident_f, 0.0, None, ALU.is_equal)
    kb.CP(ident_b, ident_f)
    kb.MS(ones_b, 1.0)
    kb.MS(ones_f, 1.0)
    icv = V(invcnt.ap[:, 0:30].rearrange("p (c t) -> p c t", c=2), invcnt.bufs)
    for ch in range(2):
        for half in range(2):
            w = (2, 4, 8, 16)[2 * ch + half]
            ps_ = slice(64 * half, 64 * half + 64)
            kb.MS(icv[ps_, ch, :], 1.0 / w)
            for t in range(w - 1):
                kb.MS(icv[ps_, ch, t:t + 1], 1.0 / (t + 1))
    for l in range(DEPTH):
        r = 0
        while r < NPT:
            n = min(128, NPT - r)
            st = V(STG.ap[0:n, 0:128], STG.bufs)
            kb.dma(sp, st.ap, pt_d[l, r:r + n, :], writes=[st])
            sl = rslot()
            kb.MM(lambda st=st, sl=sl, n=n: nc.tensor.transpose(out=sl.ap[:, 0:n], in_=st.ap, identity=ident_f.ap[0:n, 0:n]),
                  [st, ident_f], [sl])
            kb.A(PTv[:, l, r:r + n], sl[:, 0:n], AF.Copy)
            r += n
    kb.MS(PW, 0.0)
    for l in range(DEPTH):
        for g in range(4):
            ch, half = g // 2, g % 2
            dst = PWv[64 * half:64 * half + 64, l, ch, 64 * half:64 * half + 64]
            kb.dma(pool, dst.ap, pw_d[l, g, :, :], writes=[PWv])
    cst = V(STG.ap[0:17, 0:1024], STG.bufs)
    kb.dma(sp, cst.ap, cc_d[:, :], writes=[cst])
    kb.A(cst, cst, AF.Silu)
    sl = rslot()
    kb.MM(lambda: [nc.tensor.transpose(out=sl.ap[:, k * 17:(k + 1) * 17], in_=cst.ap[:, k * 128:(k + 1) * 128],
                                       identity=ident_f.ap[0:17, 0:17]) for k in range(KC)][-1],
          [cst, ident_f], [sl])
    kb.A(CTv, sl[:, 0:KC * 17].re("p (k s) -> p k s", k=KC), AF.Copy)

    def pcol(l, name, idx):
        o = PO[name] + idx
        return PTv[:, l, o:o + 1]

    def mod_block(l, blk):
        modv = V(MODT.ap[:, 0:48 * 17].rearrange("p (c s) -> p c s", c=48), MODT.bufs)
        wv = take_w(wada_d[l], blk * 384, wl=1)
        sl = rslot()

        def fn(wv=wv, sl=sl):
            last = None
            for j in range(3):
                for k in range(KC):
                    last = nc.tensor.matmul(sl.ap[:, j * 17:(j + 1) * 17], lhsT=wv.ap[:, k, j * 128:(j + 1) * 128],
                                            rhs=CTv.ap[:, k, :], start=(k == 0), stop=(k == KC - 1))
            return last
        kb.MM(fn, [wv, CTv], [sl])
        o = PO["b_ada"] + blk * 3
        kb.TT(modv[:, blk * 3:blk * 3 + 3, :], sl[:, 0:51].re("p (c s) -> p c s", c=3),
              PTv[:, l, o:o + 3].us(2).bc([128, 3, 17]), ALU.add)

    def mod_finish(l):
        modv = V(MODT.ap[:, 0:48 * 17].rearrange("p (c s) -> p c s", c=48), MODT.bufs)
        dv = V(DVA[l].ap.rearrange("p (w c s) -> p w c s", w=6, c=KC), DVA[l].bufs)
        for sub, (gpre, gpost) in enumerate((("g_mix_pre", "g_mix_post"), ("g_ffn_pre", "g_ffn_post"))):
            m0 = sub * 24
            gp = PTv[:, l, PO[gpre]:PO[gpre] + 8].us(2).bc([128, 8, 17])
            gq = PTv[:, l, PO[gpost]:PO[gpost] + 8].us(2).bc([128, 8, 17])
            kb.TS(dv[:, 3 * sub + 0], modv[:, m0 + 8:m0 + 16, :], 1.0, 32.0, ALU.add, ALU.mult)
            kb.TT(dv[:, 3 * sub + 0], dv[:, 3 * sub + 0], gp, ALU.mult)
            kb.CP(dv[:, 3 * sub + 1], modv[:, m0:m0 + 8, :])
            kb.TS(dv[:, 3 * sub + 2], modv[:, m0 + 16:m0 + 24, :], 32.0, None, ALU.mult)
            kb.TT(dv[:, 3 * sub + 2], dv[:, 3 * sub + 2], gq, ALU.mult)

    def compute_mod(l):
        for blk in range(16):
            mod_block(l, blk)
        mod_finish(l)

    def dvv(l):
        return V(DVA[l].ap.rearrange("p (w c s) -> p w c s", w=6, c=KC), DVA[l].bufs)

    def mm_chunk(lhs_fn, nk, rhs_p, rhs_s, segs, reads):
        outs = {}
        slp = rslot()
        outs["p"] = slp
        wr = [slp]
        if "s" in segs:
            sls = sslot()
            outs["s"] = sls
            wr.append(sls)

        def fn():
            last = None
            for k in range(nk):
                lh = lhs_fn(k)
                for t in range(NT):
                    last = nc.tensor.matmul(slp.ap[:, t * 512:(t + 1) * 512], lhsT=lh, rhs=rhs_p(k, t),
                                            start=(k == 0), stop=(k == nk - 1))
            if "s" in segs and not DBG_SKIP:
                for k in range(nk):
                    last = nc.tensor.matmul(sls.ap[:, 0:NS], lhsT=lhs_fn(k), rhs=rhs_s(k),
                                            start=(k == 0), stop=(k == nk - 1))
            return last
        kb.MM(fn, reads, wr)
        return outs

    def rms_stats(src, segs, sq_from_psum=None):
        for seg in segs:
            for k in range(KC):
                q = nsq()
                n = NPC if seg == "p" else NS
                qv = q[:, 0:n]
                kb.A(qv, src[seg][k], AF.Square)
                st = STATP if seg == "p" else STATS

                def fn(qv=qv, st=st, n=n, k=k):
                    last = None
                    for t in range(max(1, n // 512)):
                        w = min(512, n)
                        last = nc.tensor.matmul(st.ap[:, t * 512:t * 512 + w], lhsT=ones_b.ap, rhs=qv.ap[:, t * 512:t * 512 + w],
                                                start=(k == 0), stop=(k == KC - 1))
                    return last
                kb.MM(fn, [qv, ones_b], [st])

    def rstd_from_stats(segs):
        for seg in segs:
            st, rs = (STATP, RSp) if seg == "p" else (STATS, RSs)
            kb.A(rs, st, AF.Sqrt, bias=float(D * EPS))
            kb.RC(rs, rs)

    def prenorm(l, sub, segs):
        dv = dvv(l)
        rms_stats(X, segs)
        rstd_from_stats(segs)
        for k in range(KC):
            t = ntmp()
            kb.TT(t[:, 0:NPC], X["p"][k], RSp, ALU.mult)
            kb.A(H["p"][k], t[:, 0:NPC], AF.Identity, bias=dv[:, 3 * sub + 1, k, 0:1], scale=dv[:, 3 * sub + 0, k, 0:1])
        if "s" in segs:
            t = ntmp()
            tv = t[:, 0:KC * NS].re("p (k n) -> p k n", k=KC) if KC * NS <= NCOL else None
            kb.TT(tv, X["sall"], RSs.us(1).bc([128, KC, NS]), ALU.mult)
            t4 = tv.re("p k (b j) -> p k b j", j=4)
            kb.TT(t4, t4, dv[:, 3 * sub + 0, :, 1:17].us(3).bc([128, KC, 16, 4]), ALU.mult)
            kb.TT(H["sall"].re("p k (b j) -> p k b j", j=4), t4,
                  dv[:, 3 * sub + 1, :, 1:17].us(3).bc([128, KC, 16, 4]), ALU.add)

    def postnorm(l, sub, SRC, segs):
        dv = dvv(l)
        rstd_from_stats(segs)
        for k in range(KC):
            t = ntmp()
            kb.TT(t[:, 0:NPC], SRC["p"][k], RSp, ALU.mult)
            kb.TT(X["p"][k], X["p"][k], t[:, 0:NPC], ALU.add)
        if "s" in segs:
            t = ntmp()
            tv = t[:, 0:KC * NS].re("p (k n) -> p k n", k=KC)
            kb.TT(tv, SRC["sall"], RSs.us(1).bc([128, KC, NS]), ALU.mult)
            t4 = tv.re("p k (b j) -> p k b j", j=4)
            kb.TT(t4, t4, dv[:, 3 * sub + 2, :, 1:17].us(3).bc([128, KC, 16, 4]), ALU.mult)
            kb.TT(X["sall"], X["sall"], tv, ALU.add)

    def out_proj(l, sub, wd, nk, rhs_src, DST, segs):
        pend = None
        for mo in range(KC):
            wv = take_w(wd, mo * 128, wl=1)
            outs = mm_chunk(lambda k, wv=wv: wv.ap[:, k, :], nk,
                            lambda k, t: rhs_src["p"][k].ap[:, t * 512:(t + 1) * 512],
                            lambda k: rhs_src["s"][k].ap, segs,
                            [wv] + [rhs_src[s][k] for s in segs for k in range(nk)])
            if pend is not None:
                pend()
            qs = {}
            for seg in segs:
                if seg == "p":
                    kb.A(DST[seg][mo], outs[seg], AF.Identity, scale=dvv(l)[:, 3 * sub + 2, mo, 0:1])
                else:
                    kb.A(DST[seg][mo], outs[seg][:, 0:NS], AF.Copy)
                q = nsq()
                n = NPC if seg == "p" else NS
                qs[seg] = q[:, 0:n]
                kb.A(qs[seg], outs[seg] if seg == "p" else outs[seg][:, 0:NS], AF.Square)

            def mk(qs=qs, mo=mo):
                for seg in segs:
                    st = STATP if seg == "p" else STATS
                    n = NPC if seg == "p" else NS
                    qv = qs[seg]

                    def fn(qv=qv, st=st, n=n):
                        last = None
                        for t in range(max(1, n // 512)):
                            w = min(512, n)
                            last = nc.tensor.matmul(st.ap[:, t * 512:t * 512 + w], lhsT=ones_b.ap,
                                                    rhs=qv.ap[:, t * 512:t * 512 + w], start=(mo == 0), stop=(mo == KC - 1))
                        return last
                    kb.MM(fn, [qv, ones_b], [st])
            pend = mk
        pend()
        postnorm(l, sub, DST, segs)

    def store_rows(srcs, ncol, dsts):
        n = len(srcs)
        sl = rslot()
        kb.MM(lambda: [nc.tensor.transpose(out=sl.ap[0:ncol, i * 128:(i + 1) * 128], in_=srcs[i].ap, identity=ident_f.ap)
                       for i in range(n)][-1], list(srcs) + [ident_f], [sl])
        ov = V(OST.ap[0:ncol, 0:n * 128], OST.bufs)
        kb.A(ov, sl[0:ncol, 0:n * 128], AF.Copy)
        for (r0, nr, dfn) in dsts:
            kb.dma(sp, dfn(), OST.ap[r0:r0 + nr, 0:n * 128], reads=[ov], is_out=True)

    def build_diag(l, c, bi):
        dgb = DGS[bi]
        dgv = V(dgb.ap.rearrange("p (k n) -> p k n", k=31), dgb.bufs)
        o = PO["cconv_w"] + c
        wv_ = PTv[:, l, o:o + 93:3].us(2).bc([128, 31, 128])
        kb.TT(dgv, ident_b.us(1).bc([128, 31, 128]), wv_, ALU.mult)
        return dgv

    def mix(l, q, segs):
        last = (q == NPIECE - 1)
        first = (q == 0)
        prenorm(l, 0, segs)
        dgs_built = [build_diag(l, 0, 0)]
        chk(f'mixP{q}_{l}')
        hreads = [H[s][k] for s in segs for k in range(KC)]

        def win_chunk(wv, c0):
            return mm_chunk(lambda k: wv.ap[:, k, c0:c0 + 128], KC,
                            lambda k, t: H["p"][k].ap[:, t * 512:(t + 1) * 512],
                            lambda k: H["s"][k].ap, segs, [wv] + hreads)

        if last:
            for tI in range(2):
                st = V(STG.ap[0:120, 0:256], STG.bufs)
                kb.dma(sp, st.ap, stp_d[l, tI * 120:(tI + 1) * 120, :], writes=[st])
                sl = rslot()
                kb.MM(lambda st=st, sl=sl: [nc.tensor.transpose(out=sl.ap[:, c * 120:(c + 1) * 120], in_=st.ap[:, c * 128:(c + 1) * 128],
                                                                 identity=ident_f.ap[0:120, 0:120]) for c in range(2)][-1],
                      [st, ident_f], [sl])
                for c in range(2):
                    dstv = UPSP[c][:, tI * 8:(tI + 1) * 8, :]
                    kb.A(dstv, sl[:, c * 120:(c + 1) * 120].re("p (b r) -> p b r", r=HP), AF.Copy)
            chk(f'mixS1{q}_{l}')
            st = V(STG.ap[0:32, 0:384], STG.bufs)
            kb.dma(sp, st.ap, sts_d[l, :, :], writes=[st])
            sl = rslot()
            kb.MM(lambda: [nc.tensor.transpose(out=sl.ap[:, c * 32:(c + 1) * 32], in_=st.ap[:, c * 128:(c + 1) * 128],
                                               identity=ident_f.ap[0:32, 0:32]) for c in range(3)][-1], [st, ident_f], [sl])
            for c in range(3):
                kb.A(ZSP[c], sl[:, c * 32:(c + 1) * 32].re("p (b r) -> p b r", r=HS), AF.Copy)
            chk(f'mixS2{q}_{l}')
            for c in range(3):
                sl = rslot()
                sts_ = []
                for tI in range(4):
                    st = V(STG.ap[0:120, tI * 384:(tI + 1) * 384], STG.bufs)
                    if c == 0:
                        kb.dma(sp, st.ap, stc_d[l, tI * 120:(tI + 1) * 120, :], writes=[st])
                    sts_.append(st)
                kb.MM(lambda sl=sl, c=c, sts_=sts_: [nc.tensor.transpose(out=sl.ap[:, tI * 120:(tI + 1) * 120],
                                                                          in_=sts_[tI].ap[:, c * 128:(c + 1) * 128],
                                                                          identity=ident_f.ap[0:120, 0:120]) for tI in range(4)][-1],
                      sts_ + [ident_f], [sl])
                chk(f'mixS3{q}_{l}_{c}')
                kb.CP(GL["sf"][c][:, :, 0:HC], sl[:, 0:480].re("p (b r) -> p b r", r=HC))
                chk(f'mixS4{q}_{l}_{c}')

        chk(f'mixA{q}_{l}')
        chk(f'mixB{q}_{l}')
        whb = take_w(win_d[l], 256)
        wcg = take_w(win_d[l], 256 + 768)
        wbg = take_w(win_d[l], 256 + 384)
        for c in range(3):
            zi = c % 2
            Z, Zs = ZB["pf"][zi], ZB["sf"][zi]
            o_hb = win_chunk(whb, c * 128)
            hb = ntmp()
            kb.A(hb[:, 0:NPC], o_hb["p"], AF.Copy)
            if "s" in segs:
                kb.A(hb[:, NPC:NCOL], o_hb["s"][:, 0:NS], AF.Copy)
            o_cg = win_chunk(wcg, c * 128)
            if first:
                kb.MS(Z[:, 0:HS], 0.0)
            else:
                kb.CP(Z[:, 0:HS], V(HALO_S[l].ap[:, c * HS:(c + 1) * HS], HALO_S[l].bufs))
            kb.TT(Z[:, HS:HS + NPC], o_cg["p"], hb[:, 0:NPC], ALU.mult)
            if not last:
                kb.CP(V(HALO_S[l].ap[:, c * HS:(c + 1) * HS], HALO_S[l].bufs), Z[:, NPC:NPC + HS])
            if "s" in segs:
                kb.CP(Zs[:, :, 0:HS], ZSP[c])
                kb.TT(Zs[:, :, HS:HS + 4], o_cg["s"][:, 0:NS].re("p (b j) -> p b j", j=4),
                      hb[:, NPC:NCOL].re("p (b j) -> p b j", j=4), ALU.mult)
            ca = ntmp()
            for (zv, cav, is3) in [(Z, ca[:, 0:NPC], False)] + ([(Zs, ca[:, NPC:NCOL].re("p (b j) -> p b j", j=4), True)] if "s" in segs else []):
                n = 4 if is3 else NPC
                for kk in range(3):
                    src = zv[:, :, kk:kk + n] if is3 else zv[:, kk:kk + n]
                    wk = pcol(l, "sconv_w", kk * 3 + c)
                    if kk == 0:
                        kb.TS(cav, src, wk, None, ALU.mult)
                    else:
                        kb.STT(cav, src, wk, cav, ALU.mult, ALU.add)
            if last:
                t = TAILS
                kb.CP(t[:, c * 96:c * 96 + HS], Z[:, NPC:NPC + HS])
                kb.CP(t[:, c * 96 + HS:c * 96 + HS + 32].re("p (j b) -> p j b", j=2), Zs[:, :, HS + 2:HS + 4].re("p b j -> p j b"))
                sconv_tails.append(t[:, c * 96:c * 96 + HS + 32])
            o_bg = win_chunk(wbg, c * 128)
            kb.TT(CAT["p"][2 + c], o_bg["p"], ca[:, 0:NPC], ALU.mult)
            if "s" in segs:
                kb.TT(CAT["s"][2 + c], o_bg["s"][:, 0:NS], ca[:, NPC:NCOL], ALU.mult)
        if last:
            store_rows(sconv_tails, HS + 32,
                       [(0, HS, lambda: osp_d[l, :, :])] +
                       [(HS + 16 * j, 16, (lambda j=j: oss_d[l, :, j, :])) for j in range(2)])
            sconv_tails.clear()

        dgs_built.append(build_diag(l, 1, 1))
        chk(f'mixC{q}_{l}')
        wac = take_w(win_d[l], 1408)
        wbc = take_w(win_d[l], 1792)
        gtv = V(GT.ap[:, 0:288].rearrange("p (c n) -> p c n", c=3), GT.bufs)
        for c in range(3):
            G, Gs = GL["pf"][c], GL["sf"][c]
            o_b = win_chunk(wbc, c * 128)
            sg = ntmp()
            kb.A(sg[:, 0:NPC], o_b["p"], AF.Sigmoid)
            if "s" in segs:
                kb.A(sg[:, NPC:NCOL], o_b["s"][:, 0:NS], AF.Sigmoid)
            o_a = win_chunk(wac, c * 128)
            if first:
                kb.MS(G[:, 0:HC], 0.0)
            else:
                kb.CP(G[:, 0:HC], V(HALO_C[l].ap[:, c * HC:(c + 1) * HC], HALO_C[l].bufs))
            kb.TT(G[:, HC:HC + NPC], o_a["p"], sg[:, 0:NPC], ALU.mult)
            if not last:
                kb.CP(V(HALO_C[l].ap[:, c * HC:(c + 1) * HC], HALO_C[l].bufs), G[:, NPC:NPC + HC])
            if "s" in segs:
                kb.TT(Gs[:, :, HC:HC + 4], o_a["s"][:, 0:NS].re("p (b j) -> p b j", j=4),
                      sg[:, NPC:NCOL].re("p (b j) -> p b j", j=4), ALU.mult)
            if last:
                kb.TT(gtv[:, c, 0:HC], o_a["p"][:, NPC - HC:NPC], sg[:, NPC - HC:NPC], ALU.mult)
                kb.TT(gtv[:, c, HC:HC + NS].re("p (j b) -> p j b", j=4), o_a["s"][:, 0:NS].re("p (b j) -> p j b", j=4),
                      sg[:, NPC:NCOL].re("p (b j) -> p j b", j=4), ALU.mult)
        if last:
            store_rows([gtv[:, c, 0:HC + NS] for c in range(3)], HC + NS,
                       [(0, HC, lambda: ocp_d[l, :, :])] +
                       [(HC + 16 * j, 16, (lambda j=j: ocs_d[l, :, 26 + j, :])) for j in range(4)])
            kb.dma(sp, ocs_d[l, :, 0:26, :], stc_d[l].rearrange("(b r) c -> b r c", r=HC)[:, 4:HC, :], is_out=True)
        wv = take_w(win_d[l], 0)
        for ch in range(2):
            outs = win_chunk(wv, ch * 128)
            E = UP["pf"][0]
            Es = UP["sf"][0]
            if first:
                kb.MS(E[:, 0:HP], 0.0)
            else:
                kb.CP(E[:, 0:HP], V(HALO_P[l].ap[:, ch * HP:(ch + 1) * HP], HALO_P[l].bufs))
            kb.A(E[:, HP:HP + NPC], outs["p"], AF.Copy)
            if not last:
                kb.CP(V(HALO_P[l].ap[:, ch * HP:(ch + 1) * HP], HALO_P[l].bufs), E[:, NPC:NPC + HP])
            if "s" in segs:
                kb.CP(Es[:, :, 0:HP], UPSP[ch])
                kb.A(Es[:, :, HP:HP + 4], outs["s"][:, 0:NS].re("p (b j) -> p b j", j=4), AF.Copy)
            L = HP + NPC
            views = [(E, W1["pf"][0], W2["pf"][0], L, None)]
            if "s" in segs:
                views.append((Es, W1["sf"][0], W2["sf"][0], HP + 4, 1))
            for (e, w1, w2, Ln, is3) in views:
                def sl_(v, a, b):
                    return v[:, :, a:b] if is3 else v[:, a:b]
                kb.TT(sl_(w1, 1, Ln), sl_(e, 1, Ln), sl_(e, 0, Ln - 1), ALU.add)
                kb.TT(sl_(w2, 3, Ln), sl_(w1, 3, Ln), sl_(w1, 1, Ln - 2), ALU.add)
                if ch == 1 and not (is3 and DBG_SKIP):
                    kb.TT(sl_(w1, 7, Ln), sl_(w2, 7, Ln), sl_(w2, 3, Ln - 4), ALU.add)
                    kb.TT(sl_(w2, 15, Ln), sl_(w1, 15, Ln), sl_(w1, 7, Ln - 8), ALU.add)
                for half, wsrc in ((0, w1), (1, w2)):
                    wdw = (2, 4, 8, 16)[2 * ch + half]
                    pr = slice(64 * half, 64 * half + 64)
                    if is3:
                        o_ = PLS[ch][pr, NPC:NCOL].re("p (b j) -> p b j", j=4)
                        kb.STT(o_, wsrc[pr, :, HP:HP + 4], 1.0 / wdw, e[pr, :, HP:HP + 4], ALU.mult, ALU.subtract)
                    else:
                        kb.STT(PLS[ch][pr, 0:NPC], wsrc[pr, HP:Ln], 1.0 / wdw, e[pr, HP:Ln], ALU.mult, ALU.subtract)
                        if first:
                            t = ntmp()
                            kb.TT(t[pr, 0:HP], wsrc[pr, HP:2 * HP], icv[pr, ch, :], ALU.mult)
                            kb.TT(PLS[ch][pr, 0:HP], t[pr, 0:HP], e[pr, HP:2 * HP], ALU.subtract)
            chk(f'mixB1{q}_{l}_{ch}')
            if last:
                t = TAILS
                kb.CP(t[:, ch * 96:ch * 96 + HP], E[:, NPC:NPC + HP])
                kb.CP(t[:, ch * 96 + HP:ch * 96 + HP + NS].re("p (j b) -> p j b", j=4), Es[:, :, HP:HP + 4].re("p b j -> p j b"))
                pool_tails.append(t[:, ch * 96:ch * 96 + HP + NS])
        chk(f'mixB3{q}_{l}')
        if last:
            store_rows(pool_tails, HP + NS,
                       [(0, HP, lambda: opp_d[l, :, :])] +
                       [(HP + 16 * j, 16, (lambda j=j: ops_d[l, :, 11 + j, :])) for j in range(4)])
            pool_tails.clear()
            chk(f'mixB4{q}_{l}')
            kb.dma(sp, ops_d[l, :, 0:11, :], stp_d[l].rearrange("(b r) c -> b r c", r=HP)[:, 4:HP, :], is_out=True)

        chk(f'mixD{q}_{l}')
        units = [("p", t) for t in range(NT)] + ([("s", 0)] if "s" in segs else [])
        NVB = NPC + NS
        vbv = V(VB.ap.rearrange("p (c n) -> p c n", c=3), VB.bufs)
        vqv = V(VQ.ap.rearrange("p (c n) -> p c n", c=3), VQ.bufs)
        for c in range(3):
            dgv = dgs_built[c]
            G, Gs = GL["pf"][c], GL["sf"][c]
            for (seg, t) in units:
                n = 512 if seg == "p" else NS
                o0 = t * 512 if seg == "p" else NPC
                if seg == "p":
                    sl = rslot()
                    kb.MM(lambda sl=sl, G=G, t=t, dgv=dgv: [nc.tensor.matmul(sl.ap[:, 0:512], lhsT=dgv.ap[:, kk, :],
                                                                     rhs=G.ap[:, kk + t * 512:kk + t * 512 + 512],
                                                                     start=(kk == 0), stop=(kk == 30)) for kk in range(31)][-1],
                          [dgv, G], [sl])
                    src = sl[:, 0:512]
                else:
                    sl = sslot()
                    kb.MM(lambda sl=sl, Gs=Gs, dgv=dgv: [nc.tensor.matmul(sl.ap[:, 0:NS], lhsT=dgv.ap[:, kk, :],
                                                                  rhs=Gs.ap[:, :, kk:kk + 4],
                                                                  start=(kk == 0), stop=(kk == 30)) for kk in range(31)][-1],
                          [dgv, Gs], [sl])
                    src = sl[:, 0:NS]
                kb.A(vbv[:, c, o0:o0 + n], src, AF.Identity, bias=pcol(l, "cconv_b", c))
                kb.A(vqv[:, c, o0:o0 + n], vbv[:, c, o0:o0 + n], AF.Square)
            if c == 0:
                dgs_built.append(build_diag(l, 2, 0))
        for ch in range(2):
            chk(f'mixB2{q}_{l}_{ch}')
            o2 = mm_chunk(lambda k: PWv.ap[:, l, ch, :], 1,
                          lambda k, t: PLS[ch].ap[:, t * 512:(t + 1) * 512], lambda k: PLS[ch].ap[:, NPC:NCOL], segs, [PWv, PLS[ch]])
            for seg in segs:
                kb.A(CAT[seg][ch], o2[seg] if seg == "p" else o2[seg][:, 0:NS], AF.Identity, scale=pcol(l, "pool_scale", ch))
            chk(f'mixB5{q}_{l}_{ch}')
        for (seg, t) in units:
            n = 512 if seg == "p" else NS
            o0 = t * 512 if seg == "p" else NPC
            vbu = vbv[:, :, o0:o0 + n]
            vqu = vqv[:, :, o0:o0 + n]
            s1 = rslot()
            s2 = rslot()
            for (sv, srcv) in ((s1, vbu), (s2, vqu)):
                kb.MM(lambda sv=sv, srcv=srcv, n=n: [nc.tensor.matmul(sv.ap[:, 0:n], lhsT=ones_f.ap, rhs=srcv.ap[:, c, :],
                                                                      start=(c == 0), stop=(c == 2)) for c in range(3)][-1],
                      [srcv, ones_f], [sv])
            kb.TS(LNM[:, 0:n], s1[:, 0:n], 1.0 / DC, None, ALU.mult)
            t_ = ntmp()
            kb.TT(t_[:, 0:n], LNM[:, 0:n], LNM[:, 0:n], ALU.mult)
            kb.STT(LNR[:, 0:n], s2[:, 0:n], 1.0 / DC, t_[:, 0:n], ALU.mult, ALU.subtract)
            kb.A(LNR[:, 0:n], LNR[:, 0:n], AF.Sqrt, bias=float(EPS))
            kb.RC(LNR[:, 0:n], LNR[:, 0:n])
            for c in range(3):
                kb.TT(vbu[:, c], vbu[:, c], LNM[:, 0:n], ALU.subtract)
                kb.TT(vbu[:, c], vbu[:, c], LNR[:, 0:n], ALU.mult)
                dst = CAT["p"][5 + c][:, t * 512:t * 512 + 512] if seg == "p" else CAT["s"][5 + c]
                kb.A(dst, vbu[:, c], AF.Silu, bias=pcol(l, "cln_b", c), scale=pcol(l, "cln_g", c))

        chk(f'mixE{q}_{l}')
        out_proj(l, 0, wout_d[l], KC, CAT, MX, segs)

    def ffn(l, q, segs):
        last = (q == NPIECE - 1)
        first = (q == 0)
        prenorm(l, 1, segs)
        hreads = [H[s][k] for s in segs for k in range(KC)]
        upsv = V(UPS.ap.rearrange("p (c b r) -> p c b r", c=NFC, b=16), UPS.bufs)
        fstvs = [V(f_.ap.rearrange("p (c n) -> p c n", c=11), f_.bufs) for f_ in FST]
        hfv = V(HALO_F[l].ap.rearrange("p (c r) -> p c r", c=NFC), HALO_F[l].bufs)
        if last:
            for rd in range(4):
                st = V(STG.ap[0:32, 0:1408], STG.bufs)
                kb.dma(sp, st.ap, stf_d[l, :, rd * 1408:(rd + 1) * 1408], writes=[st])
                sl = rslot()
                kb.MM(lambda st=st, sl=sl: [nc.tensor.transpose(out=sl.ap[:, c * 32:(c + 1) * 32], in_=st.ap[:, c * 128:(c + 1) * 128],
                                                                 identity=ident_f.ap[0:32, 0:32]) for c in range(11)][-1],
                      [st, ident_f], [sl])
                kb.A(upsv[:, rd * 11:(rd + 1) * 11], sl[:, 0:352].re("p (c b r) -> p c b r", c=11, b=16), AF.Copy)
        ub_i = 0
        ac_i = 0
        pt_i = 0
        if first:
            for ub in UPB:
                kb.MS(ub["pf"][0][:, 0:HF], 0.0)
        for j in range(NPAIR):
            wv = take_w(wup_d[l], j * 128, wl=1)
            accs = []
            for gv in range(2):
                ci = j + gv * NPAIR
                outs = mm_chunk(lambda k, gv=gv: wv[gv].ap[:, k, :], KC,
                                lambda k, t: H["p"][k].ap[:, t * 512:(t + 1) * 512],
                                lambda k: H["s"][k].ap, segs, [wv[gv]] + hreads)
                ub = UPB[ub_i % NUPB]; ub_i += 1
                acc = ACC[ac_i % NACC]; ac_i += 1
                U, Us = ub["pf"][0], ub["sf"][0]
                if not first:
                    kb.A(U[:, 0:HF], hfv[:, ci, :], AF.Copy)
                kb.A(U[:, HF:HF + NPC], outs["p"], AF.Copy)
                kb.A(acc[:, 0:NPC], outs["p"], AF.Identity, scale=pcol(l, "ffn_conv_w", 2 * NFC + ci))
                if not last:
                    kb.A(hfv[:, ci, :], U[:, NPC:NPC + HF], AF.Copy)
                if "s" in segs:
                    o4 = outs["s"][:, 0:NS].re("p (b j) -> p b j", j=4)
                    kb.A(Us[:, :, 0:HF], upsv[:, ci], AF.Copy)
                    kb.A(Us[:, :, HF:HF + 4], o4, AF.Copy)
                    kb.A(acc[:, NPC:NCOL], outs["s"][:, 0:NS], AF.Identity, scale=pcol(l, "ffn_conv_w", 2 * NFC + ci))
                if last:
                    rc = ci % 11
                    fstv = fstvs[gv]
                    kb.A(fstv[:, rc, 0:HF], outs["p"][:, NPC - HF:NPC], AF.Copy)
                    kb.A(fstv[:, rc, HF:HF + 32].re("p (j b) -> p j b", j=2), o4[:, :, 2:4].re("p b j -> p j b"), AF.Copy)
                for kk in (1, 0):
                    wk = pcol(l, "ffn_conv_w", kk * NFC + ci)
                    if True:
                        kb.STT(acc[:, 0:NPC], U[:, kk:kk + NPC], wk, acc[:, 0:NPC], ALU.mult, ALU.add)
                        if "s" in segs:
                            a4 = acc[:, NPC:NCOL].re("p (b j) -> p b j", j=4)
                            kb.STT(a4, Us[:, :, kk:kk + 4], wk, a4, ALU.mult, ALU.add)
                    else:
                        pt_ = PTMP[pt_i % 2]; pt_i += 1
                        kb.TS(pt_[:, 0:NPC], U[:, kk:kk + NPC], wk, None, ALU.mult, eng=pool)
                        if "s" in segs:
                            kb.TS(pt_[:, NPC:NCOL].re("p (b j) -> p b j", j=4), Us[:, :, kk:kk + 4], wk, None, ALU.mult, eng=pool)
                        nn_ = NCOL if "s" in segs else NPC
                        kb.TT(acc[:, 0:nn_], acc[:, 0:nn_], pt_[:, 0:nn_], ALU.add, eng=pool)
                accs.append(acc)
                if last and ci % 11 == 10:
                    rd = ci // 11
                    for g0 in range(0, 11, 4):
                        ng = min(4, 11 - g0)
                        cs = (rd * 11 + g0) * 128
                        store_rows([fstvs[gv][:, g0 + i, 0:34] for i in range(ng)], 34,
                                   [(0, HF, (lambda cs=cs, ng=ng: ofp_d[l, :, cs:cs + ng * 128]))] +
                                   [(HF + 16 * jj, 16, (lambda jj=jj, cs=cs, ng=ng: ofs_d[l, :, jj, cs:cs + ng * 128])) for jj in range(2)])
            nn = NCOL if "s" in segs else NPC
            kb.A(accs[0][:, 0:nn], accs[0][:, 0:nn], AF.Silu)
            kb.TT(V(AA["full"][:, j, 0:nn], [AA["bp"][j], AA["bs"][j]]), accs[0][:, 0:nn], accs[1][:, 0:nn], ALU.mult)
            if q == 0 and l + 1 < DEPTH and j < 16:
                mod_block(l + 1, j)
                if j == 15:
                    mod_finish(l + 1)
        out_proj(l, 1, wdn_d[l], NPAIR, AA, FF, segs)

    UPSP = [kb.sbn(f"upsp{c}", 16 * HP) for c in range(2)]
    UPSP = [V(u.ap.rearrange("p (b r) -> p b r", r=HP), u.bufs) for u in UPSP]
    ZSP = [kb.sbn(f"zsp{c}", 16 * HS) for c in range(3)]
    ZSP = [V(u.ap.rearrange("p (b r) -> p b r", r=HS), u.bufs) for u in ZSP]
    pool_tails = []
    sconv_tails = []
    print("SBUF words used (final)", kb.top, "of", kb.nwords)

    try:
        chk('setup')
        compute_mod(0)
        chk('mod0')
        for q in range(NPIECE):
            last = (q == NPIECE - 1)
            segs = ["p", "s"] if last else ["p"]
            blocks = [(xp_d, q * NPC + tb * 128, 128, tb * 128) for tb in range(NPC // 128)]
            if last:
                blocks.append((xs_d, 0, NS, NPC))
            for bi, (src, r0, nr, c0) in enumerate(blocks):
                io = IOX[bi % 2]
                iov = V(io.ap[0:nr, :], io.bufs)
                kb.dma(sp, iov.ap, src[r0:r0 + nr, :], writes=[iov])
                sl = rslot() if NT == 2 else None
                if NT == 2:
                    kb.MM(lambda iov=iov, sl=sl, nr=nr: [nc.tensor.transpose(out=sl.ap[:, k * 128:k * 128 + nr], in_=iov.ap[:, k * 128:(k + 1) * 128],
                                                                             identity=ident_f.ap[0:nr, 0:nr]) for k in range(KC)][-1],
                          [iov, ident_f], [sl])
                    dst = V(X["full"][:, :, c0:c0 + nr], X["bp"] + X["bs"])
                    kb.A(dst, sl[:, 0:KC * 128].re("p (k n) -> p k n", k=KC)[:, :, 0:nr], AF.Copy)
                else:
                    for hk in range(2):
                        sl = rslot()
                        kb.MM(lambda iov=iov, sl=sl, nr=nr, hk=hk: [nc.tensor.transpose(out=sl.ap[:, k * 128:k * 128 + nr],
                                                                                        in_=iov.ap[:, (hk * 4 + k) * 128:(hk * 4 + k + 1) * 128],
                                                                                        identity=ident_f.ap[0:nr, 0:nr]) for k in range(4)][-1],
                              [iov, ident_f], [sl])
                        dst = V(X["full"][:, hk * 4:hk * 4 + 4, c0:c0 + nr], X["bp"] + X["bs"])
                        kb.A(dst, sl[:, 0:512].re("p (k n) -> p k n", k=4)[:, :, 0:nr], AF.Copy)
            chk(f'xload{q}')
            for l in range(DEPTH):
                mix(l, q, segs)
                chk(f'mix{q}_{l}')
                ffn(l, q, segs)
                chk(f'ffn{q}_{l}')
            for bi, (src, r0, nr, c0) in enumerate(blocks):
                io = IOX[bi % 2]
                iov = V(io.ap[0:nr, :], io.bufs)
                for hk in range(2):
                    sl = rslot()
                    xr = [X["p"][hk * 4 + k] if c0 < NPC else X["s"][hk * 4 + k] for k in range(4)]
                    kb.MM(lambda sl=sl, nr=nr, c0=c0, hk=hk: [nc.tensor.transpose(out=sl.ap[0:nr, k * 128:(k + 1) * 128],
                                                                                  in_=X["full"][:, hk * 4 + k, c0:c0 + nr],
                                                                                  identity=ident_f.ap) for k in range(4)][-1],
                          xr + [ident_f], [sl])
                    kb.A(iov[:, hk * 512:(hk + 1) * 512], sl[0:nr, 0:512], AF.Copy)
                dstd = (yp_d if src is xp_d else ys_d)[r0:r0 + nr, :]
                kb.dma(sp, dstd, iov.ap, reads=[iov], is_out=True)
    except _Stop:
        pass

    best = {}
    for (s, v) in kb.out_toks:
        if best.get(s.key, (None, 0))[1] < v:
            best[s.key] = (s, v)
    for k, (s, v) in best.items():
        nc.sync.wait_ge(s.h, v)
    build.marks = kb.marks
    return nc


_NC = None


def kernel(x_prompt, x_sample, c_prompt, c_sample, state_pool, state_sconv, state_cconv, state_ffn,
           w_ada, b_ada, g_mix_pre, g_mix_post, g_ffn_pre, g_ffn_post,
           w_in, pool_w, pool_scale, sconv_w, cconv_w, cconv_b, cln_g, cln_b,
           w_out, w_up, ffn_conv_w, w_down):
    global _NC
    f = lambda a: np.ascontiguousarray(np.asarray(a, dtype=np.float32))
    x_prompt, x_sample, c_prompt, c_sample = map(f, (x_prompt, x_sample, c_prompt, c_sample))
    state_pool, state_sconv, state_cconv, state_ffn = map(f, (state_pool, state_sconv, state_cconv, state_ffn))
    rows = []
    for l in range(DEPTH):
        parts = [f(b_ada)[l].reshape(48, 128), f(g_mix_pre)[l].reshape(8, 128), f(g_mix_post)[l].reshape(8, 128),
                 f(g_ffn_pre)[l].reshape(8, 128), f(g_ffn_post)[l].reshape(8, 128), f(pool_scale)[l].reshape(2, 128),
                 f(sconv_w)[l].reshape(9, 128), f(cconv_w)[l].reshape(93, 128), f(cconv_b)[l].reshape(3, 128),
                 f(cln_g)[l].reshape(3, 128), f(cln_b)[l].reshape(3, 128), f(ffn_conv_w)[l].reshape(132, 128)]
        rows.append(np.concatenate(parts, axis=0))
    ptab = np.ascontiguousarray(np.stack(rows, 0))
    shared = {"w_ada": f(w_ada), "ptab": ptab, "w_in": f(w_in), "pool_w": f(pool_w), "w_out": f(w_out),
              "w_up": f(w_up), "w_down": f(w_down)}
    in_maps = []
    for i in range(NCORES):
        sl = slice(16 * i, 16 * i + 16)
        m = dict(shared)
        m["xp"] = x_prompt[i]
        m["xs"] = np.ascontiguousarray(x_sample[sl].reshape(NS, D))
        m["cc"] = np.ascontiguousarray(np.concatenate([c_prompt[i:i + 1], c_sample[sl]], axis=0))
        m["st_pool"] = np.ascontiguousarray(state_pool[:, sl].reshape(DEPTH, 16 * HP, DP))
        m["st_sconv"] = np.ascontiguousarray(state_sconv[:, sl].reshape(DEPTH, 16 * HS, DS))
        m["st_cconv"] = np.ascontiguousarray(state_cconv[:, sl].reshape(DEPTH, 16 * HC, DC))
        m["st_ffn"] = np.ascontiguousarray(state_ffn[:, sl].reshape(DEPTH, 16 * HF, 2 * DFF))
        in_maps.append(m)
    if _NC is None:
        _NC = build()
    res = run_bass_kernel_spmd(_NC, in_maps, core_ids=list(range(NCORES)))
    R = res.results
    cat = lambda k, ax: np.concatenate([np.asarray(R[i][k]) for i in range(NCORES)], axis=ax)
    yp = np.stack([np.asarray(R[i]["yp"]) for i in range(NCORES)], 0)
    ys = cat("ys", 0).reshape(128, 4, D)
    outs = [yp, ys]
    for k in ("o_pool_p", "o_sconv_p", "o_cconv_p", "o_ffn_p"):
        outs.append(np.stack([np.asarray(R[i][k]) for i in range(NCORES)], 1))
    for k in ("o_pool_s", "o_sconv_s", "o_cconv_s", "o_ffn_s"):
        outs.append(cat(k, 1))
    return tuple(np.ascontiguousarray(o.astype(np.float32)) for o in outs)
```

```python
import numpy as np
import concourse.bass as bass
import concourse.mybir as mybir
from concourse.bass_utils import run_bass_kernel_spmd

F32 = mybir.dt.float32
BF16 = mybir.dt.bfloat16
AF = mybir.ActivationFunctionType
ALU = mybir.AluOpType

NCORES = 8
D = 1024
KC = 8
DEPTH = 4
SEQ = 2048
NSEQ_S = 16
NS = 64
DP, DS, DC, DFF = 256, 384, 384, 2816
DIN = 2176
NFC = 44
NPAIR = 22
EPS = 1e-6
NPC = 512
NPIECE = SEQ // NPC
NT = NPC // 512
NCOL = NPC + NS
HP, HS, HC, HF = 15, 2, 30, 2

PO = {}
_o = 0
for _n, _r in (("b_ada", 48), ("g_mix_pre", 8), ("g_mix_post", 8), ("g_ffn_pre", 8), ("g_ffn_post", 8),
               ("pool_scale", 2), ("sconv_w", 9), ("cconv_w", 93), ("cconv_b", 3), ("cln_g", 3),
               ("cln_b", 3), ("ffn_conv_w", 132)):
    PO[_n] = _o
    _o += _r
NPT = _o
DBG_SKIP = False


class _Stop(Exception):
    pass


class Buf:
    __slots__ = ("name", "space", "lo", "hi", "w", "r", "ov")

    def __init__(self, name, space, lo, hi):
        self.name, self.space, self.lo, self.hi = name, space, lo, hi
        self.w = None
        self.r = {}
        self.ov = None


class V:
    def __init__(self, ap, bufs):
        self.ap = ap
        self.bufs = list(bufs)

    def __getitem__(self, key):
        return V(self.ap[key], self.bufs)

    def re(self, s, **kw):
        return V(self.ap.rearrange(s, **kw), self.bufs)

    def bc(self, shape):
        return V(self.ap.to_broadcast(shape), self.bufs)

    def us(self, axis):
        return V(self.ap.unsqueeze(axis), self.bufs)


class Sem:
    def __init__(self, h, key):
        self.h, self.key, self.total = h, key, 0


class Eng:
    def __init__(self, nc, name, h, is_pe=False):
        self.name, self.h, self.is_pe = name, h, is_pe
        self.sem = Sem(nc.alloc_semaphore("s_" + name), "s_" + name)
        self.count = 0
        self.seen = {}
        self.dsems = []
        self.di = 0


class KB:
    def __init__(self):
        nc = bass.Bass("TRN2", target_bir_lowering=False)
        self.nc = nc
        self.pe = Eng(nc, "pe", nc.tensor, True)
        self.act = Eng(nc, "act", nc.scalar)
        self.dve = Eng(nc, "dve", nc.vector)
        self.pool = Eng(nc, "pool", nc.gpsimd)
        self.sp = Eng(nc, "sp", nc.sync)
        for e, n in ((self.sp, 14), (self.pool, 12)):
            for i in range(n):
                e.dsems.append(Sem(nc.alloc_semaphore(f"d_{e.name}{i}"), f"d_{e.name}{i}"))
        self.bufs = {"sb": [], "ps": []}
        self.nwords = 53100
        self.arena = nc.alloc_sbuf_tensor("arena", [128, self.nwords], F32)
        self.psum = nc.alloc_psum_tensor("psum", [128, 4096], F32)
        self.top = 0
        self.out_toks = []
        self.opn = 0
        self.stop_n = 0
        self.marks = {}

    def mkbuf(self, name, space, lo, hi):
        b = Buf(name, space, lo, hi)
        lst = self.bufs[space]
        b.ov = [b]
        for o in lst:
            if o.lo < hi and lo < o.hi:
                o.ov.append(b)
                b.ov.append(o)
        lst.append(b)
        return b

    def alloc(self, words):
        lo = self.top
        self.top += words
        assert self.top <= self.nwords, f"SBUF arena overflow {self.top}"
        return lo

    def sb(self, name, lo, words, dt=F32, shape=None):
        b = self.mkbuf(name, "sb", lo, lo + words)
        ap = self.arena[:, lo:lo + words]
        if dt != F32:
            ap = ap.bitcast(dt)
        return V(ap, [b])

    def sbn(self, name, words, dt=F32):
        return self.sb(name, self.alloc(words), words, dt)

    def ps(self, name, lo, n):
        b = self.mkbuf(name, "ps", lo, lo + n)
        return V(self.psum[:, lo:lo + n], [b])

    def _deps(self, eng, reads, writes):
        toks = {}
        for b in reads:
            for ob in b.ov:
                if ob.w is not None:
                    s, v = ob.w
                    if toks.get(s.key, (None, 0))[1] < v:
                        toks[s.key] = (s, v)
        for b in writes:
            for ob in b.ov:
                if ob.w is not None:
                    s, v = ob.w
                    if toks.get(s.key, (None, 0))[1] < v:
                        toks[s.key] = (s, v)
                for k, (s, v) in ob.r.items():
                    if toks.get(k, (None, 0))[1] < v:
                        toks[k] = (s, v)
        for k, (s, v) in toks.items():
            if eng.is_pe and s is eng.sem:
                continue
            if eng.seen.get(k, 0) >= v:
                continue
            eng.h.wait_ge(s.h, v)
            eng.seen[k] = v

    def _commit(self, tok, reads, writes):
        s, v = tok
        ws = set(id(b) for b in writes)
        for b in writes:
            b.w = tok
            b.r = {}
        for b in reads:
            if id(b) in ws:
                continue
            if b.r.get(s.key, (None, 0))[1] < v:
                b.r[s.key] = (s, v)

    def emit(self, eng, fn, reads, writes):
        self.opn += 1
        if self.stop_n and self.opn == self.stop_n:
            raise _Stop()
        rb = [b for v in reads for b in v.bufs]
        wb = [b for v in writes for b in v.bufs]
        self._deps(eng, rb, wb)
        ins = fn()
        eng.count += 1
        ins.then_inc(eng.sem.h, 1)
        tok = (eng.sem, eng.count)
        self._commit(tok, rb, wb)
        return tok

    def dma(self, eng, out, in_, reads=(), writes=(), is_out=False):
        self.opn += 1
        if self.stop_n and self.opn == self.stop_n:
            raise _Stop()
        rb = [b for v in reads for b in v.bufs]
        wb = [b for v in writes for b in v.bufs]
        s = eng.dsems[eng.di % len(eng.dsems)]
        eng.di += 1
        if s.total > 0 and eng.seen.get(s.key, 0) < s.total:
            eng.h.wait_ge(s.h, s.total)
            eng.seen[s.key] = s.total
        self._deps(eng, rb, wb)
        ins = eng.h.dma_start(out=out, in_=in_)
        ins.then_inc(s.h, 16)
        s.total += 16
        tok = (s, s.total)
        self._commit(tok, rb, wb)
        if is_out:
            self.out_toks.append(tok)
        return tok

    def A(self, out, in_, func, bias=None, scale=None, eng=None):
        rd = [in_] + [x for x in (bias, scale) if isinstance(x, V)]
        kw = {}
        if bias is not None:
            kw["bias"] = bias.ap if isinstance(bias, V) else bias
        if scale is not None:
            kw["scale"] = scale.ap if isinstance(scale, V) else scale
        return self.emit(self.act, lambda: self.nc.scalar.activation(out=out.ap, in_=in_.ap, func=func, **kw),
                         rd, [out])

    def TS(self, out, in0, s1, s2, op0, op1=None, eng=None):
        eng = eng or self.dve
        rd = [in0] + [x for x in (s1, s2) if isinstance(x, V)]
        a1 = s1.ap if isinstance(s1, V) else s1
        a2 = s2.ap if isinstance(s2, V) else s2
        kw = {}
        if op1 is not None:
            kw["op1"] = op1
        return self.emit(eng, lambda: eng.h.tensor_scalar(out=out.ap, in0=in0.ap, scalar1=a1, scalar2=a2,
                                                          op0=op0, **kw), rd, [out])

    def TT(self, out, in0, in1, op, eng=None):
        eng = eng or self.dve
        return self.emit(eng, lambda: eng.h.tensor_tensor(out=out.ap, in0=in0.ap, in1=in1.ap, op=op),
                         [in0, in1], [out])

    def STT(self, out, in0, sc, in1, op0, op1, eng=None):
        eng = eng or self.dve
        rd = [in0, in1] + ([sc] if isinstance(sc, V) else [])
        a = sc.ap if isinstance(sc, V) else sc
        return self.emit(eng, lambda: eng.h.scalar_tensor_tensor(out=out.ap, in0=in0.ap, scalar=a, in1=in1.ap,
                                                                 op0=op0, op1=op1), rd, [out])

    def RC(self, out, in_):
        return self.emit(self.dve, lambda: self.nc.vector.reciprocal(out=out.ap, in_=in_.ap), [in_], [out])

    def CP(self, out, in_, eng=None):
        eng = eng or self.dve
        return self.emit(eng, lambda: eng.h.tensor_copy(out=out.ap, in_=in_.ap), [in_], [out])

    def MS(self, out, val, eng=None):
        eng = eng or self.dve
        return self.emit(eng, lambda: eng.h.memset(out.ap, val), [], [out])

    def MM(self, fn, reads, writes):
        return self.emit(self.pe, fn, reads, writes)


def build(stop=None):
    kb = KB()

    def chk(tag):
        kb.marks[tag] = kb.opn
        if stop is not None and tag == stop:
            raise _Stop()
    nc = kb.nc
    pe, act, dve, pool, sp = kb.pe, kb.act, kb.dve, kb.pool, kb.sp

    def din(name, shape):
        return nc.dram_tensor(name, list(shape), F32, kind="ExternalInput").ap()

    def dout(name, shape):
        return nc.dram_tensor(name, list(shape), F32, kind="ExternalOutput").ap()

    xp_d = din("xp", [SEQ, D]); xs_d = din("xs", [NS, D]); cc_d = din("cc", [17, D])
    stp_d = din("st_pool", [DEPTH, 16 * HP, DP]); sts_d = din("st_sconv", [DEPTH, 16 * HS, DS])
    stc_d = din("st_cconv", [DEPTH, 16 * HC, DC]); stf_d = din("st_ffn", [DEPTH, 16 * HF, 2 * DFF])
    wada_d = din("w_ada", [DEPTH, D, 6 * D]); pt_d = din("ptab", [DEPTH, NPT, 128])
    win_d = din("w_in", [DEPTH, D, DIN]); pw_d = din("pool_w", [DEPTH, 4, 64, 64])
    wout_d = din("w_out", [DEPTH, D, D]); wup_d = din("w_up", [DEPTH, D, 2 * DFF])
    wdn_d = din("w_down", [DEPTH, DFF, D])
    yp_d = dout("yp", [SEQ, D]); ys_d = dout("ys", [NS, D])
    opp_d = dout("o_pool_p", [DEPTH, HP, DP]); osp_d = dout("o_sconv_p", [DEPTH, HS, DS])
    ocp_d = dout("o_cconv_p", [DEPTH, HC, DC]); ofp_d = dout("o_ffn_p", [DEPTH, HF, 2 * DFF])
    ops_d = dout("o_pool_s", [DEPTH, 16, HP, DP]); oss_d = dout("o_sconv_s", [DEPTH, 16, HS, DS])
    ocs_d = dout("o_cconv_s", [DEPTH, 16, HC, DC]); ofs_d = dout("o_ffn_s", [DEPTH, 16, HF, 2 * DFF])

    wada_d = [wada_d[l] for l in range(DEPTH)]; win_d = [win_d[l] for l in range(DEPTH)]
    wout_d = [wout_d[l] for l in range(DEPTH)]; wup_d = [wup_d[l] for l in range(DEPTH)]; wdn_d = [wdn_d[l] for l in range(DEPTH)]

    W_X = KC * NCOL
    x_lo = kb.alloc(W_X)
    hc_lo = kb.alloc(W_X)
    a_lo = kb.alloc(NPAIR * NCOL // 2)
    A_WORDS = NPAIR * NCOL // 2

    def chunked(name, lo, nch, dt):
        wpc = NCOL if dt == F32 else NCOL // 2
        full = kb.arena[:, lo:lo + nch * wpc]
        if dt != F32:
            full = full.bitcast(dt)
        full = full.rearrange("p (c n) -> p c n", c=nch)
        res = {"full": full, "p": [], "s": [], "bp": [], "bs": []}
        pw = NPC if dt == F32 else NPC // 2
        for c in range(nch):
            bp = kb.mkbuf(f"{name}{c}p", "sb", lo + c * wpc, lo + c * wpc + pw)
            bs = kb.mkbuf(f"{name}{c}s", "sb", lo + c * wpc + pw, lo + (c + 1) * wpc)
            res["p"].append(V(full[:, c, 0:NPC], [bp]))
            res["s"].append(V(full[:, c, NPC:NCOL], [bs]))
            res["bp"].append(bp); res["bs"].append(bs)
        res["sall"] = V(full[:, :, NPC:NCOL], res["bs"])
        return res

    X = chunked("x", x_lo, KC, F32)
    H = chunked("h", hc_lo, KC, BF16)
    CAT = chunked("cat", hc_lo + W_X // 2, KC, BF16)
    FF = chunked("ff", hc_lo, KC, F32)
    MX = chunked("mx", a_lo, KC, F32)
    AA = chunked("a", a_lo, NPAIR, BF16)

    WSLOT = 1536
    NWS = 5
    wring = [kb.sbn(f"wr{i}", WSLOT, BF16) for i in range(NWS)]
    wr_i = [0]
    whalf = [[kb.sb(f"wrh{i}_{g}", w_.bufs[0].lo + g * 512, 512, BF16) for g in range(2)] for i, w_ in enumerate(wring)]

    PT = kb.sbn("pt", DEPTH * NPT)
    PTv = V(PT.ap.rearrange("p (l n) -> p l n", l=DEPTH), PT.bufs)
    DVA = [kb.sbn(f"dv{l}", 6 * KC * 17) for l in range(DEPTH)]
    ident_f = kb.sbn("ident_f", 128); ident_b = kb.sbn("ident_b", 64, BF16)
    ones_b = kb.sbn("ones_b", 64, BF16); ones_f = kb.sbn("ones_f", 128)
    invcnt = kb.sbn("invcnt", 32)
    PW = kb.sbn("pw", DEPTH * 2 * 64, BF16)
    PWv = V(PW.ap.rearrange("p (l c n) -> p l c n", l=DEPTH, c=2), PW.bufs)
    CTt = kb.sbn("ct", 80, BF16)
    CTv = V(CTt.ap[:, 0:KC * 17].rearrange("p (k s) -> p k s", k=KC), CTt.bufs)
    RSp = kb.sbn("rsp", NPC); RSs = kb.sbn("rss", NS)
    SQ = [kb.sbn(f"sq{i}", NCOL // 2, BF16) for i in range(3)]
    TMP = [kb.sbn(f"tmp{i}", NCOL) for i in range(4)]
    sq_i = [0]; tmp_i = [0]

    def nsq():
        sq_i[0] += 1
        return SQ[sq_i[0] % len(SQ)]

    def ntmp():
        tmp_i[0] += 1
        return TMP[tmp_i[0] % len(TMP)]

    HALO_P = [kb.sbn(f"hp{l}", 2 * HP) for l in range(DEPTH)]
    HALO_S = [kb.sbn(f"hs{l}", 3 * HS) for l in range(DEPTH)]
    HALO_C = [kb.sbn(f"hcv{l}", 3 * HC // 2, BF16) for l in range(DEPTH)]
    HALO_F = [kb.sbn(f"hf{l}", NFC * HF // 2, BF16) for l in range(DEPTH)]

    def padded(name, lo, nch, Hh, dt):
        cols = Hh + NPC + 16 * (Hh + 4)
        colsw = cols if dt == F32 else (cols + 1) // 2
        res = {"cols": cols, "H": Hh, "pf": [], "sf": [], "words": nch * colsw}
        for c in range(nch):
            l0 = lo + c * colsw
            ap = kb.arena[:, l0:l0 + colsw]
            if dt != F32:
                ap = ap.bitcast(dt)
            pcw = (Hh + NPC) if dt == F32 else (Hh + NPC) // 2
            bp = kb.mkbuf(f"{name}{c}p", "sb", l0, l0 + pcw)
            bs = kb.mkbuf(f"{name}{c}s", "sb", l0 + pcw, l0 + colsw)
            res["pf"].append(V(ap[:, 0:Hh + NPC], [bp]))
            res["sf"].append(V(ap[:, Hh + NPC:Hh + NPC + 16 * (Hh + 4)].rearrange("p (b j) -> p b j", j=Hh + 4), [bs]))
        return res

    MODT = kb.sbn("modt", 48 * 17 + 16)
    STG = kb.sbn("stg", 1536)
    OST = kb.sbn("ost", 1536)
    IOX = [kb.sb(f"iox{i}", a_lo + i * 1024, 1024) for i in range(2)]
    m_lo = kb.top
    mlo = m_lo
    UP = padded("up", mlo, 1, HP, F32); mlo += UP["words"]
    W1 = padded("w1", mlo, 1, HP, F32); mlo += W1["words"]
    W2 = padded("w2", mlo, 1, HP, F32); mlo += W2["words"]
    PLS = []
    for i in range(2):
        PLS.append(kb.sb(f"pl{i}", mlo, NCOL // 2, BF16)); mlo += NCOL // 2
    ZB = padded("zb", mlo, 2, HS, F32); mlo += ZB["words"]
    GL = padded("gl", mlo, 3, HC, BF16); mlo += GL["words"]
    VB = kb.sb("vb", mlo, 3 * NCOL); mlo += 3 * NCOL
    VQ = kb.sb("vq", mlo, 3 * NCOL); mlo += 3 * NCOL
    LNM = kb.sb("lnm", mlo, 512); mlo += 512
    LNR = kb.sb("lnr", mlo, 512); mlo += 512
    GT = kb.sb("gt", mlo, 3 * 96); mlo += 3 * 96
    TAILS = kb.sb("tails", mlo, 3 * 96); mlo += 3 * 96
    DGS = []
    for i in range(2):
        DGS.append(kb.sb(f"dg{i}", mlo, 31 * 64, BF16)); mlo += 31 * 64
    m_hi = mlo
    flo = m_lo
    UPBw = (HF + NPC + 16 * (HF + 4)) // 2
    UPB = []
    NUPB, NACC = 6, 8
    for i in range(NUPB):
        UPB.append(padded(f"upb{i}", flo, 1, HF, BF16)); flo += UPBw
    ACC = []
    for i in range(NACC):
        ACC.append(kb.sb(f"acc{i}", flo, NCOL)); flo += NCOL
    PTMP = []
    for i in range(2):
        PTMP.append(kb.sb(f"ptmp{i}", flo, NCOL)); flo += NCOL
    FST = []
    for i in range(2):
        FST.append(kb.sb(f"fst{i}", flo, 11 * 34)); flo += 11 * 34
    UPS = kb.sb("ups", flo, NFC * 16 * HF // 2, BF16); flo += NFC * 16 * HF // 2
    kb.top = max(m_hi, flo)
    assert kb.top <= kb.nwords, f"SBUF overflow {kb.top}"
    print("SBUF words used", kb.top, "of", kb.nwords)

    RB = 8 - NT - 2
    nslots = RB // NT
    ring = [kb.ps(f"ring{i}", i * NT * 512, NT * 512) for i in range(nslots)]
    if NT == 1:
        ring.append(kb.ps("ring_b6", 6 * 512, 512))
        nslots += 1
    ring_i = [0]
    STATP = kb.ps("statp", RB * 512, NT * 512)
    STATS = kb.ps("stats", 7 * 512, 64)

    def rslot():
        ring_i[0] += 1
        return ring[ring_i[0] % nslots]

    def sslot():
        return rslot()

    def wslot():
        wr_i[0] += 1
        return wring[wr_i[0] % NWS]

    wplan = []

    def P_mod(l):
        for blk in range(16):
            wplan.append((wada_d[l], D, [(blk * 384, 384, 0)], 384))

    def P_mix(l):
        for (c0, n) in ((256, 384), (256 + 768, 384), (256 + 384, 384), (1408, 384), (1792, 384), (0, 256)):
            wplan.append((win_d[l], D, [(c0, n, 0)], n))
        for mo in range(KC):
            wplan.append((wout_d[l], D, [(mo * 128, 128, 0)], 128))

    def P_ffn(l, modl=None):
        for j in range(NPAIR):
            wplan.append((wup_d[l], D, [(j * 128, 128, "pair")], 256))
            if modl is not None and j < 16:
                wplan.append((wada_d[modl], D, [(j * 384, 384, 0)], 384))
        for mo in range(KC):
            wplan.append((wdn_d[l], DFF, [(mo * 128, 128, 0)], 128))

    P_mod(0)
    for q_ in range(NPIECE):
        for l_ in range(DEPTH):
            P_mix(l_)
            P_ffn(l_, (l_ + 1) if (q_ == 0 and l_ + 1 < DEPTH) else None)
    wst = {"issued": 0, "taken": 0, "views": {}}
    WLIVE = 3

    def w_issue(j):
        dram2d, nrows, parts, tcols = wplan[j]
        nkc = nrows // 128
        slot = wring[j % NWS]
        view = V(slot.ap[:, 0:nkc * tcols].rearrange("p (k c) -> p k c", k=nkc), slot.bufs)
        for (c0, ncols, coff) in parts:
            if coff == "pair":
                hv = []
                for g in range(2):
                    hb_ = whalf[j % NWS][g]
                    hview = V(hb_.ap.rearrange("p (k c) -> p k c", k=nkc), hb_.bufs)
                    src = dram2d[0:nrows, g * DFF + c0:g * DFF + c0 + ncols].rearrange("(k p) c -> p k c", p=128)
                    kb.dma(pool, hview.ap, src, writes=[hview])
                    hv.append(hview)
                view = hv
                continue
            src = dram2d[0:nrows, c0:c0 + ncols].rearrange("(k p) c -> p k c", p=128)
            kb.dma(pool, view.ap[:, :, coff:coff + ncols], src, writes=[view])
        wst["views"][j] = view

    def take_w(dram2d, c0, wl=WLIVE):
        i = wst["taken"]
        assert wplan[i][0] is dram2d and wplan[i][2][0][0] == c0, (i, c0, wplan[i][2])
        lim = min(len(wplan) - 1, i + NWS - wl)
        while wst["issued"] <= lim:
            w_issue(wst["issued"])
            wst["issued"] += 1
        wst["taken"] += 1
        return wst["views"].pop(i)

    iot = V(STG.ap[:, 0:128].bitcast(mybir.dt.int32), STG.bufs)
    kb.emit(pool, lambda: nc.gpsimd.iota(iot.ap, pattern=[[1, 128]], base=0, channel_multiplier=-1), [], [iot])
    kb.CP(ident_f, iot)
    kb.TS(ident_f, ident_f, 0.0, None, ALU.is_equal)
    kb.CP(ident_b, ident_f)
    kb.MS(ones_b, 1.0)
    kb.MS(ones_f, 1.0)
    icv = V(invcnt.ap[:, 0:30].rearrange("p (c t) -> p c t", c=2), invcnt.bufs)
    for ch in range(2):
        for half in range(2):
            w = (2, 4, 8, 16)[2 * ch + half]
            ps_ = slice(64 * half, 64 * half + 64)
            kb.MS(icv[ps_, ch, :], 1.0 / w)
            for t in range(w - 1):
                kb.MS(icv[ps_, ch, t:t + 1], 1.0 / (t + 1))
    for l in range(DEPTH):
        r = 0
        while r < NPT:
            n = min(128, NPT - r)
            st = V(STG.ap[0:n, 0:128], STG.bufs)
            kb.dma(sp, st.ap, pt_d[l, r:r + n, :], writes=[st])
            sl = rslot()
            kb.MM(lambda st=st, sl=sl, n=n: nc.tensor.transpose(out=sl.ap[:, 0:n], in_=st.ap, identity=ident_f.ap[0:n, 0:n]),
                  [st, ident_f], [sl])
            kb.A(PTv[:, l, r:r + n], sl[:, 0:n], AF.Copy)
            r += n
    kb.MS(PW, 0.0)
    for l in range(DEPTH):
        for g in range(4):
            ch, half = g // 2, g % 2
            dst = PWv[64 * half:64 * half + 64, l, ch, 64 * half:64 * half + 64]
            kb.dma(pool, dst.ap, pw_d[l, g, :, :], writes=[PWv])
    cst = V(STG.ap[0:17, 0:1024], STG.bufs)
    kb.dma(sp, cst.ap, cc_d[:, :], writes=[cst])
    kb.A(cst, cst, AF.Silu)
    sl = rslot()
    kb.MM(lambda: [nc.tensor.transpose(out=sl.ap[:, k * 17:(k + 1) * 17], in_=cst.ap[:, k * 128:(k + 1) * 128],
                                       identity=ident_f.ap[0:17, 0:17]) for k in range(KC)][-1],
          [cst, ident_f], [sl])
    kb.A(CTv, sl[:, 0:KC * 17].re("p (k s) -> p k s", k=KC), AF.Copy)

    def pcol(l, name, idx):
        o = PO[name] + idx
        return PTv[:, l, o:o + 1]

    def mod_block(l, blk):
        modv = V(MODT.ap[:, 0:48 * 17].rearrange("p (c s) -> p c s", c=48), MODT.bufs)
        wv = take_w(wada_d[l], blk * 384, wl=1)
        sl = rslot()

        def fn(wv=wv, sl=sl):
            last = None
            for j in range(3):
                for k in range(KC):
                    last = nc.tensor.matmul(sl.ap[:, j * 17:(j + 1) * 17], lhsT=wv.ap[:, k, j * 128:(j + 1) * 128],
                                            rhs=CTv.ap[:, k, :], start=(k == 0), stop=(k == KC - 1))
            return last
        kb.MM(fn, [wv, CTv], [sl])
        o = PO["b_ada"] + blk * 3
        kb.TT(modv[:, blk * 3:blk * 3 + 3, :], sl[:, 0:51].re("p (c s) -> p c s", c=3),
              PTv[:, l, o:o + 3].us(2).bc([128, 3, 17]), ALU.add)

    def mod_finish(l):
        modv = V(MODT.ap[:, 0:48 * 17].rearrange("p (c s) -> p c s", c=48), MODT.bufs)
        dv = V(DVA[l].ap.rearrange("p (w c s) -> p w c s", w=6, c=KC), DVA[l].bufs)
        for sub, (gpre, gpost) in enumerate((("g_mix_pre", "g_mix_post"), ("g_ffn_pre", "g_ffn_post"))):
            m0 = sub * 24
            gp = PTv[:, l, PO[gpre]:PO[gpre] + 8].us(2).bc([128, 8, 17])
            gq = PTv[:, l, PO[gpost]:PO[gpost] + 8].us(2).bc([128, 8, 17])
            kb.TS(dv[:, 3 * sub + 0], modv[:, m0 + 8:m0 + 16, :], 1.0, 32.0, ALU.add, ALU.mult)
            kb.TT(dv[:, 3 * sub + 0], dv[:, 3 * sub + 0], gp, ALU.mult)
            kb.CP(dv[:, 3 * sub + 1], modv[:, m0:m0 + 8, :])
            kb.TS(dv[:, 3 * sub + 2], modv[:, m0 + 16:m0 + 24, :], 32.0, None, ALU.mult)
            kb.TT(dv[:, 3 * sub + 2], dv[:, 3 * sub + 2], gq, ALU.mult)

    def compute_mod(l):
        for blk in range(16):
            mod_block(l, blk)
        mod_finish(l)

    def dvv(l):
        return V(DVA[l].ap.rearrange("p (w c s) -> p w c s", w=6, c=KC), DVA[l].bufs)

    def mm_chunk(lhs_fn, nk, rhs_p, rhs_s, segs, reads):
        outs = {}
        slp = rslot()
        outs["p"] = slp
        wr = [slp]
        if "s" in segs:
            sls = sslot()
            outs["s"] = sls
            wr.append(sls)

        def fn():
            last = None
            for k in range(nk):
                lh = lhs_fn(k)
                for t in range(NT):
                    last = nc.tensor.matmul(slp.ap[:, t * 512:(t + 1) * 512], lhsT=lh, rhs=rhs_p(k, t),
                                            start=(k == 0), stop=(k == nk - 1))
            if "s" in segs and not DBG_SKIP:
                for k in range(nk):
                    last = nc.tensor.matmul(sls.ap[:, 0:NS], lhsT=lhs_fn(k), rhs=rhs_s(k),
                                            start=(k == 0), stop=(k == nk - 1))
            return last
        kb.MM(fn, reads, wr)
        return outs

    def rms_stats(src, segs, sq_from_psum=None):
        for seg in segs:
            for k in range(KC):
                q = nsq()
                n = NPC if seg == "p" else NS
                qv = q[:, 0:n]
                kb.A(qv, src[seg][k], AF.Square)
                st = STATP if seg == "p" else STATS

                def fn(qv=qv, st=st, n=n, k=k):
                    last = None
                    for t in range(max(1, n // 512)):
                        w = min(512, n)
                        last = nc.tensor.matmul(st.ap[:, t * 512:t * 512 + w], lhsT=ones_b.ap, rhs=qv.ap[:, t * 512:t * 512 + w],
                                                start=(k == 0), stop=(k == KC - 1))
                    return last
                kb.MM(fn, [qv, ones_b], [st])

    def rstd_from_stats(segs):
        for seg in segs:
            st, rs = (STATP, RSp) if seg == "p" else (STATS, RSs)
            kb.A(rs, st, AF.Ln, bias=float(D * EPS))
            kb.A(rs, rs, AF.Exp, scale=-0.5)

    def prenorm(l, sub, segs):
        dv = dvv(l)
        rms_stats(X, segs)
        rstd_from_stats(segs)
        for k in range(KC):
            t = ntmp()
            kb.TT(t[:, 0:NPC], X["p"][k], RSp, ALU.mult)
            kb.A(H["p"][k], t[:, 0:NPC], AF.Identity, bias=dv[:, 3 * sub + 1, k, 0:1], scale=dv[:, 3 * sub + 0, k, 0:1])
        if "s" in segs:
            t = ntmp()
            tv = t[:, 0:KC * NS].re("p (k n) -> p k n", k=KC) if KC * NS <= NCOL else None
            kb.TT(tv, X["sall"], RSs.us(1).bc([128, KC, NS]), ALU.mult)
            t4 = tv.re("p k (b j) -> p k b j", j=4)
            kb.TT(t4, t4, dv[:, 3 * sub + 0, :, 1:17].us(3).bc([128, KC, 16, 4]), ALU.mult)
            kb.TT(H["sall"].re("p k (b j) -> p k b j", j=4), t4,
                  dv[:, 3 * sub + 1, :, 1:17].us(3).bc([128, KC, 16, 4]), ALU.add)

    def postnorm(l, sub, SRC, segs):
        dv = dvv(l)
        rstd_from_stats(segs)
        for k in range(KC):
            t = ntmp()
            kb.TT(t[:, 0:NPC], SRC["p"][k], RSp, ALU.mult)
            kb.TT(X["p"][k], X["p"][k], t[:, 0:NPC], ALU.add)
        if "s" in segs:
            t = ntmp()
            tv = t[:, 0:KC * NS].re("p (k n) -> p k n", k=KC)
            kb.TT(tv, SRC["sall"], RSs.us(1).bc([128, KC, NS]), ALU.mult)
            t4 = tv.re("p k (b j) -> p k b j", j=4)
            kb.TT(t4, t4, dv[:, 3 * sub + 2, :, 1:17].us(3).bc([128, KC, 16, 4]), ALU.mult)
            kb.TT(X["sall"], X["sall"], tv, ALU.add)

    def out_proj(l, sub, wd, nk, rhs_src, DST, segs):
        pend = None
        for mo in range(KC):
            wv = take_w(wd, mo * 128, wl=1)
            outs = mm_chunk(lambda k, wv=wv: wv.ap[:, k, :], nk,
                            lambda k, t: rhs_src["p"][k].ap[:, t * 512:(t + 1) * 512],
                            lambda k: rhs_src["s"][k].ap, segs,
                            [wv] + [rhs_src[s][k] for s in segs for k in range(nk)])
            if pend is not None:
                pend()
            qs = {}
            for seg in segs:
                if seg == "p":
                    kb.A(DST[seg][mo], outs[seg], AF.Identity, scale=dvv(l)[:, 3 * sub + 2, mo, 0:1])
                else:
                    kb.A(DST[seg][mo], outs[seg][:, 0:NS], AF.Copy)
                q = nsq()
                n = NPC if seg == "p" else NS
                qs[seg] = q[:, 0:n]
                kb.A(qs[seg], outs[seg] if seg == "p" else outs[seg][:, 0:NS], AF.Square)

            def mk(qs=qs, mo=mo):
                for seg in segs:
                    st = STATP if seg == "p" else STATS
                    n = NPC if seg == "p" else NS
                    qv = qs[seg]

                    def fn(qv=qv, st=st, n=n):
                        last = None
                        for t in range(max(1, n // 512)):
                            w = min(512, n)
                            last = nc.tensor.matmul(st.ap[:, t * 512:t * 512 + w], lhsT=ones_b.ap,
                                                    rhs=qv.ap[:, t * 512:t * 512 + w], start=(mo == 0), stop=(mo == KC - 1))
                        return last
                    kb.MM(fn, [qv, ones_b], [st])
            pend = mk
        pend()
        postnorm(l, sub, DST, segs)

    def store_rows(srcs, ncol, dsts):
        n = len(srcs)
        sl = rslot()
        kb.MM(lambda: [nc.tensor.transpose(out=sl.ap[0:ncol, i * 128:(i + 1) * 128], in_=srcs[i].ap, identity=ident_f.ap)
                       for i in range(n)][-1], list(srcs) + [ident_f], [sl])
        ov = V(OST.ap[0:ncol, 0:n * 128], OST.bufs)
        kb.A(ov, sl[0:ncol, 0:n * 128], AF.Copy)
        for (r0, nr, dfn) in dsts:
            kb.dma(sp, dfn(), OST.ap[r0:r0 + nr, 0:n * 128], reads=[ov], is_out=True)

    def build_diag(l, c, bi):
        dgb = DGS[bi]
        dgv = V(dgb.ap.rearrange("p (k n) -> p k n", k=31), dgb.bufs)
        o = PO["cconv_w"] + c
        wv_ = PTv[:, l, o:o + 93:3].us(2).bc([128, 31, 128])
        kb.TT(dgv, ident_b.us(1).bc([128, 31, 128]), wv_, ALU.mult)
        return dgv

    def mix(l, q, segs):
        last = (q == NPIECE - 1)
        first = (q == 0)
        prenorm(l, 0, segs)
        dgs_built = [build_diag(l, 0, 0)]
        chk(f'mixP{q}_{l}')
        hreads = [H[s][k] for s in segs for k in range(KC)]

        def win_chunk(wv, c0):
            return mm_chunk(lambda k: wv.ap[:, k, c0:c0 + 128], KC,
                            lambda k, t: H["p"][k].ap[:, t * 512:(t + 1) * 512],
                            lambda k: H["s"][k].ap, segs, [wv] + hreads)

        if last:
            for tI in range(2):
                st = V(STG.ap[0:120, 0:256], STG.bufs)
                kb.dma(sp, st.ap, stp_d[l, tI * 120:(tI + 1) * 120, :], writes=[st])
                sl = rslot()
                kb.MM(lambda st=st, sl=sl: [nc.tensor.transpose(out=sl.ap[:, c * 120:(c + 1) * 120], in_=st.ap[:, c * 128:(c + 1) * 128],
                                                                 identity=ident_f.ap[0:120, 0:120]) for c in range(2)][-1],
                      [st, ident_f], [sl])
                for c in range(2):
                    dstv = UPSP[c][:, tI * 8:(tI + 1) * 8, :]
                    kb.A(dstv, sl[:, c * 120:(c + 1) * 120].re("p (b r) -> p b r", r=HP), AF.Copy)
            chk(f'mixS1{q}_{l}')
            st = V(STG.ap[0:32, 0:384], STG.bufs)
            kb.dma(sp, st.ap, sts_d[l, :, :], writes=[st])
            sl = rslot()
            kb.MM(lambda: [nc.tensor.transpose(out=sl.ap[:, c * 32:(c + 1) * 32], in_=st.ap[:, c * 128:(c + 1) * 128],
                                               identity=ident_f.ap[0:32, 0:32]) for c in range(3)][-1], [st, ident_f], [sl])
            for c in range(3):
                kb.A(ZSP[c], sl[:, c * 32:(c + 1) * 32].re("p (b r) -> p b r", r=HS), AF.Copy)
            chk(f'mixS2{q}_{l}')
            for c in range(3):
                sl = rslot()
                sts_ = []
                for tI in range(4):
                    st = V(STG.ap[0:120, tI * 384:(tI + 1) * 384], STG.bufs)
                    if c == 0:
                        kb.dma(sp, st.ap, stc_d[l, tI * 120:(tI + 1) * 120, :], writes=[st])
                    sts_.append(st)
                kb.MM(lambda sl=sl, c=c, sts_=sts_: [nc.tensor.transpose(out=sl.ap[:, tI * 120:(tI + 1) * 120],
                                                                          in_=sts_[tI].ap[:, c * 128:(c + 1) * 128],
                                                                          identity=ident_f.ap[0:120, 0:120]) for tI in range(4)][-1],
                      sts_ + [ident_f], [sl])
                chk(f'mixS3{q}_{l}_{c}')
                kb.CP(GL["sf"][c][:, :, 0:HC], sl[:, 0:480].re("p (b r) -> p b r", r=HC))
                chk(f'mixS4{q}_{l}_{c}')

        chk(f'mixA{q}_{l}')
        chk(f'mixB{q}_{l}')
        whb = take_w(win_d[l], 256)
        wcg = take_w(win_d[l], 256 + 768)
        wbg = take_w(win_d[l], 256 + 384)
        for c in range(3):
            zi = c % 2
            Z, Zs = ZB["pf"][zi], ZB["sf"][zi]
            o_hb = win_chunk(whb, c * 128)
            hb = ntmp()
            kb.A(hb[:, 0:NPC], o_hb["p"], AF.Copy)
            if "s" in segs:
                kb.A(hb[:, NPC:NCOL], o_hb["s"][:, 0:NS], AF.Copy)
            o_cg = win_chunk(wcg, c * 128)
            if first:
                kb.MS(Z[:, 0:HS], 0.0)
            else:
                kb.CP(Z[:, 0:HS], V(HALO_S[l].ap[:, c * HS:(c + 1) * HS], HALO_S[l].bufs))
            kb.TT(Z[:, HS:HS + NPC], o_cg["p"], hb[:, 0:NPC], ALU.mult)
            if not last:
                kb.CP(V(HALO_S[l].ap[:, c * HS:(c + 1) * HS], HALO_S[l].bufs), Z[:, NPC:NPC + HS])
            if "s" in segs:
                kb.CP(Zs[:, :, 0:HS], ZSP[c])
                kb.TT(Zs[:, :, HS:HS + 4], o_cg["s"][:, 0:NS].re("p (b j) -> p b j", j=4),
                      hb[:, NPC:NCOL].re("p (b j) -> p b j", j=4), ALU.mult)
            ca = ntmp()
            for (zv, cav, is3) in [(Z, ca[:, 0:NPC], False)] + ([(Zs, ca[:, NPC:NCOL].re("p (b j) -> p b j", j=4), True)] if "s" in segs else []):
                n = 4 if is3 else NPC
                for kk in range(3):
                    src = zv[:, :, kk:kk + n] if is3 else zv[:, kk:kk + n]
                    wk = pcol(l, "sconv_w", kk * 3 + c)
                    if kk == 0:
                        kb.TS(cav, src, wk, None, ALU.mult)
                    else:
                        kb.STT(cav, src, wk, cav, ALU.mult, ALU.add)
            if last:
                t = TAILS
                kb.CP(t[:, c * 96:c * 96 + HS], Z[:, NPC:NPC + HS])
                kb.CP(t[:, c * 96 + HS:c * 96 + HS + 32].re("p (j b) -> p j b", j=2), Zs[:, :, HS + 2:HS + 4].re("p b j -> p j b"))
                sconv_tails.append(t[:, c * 96:c * 96 + HS + 32])
            o_bg = win_chunk(wbg, c * 128)
            kb.TT(CAT["p"][2 + c], o_bg["p"], ca[:, 0:NPC], ALU.mult)
            if "s" in segs:
                kb.TT(CAT["s"][2 + c], o_bg["s"][:, 0:NS], ca[:, NPC:NCOL], ALU.mult)
        if last:
            store_rows(sconv_tails, HS + 32,
                       [(0, HS, lambda: osp_d[l, :, :])] +
                       [(HS + 16 * j, 16, (lambda j=j: oss_d[l, :, j, :])) for j in range(2)])
            sconv_tails.clear()

        dgs_built.append(build_diag(l, 1, 1))
        chk(f'mixC{q}_{l}')
        wac = take_w(win_d[l], 1408)
        wbc = take_w(win_d[l], 1792)
        gtv = V(GT.ap[:, 0:288].rearrange("p (c n) -> p c n", c=3), GT.bufs)
        for c in range(3):
            G, Gs = GL["pf"][c], GL["sf"][c]
            o_b = win_chunk(wbc, c * 128)
            sg = ntmp()
            kb.A(sg[:, 0:NPC], o_b["p"], AF.Sigmoid)
            if "s" in segs:
                kb.A(sg[:, NPC:NCOL], o_b["s"][:, 0:NS], AF.Sigmoid)
            o_a = win_chunk(wac, c * 128)
            if first:
                kb.MS(G[:, 0:HC], 0.0)
            else:
                kb.CP(G[:, 0:HC], V(HALO_C[l].ap[:, c * HC:(c + 1) * HC], HALO_C[l].bufs))
            kb.TT(G[:, HC:HC + NPC], o_a["p"], sg[:, 0:NPC], ALU.mult)
            if not last:
                kb.CP(V(HALO_C[l].ap[:, c * HC:(c + 1) * HC], HALO_C[l].bufs), G[:, NPC:NPC + HC])
            if "s" in segs:
                kb.TT(Gs[:, :, HC:HC + 4], o_a["s"][:, 0:NS].re("p (b j) -> p b j", j=4),
                      sg[:, NPC:NCOL].re("p (b j) -> p b j", j=4), ALU.mult)
            if last:
                kb.TT(gtv[:, c, 0:HC], o_a["p"][:, NPC - HC:NPC], sg[:, NPC - HC:NPC], ALU.mult)
                kb.TT(gtv[:, c, HC:HC + NS].re("p (j b) -> p j b", j=4), o_a["s"][:, 0:NS].re("p (b j) -> p j b", j=4),
                      sg[:, NPC:NCOL].re("p (b j) -> p j b", j=4), ALU.mult)
        if last:
            store_rows([gtv[:, c, 0:HC + NS] for c in range(3)], HC + NS,
                       [(0, HC, lambda: ocp_d[l, :, :])] +
                       [(HC + 16 * j, 16, (lambda j=j: ocs_d[l, :, 26 + j, :])) for j in range(4)])
            kb.dma(sp, ocs_d[l, :, 0:26, :], stc_d[l].rearrange("(b r) c -> b r c", r=HC)[:, 4:HC, :], is_out=True)
        wv = take_w(win_d[l], 0)
        for ch in range(2):
            outs = win_chunk(wv, ch * 128)
            E = UP["pf"][0]
            Es = UP["sf"][0]
            if first:
                kb.MS(E[:, 0:HP], 0.0)
            else:
                kb.CP(E[:, 0:HP], V(HALO_P[l].ap[:, ch * HP:(ch + 1) * HP], HALO_P[l].bufs))
            kb.A(E[:, HP:HP + NPC], outs["p"], AF.Copy)
            if not last:
                kb.CP(V(HALO_P[l].ap[:, ch * HP:(ch + 1) * HP], HALO_P[l].bufs), E[:, NPC:NPC + HP])
            if "s" in segs:
                kb.CP(Es[:, :, 0:HP], UPSP[ch])
                kb.A(Es[:, :, HP:HP + 4], outs["s"][:, 0:NS].re("p (b j) -> p b j", j=4), AF.Copy)
            L = HP + NPC
            views = [(E, W1["pf"][0], W2["pf"][0], L, None)]
            if "s" in segs:
                views.append((Es, W1["sf"][0], W2["sf"][0], HP + 4, 1))
            for (e, w1, w2, Ln, is3) in views:
                def sl_(v, a, b):
                    return v[:, :, a:b] if is3 else v[:, a:b]
                kb.TT(sl_(w1, 1, Ln), sl_(e, 1, Ln), sl_(e, 0, Ln - 1), ALU.add)
                kb.TT(sl_(w2, 3, Ln), sl_(w1, 3, Ln), sl_(w1, 1, Ln - 2), ALU.add)
                if ch == 1 and not (is3 and DBG_SKIP):
                    kb.TT(sl_(w1, 7, Ln), sl_(w2, 7, Ln), sl_(w2, 3, Ln - 4), ALU.add)
                    kb.TT(sl_(w2, 15, Ln), sl_(w1, 15, Ln), sl_(w1, 7, Ln - 8), ALU.add)
                for half, wsrc in ((0, w1), (1, w2)):
                    wdw = (2, 4, 8, 16)[2 * ch + half]
                    pr = slice(64 * half, 64 * half + 64)
                    if is3:
                        o_ = PLS[ch][pr, NPC:NCOL].re("p (b j) -> p b j", j=4)
                        kb.STT(o_, wsrc[pr, :, HP:HP + 4], 1.0 / wdw, e[pr, :, HP:HP + 4], ALU.mult, ALU.subtract)
                    else:
                        kb.STT(PLS[ch][pr, 0:NPC], wsrc[pr, HP:Ln], 1.0 / wdw, e[pr, HP:Ln], ALU.mult, ALU.subtract)
                        if first:
                            t = ntmp()
                            kb.TT(t[pr, 0:HP], wsrc[pr, HP:2 * HP], icv[pr, ch, :], ALU.mult)
                            kb.TT(PLS[ch][pr, 0:HP], t[pr, 0:HP], e[pr, HP:2 * HP], ALU.subtract)
            chk(f'mixB1{q}_{l}_{ch}')
            if last:
                t = TAILS
                kb.CP(t[:, ch * 96:ch * 96 + HP], E[:, NPC:NPC + HP])
                kb.CP(t[:, ch * 96 + HP:ch * 96 + HP + NS].re("p (j b) -> p j b", j=4), Es[:, :, HP:HP + 4].re("p b j -> p j b"))
                pool_tails.append(t[:, ch * 96:ch * 96 + HP + NS])
        chk(f'mixB3{q}_{l}')
        if last:
            store_rows(pool_tails, HP + NS,
                       [(0, HP, lambda: opp_d[l, :, :])] +
                       [(HP + 16 * j, 16, (lambda j=j: ops_d[l, :, 11 + j, :])) for j in range(4)])
            pool_tails.clear()
            chk(f'mixB4{q}_{l}')
            kb.dma(sp, ops_d[l, :, 0:11, :], stp_d[l].rearrange("(b r) c -> b r c", r=HP)[:, 4:HP, :], is_out=True)

        chk(f'mixD{q}_{l}')
        units = [("p", t) for t in range(NT)] + ([("s", 0)] if "s" in segs else [])
        NVB = NPC + NS
        vbv = V(VB.ap.rearrange("p (c n) -> p c n", c=3), VB.bufs)
        vqv = V(VQ.ap.rearrange("p (c n) -> p c n", c=3), VQ.bufs)
        for c in range(3):
            dgv = dgs_built[c]
            G, Gs = GL["pf"][c], GL["sf"][c]
            for (seg, t) in units:
                n = 512 if seg == "p" else NS
                o0 = t * 512 if seg == "p" else NPC
                if seg == "p":
                    sl = rslot()
                    kb.MM(lambda sl=sl, G=G, t=t, dgv=dgv: [nc.tensor.matmul(sl.ap[:, 0:512], lhsT=dgv.ap[:, kk, :],
                                                                     rhs=G.ap[:, kk + t * 512:kk + t * 512 + 512],
                                                                     start=(kk == 0), stop=(kk == 30)) for kk in range(31)][-1],
                          [dgv, G], [sl])
                    src = sl[:, 0:512]
                else:
                    sl = sslot()
                    kb.MM(lambda sl=sl, Gs=Gs, dgv=dgv: [nc.tensor.matmul(sl.ap[:, 0:NS], lhsT=dgv.ap[:, kk, :],
                                                                  rhs=Gs.ap[:, :, kk:kk + 4],
                                                                  start=(kk == 0), stop=(kk == 30)) for kk in range(31)][-1],
                          [dgv, Gs], [sl])
                    src = sl[:, 0:NS]
                kb.A(vbv[:, c, o0:o0 + n], src, AF.Identity, bias=pcol(l, "cconv_b", c))
                kb.A(vqv[:, c, o0:o0 + n], vbv[:, c, o0:o0 + n], AF.Square)
            if c == 0:
                dgs_built.append(build_diag(l, 2, 0))
        for ch in range(2):
            chk(f'mixB2{q}_{l}_{ch}')
            o2 = mm_chunk(lambda k: PWv.ap[:, l, ch, :], 1,
                          lambda k, t: PLS[ch].ap[:, t * 512:(t + 1) * 512], lambda k: PLS[ch].ap[:, NPC:NCOL], segs, [PWv, PLS[ch]])
            for seg in segs:
                kb.A(CAT[seg][ch], o2[seg] if seg == "p" else o2[seg][:, 0:NS], AF.Identity, scale=pcol(l, "pool_scale", ch))
            chk(f'mixB5{q}_{l}_{ch}')
        for (seg, t) in units:
            n = 512 if seg == "p" else NS
            o0 = t * 512 if seg == "p" else NPC
            vbu = vbv[:, :, o0:o0 + n]
            vqu = vqv[:, :, o0:o0 + n]
            s1 = rslot()
            s2 = rslot()
            for (sv, srcv) in ((s1, vbu), (s2, vqu)):
                kb.MM(lambda sv=sv, srcv=srcv, n=n: [nc.tensor.matmul(sv.ap[:, 0:n], lhsT=ones_f.ap, rhs=srcv.ap[:, c, :],
                                                                      start=(c == 0), stop=(c == 2)) for c in range(3)][-1],
                      [srcv, ones_f], [sv])
            kb.TS(LNM[:, 0:n], s1[:, 0:n], 1.0 / DC, None, ALU.mult)
            t_ = ntmp()
            kb.TT(t_[:, 0:n], LNM[:, 0:n], LNM[:, 0:n], ALU.mult)
            kb.STT(LNR[:, 0:n], s2[:, 0:n], 1.0 / DC, t_[:, 0:n], ALU.mult, ALU.subtract)
            kb.A(LNR[:, 0:n], LNR[:, 0:n], AF.Ln, bias=float(EPS))
            kb.A(LNR[:, 0:n], LNR[:, 0:n], AF.Exp, scale=-0.5)
            for c in range(3):
                kb.TT(vbu[:, c], vbu[:, c], LNM[:, 0:n], ALU.subtract)
                kb.TT(vbu[:, c], vbu[:, c], LNR[:, 0:n], ALU.mult)
                dst = CAT["p"][5 + c][:, t * 512:t * 512 + 512] if seg == "p" else CAT["s"][5 + c]
                kb.A(dst, vbu[:, c], AF.Silu, bias=pcol(l, "cln_b", c), scale=pcol(l, "cln_g", c))

        chk(f'mixE{q}_{l}')
        out_proj(l, 0, wout_d[l], KC, CAT, MX, segs)

    def ffn(l, q, segs):
        last = (q == NPIECE - 1)
        first = (q == 0)
        prenorm(l, 1, segs)
        hreads = [H[s][k] for s in segs for k in range(KC)]
        upsv = V(UPS.ap.rearrange("p (c b r) -> p c b r", c=NFC, b=16), UPS.bufs)
        fstvs = [V(f_.ap.rearrange("p (c n) -> p c n", c=11), f_.bufs) for f_ in FST]
        hfv = V(HALO_F[l].ap.rearrange("p (c r) -> p c r", c=NFC), HALO_F[l].bufs)
        if last:
            for rd in range(4):
                st = V(STG.ap[0:32, 0:1408], STG.bufs)
                kb.dma(sp, st.ap, stf_d[l, :, rd * 1408:(rd + 1) * 1408], writes=[st])
                sl = rslot()
                kb.MM(lambda st=st, sl=sl: [nc.tensor.transpose(out=sl.ap[:, c * 32:(c + 1) * 32], in_=st.ap[:, c * 128:(c + 1) * 128],
                                                                 identity=ident_f.ap[0:32, 0:32]) for c in range(11)][-1],
                      [st, ident_f], [sl])
                kb.A(upsv[:, rd * 11:(rd + 1) * 11], sl[:, 0:352].re("p (c b r) -> p c b r", c=11, b=16), AF.Copy)
        ub_i = 0
        ac_i = 0
        pt_i = 0
        if first:
            for ub in UPB:
                kb.MS(ub["pf"][0][:, 0:HF], 0.0)
        for j in range(NPAIR):
            wv = take_w(wup_d[l], j * 128, wl=1)
            accs = []
            for gv in range(2):
                ci = j + gv * NPAIR
                outs = mm_chunk(lambda k, gv=gv: wv[gv].ap[:, k, :], KC,
                                lambda k, t: H["p"][k].ap[:, t * 512:(t + 1) * 512],
                                lambda k: H["s"][k].ap, segs, [wv[gv]] + hreads)
                ub = UPB[ub_i % NUPB]; ub_i += 1
                acc = ACC[ac_i % NACC]; ac_i += 1
                U, Us = ub["pf"][0], ub["sf"][0]
                if not first:
                    kb.A(U[:, 0:HF], hfv[:, ci, :], AF.Copy)
                kb.A(U[:, HF:HF + NPC], outs["p"], AF.Copy)
                kb.A(acc[:, 0:NPC], outs["p"], AF.Identity, scale=pcol(l, "ffn_conv_w", 2 * NFC + ci))
                if not last:
                    kb.A(hfv[:, ci, :], U[:, NPC:NPC + HF], AF.Copy)
                if "s" in segs:
                    o4 = outs["s"][:, 0:NS].re("p (b j) -> p b j", j=4)
                    kb.A(Us[:, :, 0:HF], upsv[:, ci], AF.Copy)
                    kb.A(Us[:, :, HF:HF + 4], o4, AF.Copy)
                    kb.A(acc[:, NPC:NCOL], outs["s"][:, 0:NS], AF.Identity, scale=pcol(l, "ffn_conv_w", 2 * NFC + ci))
                if last:
                    rc = ci % 11
                    fstv = fstvs[gv]
                    kb.A(fstv[:, rc, 0:HF], outs["p"][:, NPC - HF:NPC], AF.Copy)
                    kb.A(fstv[:, rc, HF:HF + 32].re("p (j b) -> p j b", j=2), o4[:, :, 2:4].re("p b j -> p j b"), AF.Copy)
                for kk in (1, 0):
                    wk = pcol(l, "ffn_conv_w", kk * NFC + ci)
                    if True:
                        kb.STT(acc[:, 0:NPC], U[:, kk:kk + NPC], wk, acc[:, 0:NPC], ALU.mult, ALU.add)
                        if "s" in segs:
                            a4 = acc[:, NPC:NCOL].re("p (b j) -> p b j", j=4)
                            kb.STT(a4, Us[:, :, kk:kk + 4], wk, a4, ALU.mult, ALU.add)
                    else:
                        pt_ = PTMP[pt_i % 2]; pt_i += 1
                        kb.TS(pt_[:, 0:NPC], U[:, kk:kk + NPC], wk, None, ALU.mult, eng=pool)
                        if "s" in segs:
                            kb.TS(pt_[:, NPC:NCOL].re("p (b j) -> p b j", j=4), Us[:, :, kk:kk + 4], wk, None, ALU.mult, eng=pool)
                        nn_ = NCOL if "s" in segs else NPC
                        kb.TT(acc[:, 0:nn_], acc[:, 0:nn_], pt_[:, 0:nn_], ALU.add, eng=pool)
                accs.append(acc)
                if last and ci % 11 == 10:
                    rd = ci // 11
                    for g0 in range(0, 11, 4):
                        ng = min(4, 11 - g0)
                        cs = (rd * 11 + g0) * 128
                        store_rows([fstvs[gv][:, g0 + i, 0:34] for i in range(ng)], 34,
                                   [(0, HF, (lambda cs=cs, ng=ng: ofp_d[l, :, cs:cs + ng * 128]))] +
                                   [(HF + 16 * jj, 16, (lambda jj=jj, cs=cs, ng=ng: ofs_d[l, :, jj, cs:cs + ng * 128])) for jj in range(2)])
            nn = NCOL if "s" in segs else NPC
            kb.A(accs[0][:, 0:nn], accs[0][:, 0:nn], AF.Silu)
            kb.TT(V(AA["full"][:, j, 0:nn], [AA["bp"][j], AA["bs"][j]]), accs[0][:, 0:nn], accs[1][:, 0:nn], ALU.mult)
            if q == 0 and l + 1 < DEPTH and j < 16:
                mod_block(l + 1, j)
                if j == 15:
                    mod_finish(l + 1)
        out_proj(l, 1, wdn_d[l], NPAIR, AA, FF, segs)

    UPSP = [kb.sbn(f"upsp{c}", 16 * HP) for c in range(2)]
    UPSP = [V(u.ap.rearrange("p (b r) -> p b r", r=HP), u.bufs) for u in UPSP]
    ZSP = [kb.sbn(f"zsp{c}", 16 * HS) for c in range(3)]
    ZSP = [V(u.ap.rearrange("p (b r) -> p b r", r=HS), u.bufs) for u in ZSP]
    pool_tails = []
    sconv_tails = []
    print("SBUF words used (final)", kb.top, "of", kb.nwords)

    try:
        chk('setup')
        compute_mod(0)
        chk('mod0')
        for q in range(NPIECE):
            last = (q == NPIECE - 1)
            segs = ["p", "s"] if last else ["p"]
            blocks = [(xp_d, q * NPC + tb * 128, 128, tb * 128) for tb in range(NPC // 128)]
            if last:
                blocks.append((xs_d, 0, NS, NPC))
            for bi, (src, r0, nr, c0) in enumerate(blocks):
                io = IOX[bi % 2]
                iov = V(io.ap[0:nr, :], io.bufs)
                kb.dma(sp, iov.ap, src[r0:r0 + nr, :], writes=[iov])
                sl = rslot() if NT == 2 else None
                if NT == 2:
                    kb.MM(lambda iov=iov, sl=sl, nr=nr: [nc.tensor.transpose(out=sl.ap[:, k * 128:k * 128 + nr], in_=iov.ap[:, k * 128:(k + 1) * 128],
                                                                             identity=ident_f.ap[0:nr, 0:nr]) for k in range(KC)][-1],
                          [iov, ident_f], [sl])
                    dst = V(X["full"][:, :, c0:c0 + nr], X["bp"] + X["bs"])
                    kb.A(dst, sl[:, 0:KC * 128].re("p (k n) -> p k n", k=KC)[:, :, 0:nr], AF.Copy)
                else:
                    for hk in range(2):
                        sl = rslot()
                        kb.MM(lambda iov=iov, sl=sl, nr=nr, hk=hk: [nc.tensor.transpose(out=sl.ap[:, k * 128:k * 128 + nr],
                                                                                        in_=iov.ap[:, (hk * 4 + k) * 128:(hk * 4 + k + 1) * 128],
                                                                                        identity=ident_f.ap[0:nr, 0:nr]) for k in range(4)][-1],
                              [iov, ident_f], [sl])
                        dst = V(X["full"][:, hk * 4:hk * 4 + 4, c0:c0 + nr], X["bp"] + X["bs"])
                        kb.A(dst, sl[:, 0:512].re("p (k n) -> p k n", k=4)[:, :, 0:nr], AF.Copy)
            chk(f'xload{q}')
            for l in range(DEPTH):
                mix(l, q, segs)
                chk(f'mix{q}_{l}')
                ffn(l, q, segs)
                chk(f'ffn{q}_{l}')
            for bi, (src, r0, nr, c0) in enumerate(blocks):
                io = IOX[bi % 2]
                iov = V(io.ap[0:nr, :], io.bufs)
                for hk in range(2):
                    sl = rslot()
                    xr = [X["p"][hk * 4 + k] if c0 < NPC else X["s"][hk * 4 + k] for k in range(4)]
                    kb.MM(lambda sl=sl, nr=nr, c0=c0, hk=hk: [nc.tensor.transpose(out=sl.ap[0:nr, k * 128:(k + 1) * 128],
                                                                                  in_=X["full"][:, hk * 4 + k, c0:c0 + nr],
                                                                                  identity=ident_f.ap) for k in range(4)][-1],
                          xr + [ident_f], [sl])
                    kb.A(iov[:, hk * 512:(hk + 1) * 512], sl[0:nr, 0:512], AF.Copy)
                dstd = (yp_d if src is xp_d else ys_d)[r0:r0 + nr, :]
                kb.dma(sp, dstd, iov.ap, reads=[iov], is_out=True)
    except _Stop:
        pass

    best = {}
    for (s, v) in kb.out_toks:
        if best.get(s.key, (None, 0))[1] < v:
            best[s.key] = (s, v)
    for k, (s, v) in best.items():
        nc.sync.wait_ge(s.h, v)
    build.marks = kb.marks
    return nc


_NC = None


def kernel(x_prompt, x_sample, c_prompt, c_sample, state_pool, state_sconv, state_cconv, state_ffn,
           w_ada, b_ada, g_mix_pre, g_mix_post, g_ffn_pre, g_ffn_post,
           w_in, pool_w, pool_scale, sconv_w, cconv_w, cconv_b, cln_g, cln_b,
           w_out, w_up, ffn_conv_w, w_down):
    global _NC
    f = lambda a: np.ascontiguousarray(np.asarray(a, dtype=np.float32))
    x_prompt, x_sample, c_prompt, c_sample = map(f, (x_prompt, x_sample, c_prompt, c_sample))
    state_pool, state_sconv, state_cconv, state_ffn = map(f, (state_pool, state_sconv, state_cconv, state_ffn))
    rows = []
    for l in range(DEPTH):
        parts = [f(b_ada)[l].reshape(48, 128), f(g_mix_pre)[l].reshape(8, 128), f(g_mix_post)[l].reshape(8, 128),
                 f(g_ffn_pre)[l].reshape(8, 128), f(g_ffn_post)[l].reshape(8, 128), f(pool_scale)[l].reshape(2, 128),
                 f(sconv_w)[l].reshape(9, 128), f(cconv_w)[l].reshape(93, 128), f(cconv_b)[l].reshape(3, 128),
                 f(cln_g)[l].reshape(3, 128), f(cln_b)[l].reshape(3, 128), f(ffn_conv_w)[l].reshape(132, 128)]
        rows.append(np.concatenate(parts, axis=0))
    ptab = np.ascontiguousarray(np.stack(rows, 0))
    shared = {"w_ada": f(w_ada), "ptab": ptab, "w_in": f(w_in), "pool_w": f(pool_w), "w_out": f(w_out),
              "w_up": f(w_up), "w_down": f(w_down)}
    in_maps = []
    for i in range(NCORES):
        sl = slice(16 * i, 16 * i + 16)
        m = dict(shared)
        m["xp"] = x_prompt[i]
        m["xs"] = np.ascontiguousarray(x_sample[sl].reshape(NS, D))
        m["cc"] = np.ascontiguousarray(np.concatenate([c_prompt[i:i + 1], c_sample[sl]], axis=0))
        m["st_pool"] = np.ascontiguousarray(state_pool[:, sl].reshape(DEPTH, 16 * HP, DP))
        m["st_sconv"] = np.ascontiguousarray(state_sconv[:, sl].reshape(DEPTH, 16 * HS, DS))
        m["st_cconv"] = np.ascontiguousarray(state_cconv[:, sl].reshape(DEPTH, 16 * HC, DC))
        m["st_ffn"] = np.ascontiguousarray(state_ffn[:, sl].reshape(DEPTH, 16 * HF, 2 * DFF))
        in_maps.append(m)
    if _NC is None:
        _NC = build()
    res = run_bass_kernel_spmd(_NC, in_maps, core_ids=list(range(NCORES)))
    R = res.results
    cat = lambda k, ax: np.concatenate([np.asarray(R[i][k]) for i in range(NCORES)], axis=ax)
    yp = np.stack([np.asarray(R[i]["yp"]) for i in range(NCORES)], 0)
    ys = cat("ys", 0).reshape(128, 4, D)
    outs = [yp, ys]
    for k in ("o_pool_p", "o_sconv_p", "o_cconv_p", "o_ffn_p"):
        outs.append(np.stack([np.asarray(R[i][k]) for i in range(NCORES)], 1))
    for k in ("o_pool_s", "o_sconv_s", "o_cconv_s", "o_ffn_s"):
        outs.append(cat(k, 1))
    return tuple(np.ascontiguousarray(o.astype(np.float32)) for o in outs)
```

```python
import numpy as np
import concourse.bass as bass
import concourse.mybir as mybir
from concourse.bass_utils import run_bass_kernel_spmd

F32 = mybir.dt.float32
BF16 = mybir.dt.bfloat16
AF = mybir.ActivationFunctionType
ALU = mybir.AluOpType

NCORES = 8
D = 1024
KC = 8
DEPTH = 4
SEQ = 2048
NSEQ_S = 16
NS = 64
DP, DS, DC, DFF = 256, 384, 384, 2816
DIN = 2176
NFC = 44
NPAIR = 22
EPS = 1e-6
NPC = 512
NPIECE = SEQ // NPC
NT = NPC // 512
NCOL = NPC + NS
HP, HS, HC, HF = 15, 2, 30, 2

PO = {}
_o = 0
for _n, _r in (("b_ada", 48), ("g_mix_pre", 8), ("g_mix_post", 8), ("g_ffn_pre", 8), ("g_ffn_post", 8),
               ("pool_scale", 2), ("sconv_w", 9), ("cconv_w", 93), ("cconv_b", 3), ("cln_g", 3),
               ("cln_b", 3), ("ffn_conv_w", 132)):
    PO[_n] = _o
    _o += _r
NPT = _o
DBG_SKIP = False


class _Stop(Exception):
    pass


class Buf:
    __slots__ = ("name", "space", "lo", "hi", "w", "r", "ov")

    def __init__(self, name, space, lo, hi):
        self.name, self.space, self.lo, self.hi = name, space, lo, hi
        self.w = None
        self.r = {}
        self.ov = None


class V:
    def __init__(self, ap, bufs):
        self.ap = ap
        self.bufs = list(bufs)

    def __getitem__(self, key):
        return V(self.ap[key], self.bufs)

    def re(self, s, **kw):
        return V(self.ap.rearrange(s, **kw), self.bufs)

    def bc(self, shape):
        return V(self.ap.to_broadcast(shape), self.bufs)

    def us(self, axis):
        return V(self.ap.unsqueeze(axis), self.bufs)


class Sem:
    def __init__(self, h, key):
        self.h, self.key, self.total = h, key, 0


class Eng:
    def __init__(self, nc, name, h, is_pe=False):
        self.name, self.h, self.is_pe = name, h, is_pe
        self.sem = Sem(nc.alloc_semaphore("s_" + name), "s_" + name)
        self.count = 0
        self.seen = {}
        self.dsems = []
        self.di = 0


class KB:
    def __init__(self):
        nc = bass.Bass("TRN2", target_bir_lowering=False)
        self.nc = nc
        self.pe = Eng(nc, "pe", nc.tensor, True)
        self.act = Eng(nc, "act", nc.scalar)
        self.dve = Eng(nc, "dve", nc.vector)
        self.pool = Eng(nc, "pool", nc.gpsimd)
        self.sp = Eng(nc, "sp", nc.sync)
        for e, n in ((self.sp, 14), (self.pool, 12)):
            for i in range(n):
                e.dsems.append(Sem(nc.alloc_semaphore(f"d_{e.name}{i}"), f"d_{e.name}{i}"))
        self.bufs = {"sb": [], "ps": []}
        self.nwords = 53100
        self.arena = nc.alloc_sbuf_tensor("arena", [128, self.nwords], F32)
        self.psum = nc.alloc_psum_tensor("psum", [128, 4096], F32)
        self.top = 0
        self.out_toks = []
        self.opn = 0
        self.stop_n = 0
        self.marks = {}

    def mkbuf(self, name, space, lo, hi):
        b = Buf(name, space, lo, hi)
        lst = self.bufs[space]
        b.ov = [b]
        for o in lst:
            if o.lo < hi and lo < o.hi:
                o.ov.append(b)
                b.ov.append(o)
        lst.append(b)
        return b

    def alloc(self, words):
        lo = self.top
        self.top += words
        assert self.top <= self.nwords, f"SBUF arena overflow {self.top}"
        return lo

    def sb(self, name, lo, words, dt=F32, shape=None):
        b = self.mkbuf(name, "sb", lo, lo + words)
        ap = self.arena[:, lo:lo + words]
        if dt != F32:
            ap = ap.bitcast(dt)
        return V(ap, [b])

    def sbn(self, name, words, dt=F32):
        return self.sb(name, self.alloc(words), words, dt)

    def ps(self, name, lo, n):
        b = self.mkbuf(name, "ps", lo, lo + n)
        return V(self.psum[:, lo:lo + n], [b])

    def _deps(self, eng, reads, writes):
        toks = {}
        for b in reads:
            for ob in b.ov:
                if ob.w is not None:
                    s, v = ob.w
                    if toks.get(s.key, (None, 0))[1] < v:
                        toks[s.key] = (s, v)
        for b in writes:
            for ob in b.ov:
                if ob.w is not None:
                    s, v = ob.w
                    if toks.get(s.key, (None, 0))[1] < v:
                        toks[s.key] = (s, v)
                for k, (s, v) in ob.r.items():
                    if toks.get(k, (None, 0))[1] < v:
                        toks[k] = (s, v)
        for k, (s, v) in toks.items():
            if eng.is_pe and s is eng.sem:
                continue
            if eng.seen.get(k, 0) >= v:
                continue
            eng.h.wait_ge(s.h, v)
            eng.seen[k] = v

    def _commit(self, tok, reads, writes):
        s, v = tok
        ws = set(id(b) for b in writes)
        for b in writes:
            b.w = tok
            b.r = {}
        for b in reads:
            if id(b) in ws:
                continue
            if b.r.get(s.key, (None, 0))[1] < v:
                b.r[s.key] = (s, v)

    def emit(self, eng, fn, reads, writes):
        self.opn += 1
        if self.stop_n and self.opn == self.stop_n:
            raise _Stop()
        rb = [b for v in reads for b in v.bufs]
        wb = [b for v in writes for b in v.bufs]
        self._deps(eng, rb, wb)
        ins = fn()
        eng.count += 1
        ins.then_inc(eng.sem.h, 1)
        tok = (eng.sem, eng.count)
        self._commit(tok, rb, wb)
        return tok

    def dma(self, eng, out, in_, reads=(), writes=(), is_out=False):
        self.opn += 1
        if self.stop_n and self.opn == self.stop_n:
            raise _Stop()
        rb = [b for v in reads for b in v.bufs]
        wb = [b for v in writes for b in v.bufs]
        s = eng.dsems[eng.di % len(eng.dsems)]
        eng.di += 1
        if s.total > 0 and eng.seen.get(s.key, 0) < s.total:
            eng.h.wait_ge(s.h, s.total)
            eng.seen[s.key] = s.total
        self._deps(eng, rb, wb)
        ins = eng.h.dma_start(out=out, in_=in_)
        ins.then_inc(s.h, 16)
        s.total += 16
        tok = (s, s.total)
        self._commit(tok, rb, wb)
        if is_out:
            self.out_toks.append(tok)
        return tok

    def A(self, out, in_, func, bias=None, scale=None, eng=None):
        rd = [in_] + [x for x in (bias, scale) if isinstance(x, V)]
        kw = {}
        if bias is not None:
            kw["bias"] = bias.ap if isinstance(bias, V) else bias
        if scale is not None:
            kw["scale"] = scale.ap if isinstance(scale, V) else scale
        return self.emit(self.act, lambda: self.nc.scalar.activation(out=out.ap, in_=in_.ap, func=func, **kw),
                         rd, [out])

    def TS(self, out, in0, s1, s2, op0, op1=None, eng=None):
        eng = eng or self.dve
        rd = [in0] + [x for x in (s1, s2) if isinstance(x, V)]
        a1 = s1.ap if isinstance(s1, V) else s1
        a2 = s2.ap if isinstance(s2, V) else s2
        kw = {}
        if op1 is not None:
            kw["op1"] = op1
        return self.emit(eng, lambda: eng.h.tensor_scalar(out=out.ap, in0=in0.ap, scalar1=a1, scalar2=a2,
                                                          op0=op0, **kw), rd, [out])

    def TT(self, out, in0, in1, op, eng=None):
        eng = eng or self.dve
        return self.emit(eng, lambda: eng.h.tensor_tensor(out=out.ap, in0=in0.ap, in1=in1.ap, op=op),
                         [in0, in1], [out])

    def STT(self, out, in0, sc, in1, op0, op1, eng=None):
        eng = eng or self.dve
        rd = [in0, in1] + ([sc] if isinstance(sc, V) else [])
        a = sc.ap if isinstance(sc, V) else sc
        return self.emit(eng, lambda: eng.h.scalar_tensor_tensor(out=out.ap, in0=in0.ap, scalar=a, in1=in1.ap,
                                                                 op0=op0, op1=op1), rd, [out])

    def RC(self, out, in_):
        return self.emit(self.dve, lambda: self.nc.vector.reciprocal(out=out.ap, in_=in_.ap), [in_], [out])

    def CP(self, out, in_, eng=None):
        eng = eng or self.dve
        return self.emit(eng, lambda: eng.h.tensor_copy(out=out.ap, in_=in_.ap), [in_], [out])

    def MS(self, out, val, eng=None):
        eng = eng or self.dve
        return self.emit(eng, lambda: eng.h.memset(out.ap, val), [], [out])

    def MM(self, fn, reads, writes):
        return self.emit(self.pe, fn, reads, writes)


def build(stop=None):
    kb = KB()

    def chk(tag):
        kb.marks[tag] = kb.opn
        if stop is not None and tag == stop:
            raise _Stop()
    nc = kb.nc
    pe, act, dve, pool, sp = kb.pe, kb.act, kb.dve, kb.pool, kb.sp

    def din(name, shape):
        return nc.dram_tensor(name, list(shape), F32, kind="ExternalInput").ap()

    def dout(name, shape):
        return nc.dram_tensor(name, list(shape), F32, kind="ExternalOutput").ap()

    xp_d = din("xp", [SEQ, D]); xs_d = din("xs", [NS, D]); cc_d = din("cc", [17, D])
    stp_d = din("st_pool", [DEPTH, 16 * HP, DP]); sts_d = din("st_sconv", [DEPTH, 16 * HS, DS])
    stc_d = din("st_cconv", [DEPTH, 16 * HC, DC]); stf_d = din("st_ffn", [DEPTH, 16 * HF, 2 * DFF])
    wada_d = din("w_ada", [DEPTH, D, 6 * D]); pt_d = din("ptab", [DEPTH, NPT, 128])
    win_d = din("w_in", [DEPTH, D, DIN]); pw_d = din("pool_w", [DEPTH, 4, 64, 64])
    wout_d = din("w_out", [DEPTH, D, D]); wup_d = din("w_up", [DEPTH, D, 2 * DFF])
    wdn_d = din("w_down", [DEPTH, DFF, D])
    yp_d = dout("yp", [SEQ, D]); ys_d = dout("ys", [NS, D])
    opp_d = dout("o_pool_p", [DEPTH, HP, DP]); osp_d = dout("o_sconv_p", [DEPTH, HS, DS])
    ocp_d = dout("o_cconv_p", [DEPTH, HC, DC]); ofp_d = dout("o_ffn_p", [DEPTH, HF, 2 * DFF])
    ops_d = dout("o_pool_s", [DEPTH, 16, HP, DP]); oss_d = dout("o_sconv_s", [DEPTH, 16, HS, DS])
    ocs_d = dout("o_cconv_s", [DEPTH, 16, HC, DC]); ofs_d = dout("o_ffn_s", [DEPTH, 16, HF, 2 * DFF])

    wada_d = [wada_d[l] for l in range(DEPTH)]; win_d = [win_d[l] for l in range(DEPTH)]
    wout_d = [wout_d[l] for l in range(DEPTH)]; wup_d = [wup_d[l] for l in range(DEPTH)]; wdn_d = [wdn_d[l] for l in range(DEPTH)]

    W_X = KC * NCOL
    x_lo = kb.alloc(W_X)
    hc_lo = kb.alloc(W_X)
    a_lo = kb.alloc(NPAIR * NCOL // 2)
    A_WORDS = NPAIR * NCOL // 2

    def chunked(name, lo, nch, dt):
        wpc = NCOL if dt == F32 else NCOL // 2
        full = kb.arena[:, lo:lo + nch * wpc]
        if dt != F32:
            full = full.bitcast(dt)
        full = full.rearrange("p (c n) -> p c n", c=nch)
        res = {"full": full, "p": [], "s": [], "bp": [], "bs": []}
        pw = NPC if dt == F32 else NPC // 2
        for c in range(nch):
            bp = kb.mkbuf(f"{name}{c}p", "sb", lo + c * wpc, lo + c * wpc + pw)
            bs = kb.mkbuf(f"{name}{c}s", "sb", lo + c * wpc + pw, lo + (c + 1) * wpc)
            res["p"].append(V(full[:, c, 0:NPC], [bp]))
            res["s"].append(V(full[:, c, NPC:NCOL], [bs]))
            res["bp"].append(bp); res["bs"].append(bs)
        res["sall"] = V(full[:, :, NPC:NCOL], res["bs"])
        return res

    X = chunked("x", x_lo, KC, F32)
    H = chunked("h", hc_lo, KC, BF16)
    CAT = chunked("cat", hc_lo + W_X // 2, KC, BF16)
    FF = chunked("ff", hc_lo, KC, F32)
    MX = chunked("mx", a_lo, KC, F32)
    AA = chunked("a", a_lo, NPAIR, BF16)

    WSLOT = 1536
    NWS = 5
    wring = [kb.sbn(f"wr{i}", WSLOT, BF16) for i in range(NWS)]
    wr_i = [0]
    whalf = [[kb.sb(f"wrh{i}_{g}", w_.bufs[0].lo + g * 512, 512, BF16) for g in range(2)] for i, w_ in enumerate(wring)]

    PT = kb.sbn("pt", DEPTH * NPT)
    PTv = V(PT.ap.rearrange("p (l n) -> p l n", l=DEPTH), PT.bufs)
    DVA = [kb.sbn(f"dv{l}", 6 * KC * 17) for l in range(DEPTH)]
    ident_f = kb.sbn("ident_f", 128); ident_b = kb.sbn("ident_b", 64, BF16)
    ones_b = kb.sbn("ones_b", 64, BF16); ones_f = kb.sbn("ones_f", 128)
    invcnt = kb.sbn("invcnt", 32)
    PW = kb.sbn("pw", DEPTH * 2 * 64, BF16)
    PWv = V(PW.ap.rearrange("p (l c n) -> p l c n", l=DEPTH, c=2), PW.bufs)
    CTt = kb.sbn("ct", 80, BF16)
    CTv = V(CTt.ap[:, 0:KC * 17].rearrange("p (k s) -> p k s", k=KC), CTt.bufs)
    RSp = kb.sbn("rsp", NPC); RSs = kb.sbn("rss", NS)
    SQ = [kb.sbn(f"sq{i}", NCOL // 2, BF16) for i in range(3)]
    TMP = [kb.sbn(f"tmp{i}", NCOL) for i in range(4)]
    sq_i = [0]; tmp_i = [0]

    def nsq():
        sq_i[0] += 1
        return SQ[sq_i[0] % len(SQ)]

    def ntmp():
        tmp_i[0] += 1
        return TMP[tmp_i[0] % len(TMP)]

    HALO_P = [kb.sbn(f"hp{l}", 2 * HP) for l in range(DEPTH)]
    HALO_S = [kb.sbn(f"hs{l}", 3 * HS) for l in range(DEPTH)]
    HALO_C = [kb.sbn(f"hcv{l}", 3 * HC // 2, BF16) for l in range(DEPTH)]
    HALO_F = [kb.sbn(f"hf{l}", NFC * HF // 2, BF16) for l in range(DEPTH)]

    def padded(name, lo, nch, Hh, dt):
        cols = Hh + NPC + 16 * (Hh + 4)
        colsw = cols if dt == F32 else (cols + 1) // 2
        res = {"cols": cols, "H": Hh, "pf": [], "sf": [], "words": nch * colsw}
        for c in range(nch):
            l0 = lo + c * colsw
            ap = kb.arena[:, l0:l0 + colsw]
            if dt != F32:
                ap = ap.bitcast(dt)
            pcw = (Hh + NPC) if dt == F32 else (Hh + NPC) // 2
            bp = kb.mkbuf(f"{name}{c}p", "sb", l0, l0 + pcw)
            bs = kb.mkbuf(f"{name}{c}s", "sb", l0 + pcw, l0 + colsw)
            res["pf"].append(V(ap[:, 0:Hh + NPC], [bp]))
            res["sf"].append(V(ap[:, Hh + NPC:Hh + NPC + 16 * (Hh + 4)].rearrange("p (b j) -> p b j", j=Hh + 4), [bs]))
        return res

    MODT = kb.sbn("modt", 48 * 17 + 16)
    STG = kb.sbn("stg", 1536)
    OST = kb.sbn("ost", 1536)
    IOX = [kb.sb(f"iox{i}", a_lo + i * 1024, 1024) for i in range(2)]
    m_lo = kb.top
    mlo = m_lo
    UP = padded("up", mlo, 1, HP, F32); mlo += UP["words"]
    W1 = padded("w1", mlo, 1, HP, F32); mlo += W1["words"]
    W2 = padded("w2", mlo, 1, HP, F32); mlo += W2["words"]
    PLS = []
    for i in range(2):
        PLS.append(kb.sb(f"pl{i}", mlo, NCOL // 2, BF16)); mlo += NCOL // 2
    ZB = padded("zb", mlo, 2, HS, F32); mlo += ZB["words"]
    GL = padded("gl", mlo, 3, HC, BF16); mlo += GL["words"]
    VB = kb.sb("vb", mlo, 3 * NCOL); mlo += 3 * NCOL
    VQ = kb.sb("vq", mlo, 3 * NCOL); mlo += 3 * NCOL
    LNM = kb.sb("lnm", mlo, 512); mlo += 512
    LNR = kb.sb("lnr", mlo, 512); mlo += 512
    GT = kb.sb("gt", mlo, 3 * 96); mlo += 3 * 96
    TAILS = kb.sb("tails", mlo, 3 * 96); mlo += 3 * 96
    DGS = []
    for i in range(2):
        DGS.append(kb.sb(f"dg{i}", mlo, 31 * 64, BF16)); mlo += 31 * 64
    m_hi = mlo
    flo = m_lo
    UPBw = (HF + NPC + 16 * (HF + 4)) // 2
    UPB = []
    NUPB, NACC = 6, 8
    for i in range(NUPB):
        UPB.append(padded(f"upb{i}", flo, 1, HF, BF16)); flo += UPBw
    ACC = []
    for i in range(NACC):
        ACC.append(kb.sb(f"acc{i}", flo, NCOL)); flo += NCOL
    PTMP = []
    for i in range(2):
        PTMP.append(kb.sb(f"ptmp{i}", flo, NCOL)); flo += NCOL
    FST = []
    for i in range(2):
        FST.append(kb.sb(f"fst{i}", flo, 11 * 34)); flo += 11 * 34
    UPS = kb.sb("ups", flo, NFC * 16 * HF // 2, BF16); flo += NFC * 16 * HF // 2
    kb.top = max(m_hi, flo)
    assert kb.top <= kb.nwords, f"SBUF overflow {kb.top}"
    print("SBUF words used", kb.top, "of", kb.nwords)

    RB = 8 - NT - 2
    nslots = RB // NT
    ring = [kb.ps(f"ring{i}", i * NT * 512, NT * 512) for i in range(nslots)]
    if NT == 1:
        ring.append(kb.ps("ring_b6", 6 * 512, 512))
        nslots += 1
    ring_i = [0]
    STATP = kb.ps("statp", RB * 512, NT * 512)
    STATS = kb.ps("stats", 7 * 512, 64)

    def rslot():
        ring_i[0] += 1
        return ring[ring_i[0] % nslots]

    def sslot():
        return rslot()

    def wslot():
        wr_i[0] += 1
        return wring[wr_i[0] % NWS]

    wplan = []

    def P_mod(l):
        for blk in range(16):
            wplan.append((wada_d[l], D, [(blk * 384, 384, 0)], 384))

    def P_mix(l):
        for (c0, n) in ((256, 384), (256 + 768, 384), (256 + 384, 384), (1408, 384), (1792, 384), (0, 256)):
            wplan.append((win_d[l], D, [(c0, n, 0)], n))
        for mo in range(KC):
            wplan.append((wout_d[l], D, [(mo * 128, 128, 0)], 128))

    def P_ffn(l, modl=None):
        for j in range(NPAIR):
            wplan.append((wup_d[l], D, [(j * 128, 128, "pair")], 256))
            if modl is not None and j < 16:
                wplan.append((wada_d[modl], D, [(j * 384, 384, 0)], 384))
        for mo in range(KC):
            wplan.append((wdn_d[l], DFF, [(mo * 128, 128, 0)], 128))

    P_mod(0)
    for q_ in range(NPIECE):
        for l_ in range(DEPTH):
            P_mix(l_)
            P_ffn(l_, (l_ + 1) if (q_ == 0 and l_ + 1 < DEPTH) else None)
    wst = {"issued": 0, "taken": 0, "views": {}}
    WLIVE = 3

    def w_issue(j):
        dram2d, nrows, parts, tcols = wplan[j]
        nkc = nrows // 128
        slot = wring[j % NWS]
        view = V(slot.ap[:, 0:nkc * tcols].rearrange("p (k c) -> p k c", k=nkc), slot.bufs)
        for (c0, ncols, coff) in parts:
            if coff == "pair":
                hv = []
                for g in range(2):
                    hb_ = whalf[j % NWS][g]
                    hview = V(hb_.ap.rearrange("p (k c) -> p k c", k=nkc), hb_.bufs)
                    src = dram2d[0:nrows, g * DFF + c0:g * DFF + c0 + ncols].rearrange("(k p) c -> p k c", p=128)
                    kb.dma(pool, hview.ap, src, writes=[hview])
                    hv.append(hview)
                view = hv
                continue
            src = dram2d[0:nrows, c0:c0 + ncols].rearrange("(k p) c -> p k c", p=128)
            kb.dma(pool, view.ap[:, :, coff:coff + ncols], src, writes=[view])
        wst["views"][j] = view

    def take_w(dram2d, c0, wl=WLIVE):
        i = wst["taken"]
        assert wplan[i][0] is dram2d and wplan[i][2][0][0] == c0, (i, c0, wplan[i][2])
        lim = min(len(wplan) - 1, i + NWS - wl)
        while wst["issued"] <= lim:
            w_issue(wst["issued"])
            wst["issued"] += 1
        wst["taken"] += 1
        return wst["views"].pop(i)

    iot = V(STG.ap[:, 0:128].bitcast(mybir.dt.int32), STG.bufs)
    kb.emit(pool, lambda: nc.gpsimd.iota(iot.ap, pattern=[[1, 128]], base=0, channel_multiplier=-1), [], [iot])
    kb.CP(ident_f, iot)
    kb.TS(ident_f, ident_f, 0.0, None, ALU.is_equal)
    kb.CP(ident_b, ident_f)
    kb.MS(ones_b, 1.0)
    kb.MS(ones_f, 1.0)
    icv = V(invcnt.ap[:, 0:30].rearrange("p (c t) -> p c t", c=2), invcnt.bufs)
    for ch in range(2):
        for half in range(2):
            w = (2, 4, 8, 16)[2 * ch + half]
            ps_ = slice(64 * half, 64 * half + 64)
            kb.MS(icv[ps_, ch, :], 1.0 / w)
            for t in range(w - 1):
                kb.MS(icv[ps_, ch, t:t + 1], 1.0 / (t + 1))
    for l in range(DEPTH):
        r = 0
        while r < NPT:
            n = min(128, NPT - r)
            st = V(STG.ap[0:n, 0:128], STG.bufs)
            kb.dma(sp, st.ap, pt_d[l, r:r + n, :], writes=[st])
            sl = rslot()
            kb.MM(lambda st=st, sl=sl, n=n: nc.tensor.transpose(out=sl.ap[:, 0:n], in_=st.ap, identity=ident_f.ap[0:n, 0:n]),
                  [st, ident_f], [sl])
            kb.A(PTv[:, l, r:r + n], sl[:, 0:n], AF.Copy)
            r += n
    kb.MS(PW, 0.0)
    for l in range(DEPTH):
        for g in range(4):
            ch, half = g // 2, g % 2
            dst = PWv[64 * half:64 * half + 64, l, ch, 64 * half:64 * half + 64]
            kb.dma(pool, dst.ap, pw_d[l, g, :, :], writes=[PWv])
    cst = V(STG.ap[0:17, 0:1024], STG.bufs)
    kb.dma(sp, cst.ap, cc_d[:, :], writes=[cst])
    kb.A(cst, cst, AF.Silu)
    sl = rslot()
    kb.MM(lambda: [nc.tensor.transpose(out=sl.ap[:, k * 17:(k + 1) * 17], in_=cst.ap[:, k * 128:(k + 1) * 128],
                                       identity=ident_f.ap[0:17, 0:17]) for k in range(KC)][-1],
          [cst, ident_f], [sl])
    kb.A(CTv, sl[:, 0:KC * 17].re("p (k s) -> p k s", k=KC), AF.Copy)

    def pcol(l, name, idx):
        o = PO[name] + idx
        return PTv[:, l, o:o + 1]

    def mod_block(l, blk):
        modv = V(MODT.ap[:, 0:48 * 17].rearrange("p (c s) -> p c s", c=48), MODT.bufs)
        wv = take_w(wada_d[l], blk * 384, wl=1)
        sl = rslot()

        def fn(wv=wv, sl=sl):
            last = None
            for j in range(3):
                for k in range(KC):
                    last = nc.tensor.matmul(sl.ap[:, j * 17:(j + 1) * 17], lhsT=wv.ap[:, k, j * 128:(j + 1) * 128],
                                            rhs=CTv.ap[:, k, :], start=(k == 0), stop=(k == KC - 1))
            return last
        kb.MM(fn, [wv, CTv], [sl])
        o = PO["b_ada"] + blk * 3
        kb.TT(modv[:, blk * 3:blk * 3 + 3, :], sl[:, 0:51].re("p (c s) -> p c s", c=3),
              PTv[:, l, o:o + 3].us(2).bc([128, 3, 17]), ALU.add)

    def mod_finish(l):
        modv = V(MODT.ap[:, 0:48 * 17].rearrange("p (c s) -> p c s", c=48), MODT.bufs)
        dv = V(DVA[l].ap.rearrange("p (w c s) -> p w c s", w=6, c=KC), DVA[l].bufs)
        for sub, (gpre, gpost) in enumerate((("g_mix_pre", "g_mix_post"), ("g_ffn_pre", "g_ffn_post"))):
            m0 = sub * 24
            gp = PTv[:, l, PO[gpre]:PO[gpre] + 8].us(2).bc([128, 8, 17])
            gq = PTv[:, l, PO[gpost]:PO[gpost] + 8].us(2).bc([128, 8, 17])
            kb.TS(dv[:, 3 * sub + 0], modv[:, m0 + 8:m0 + 16, :], 1.0, 32.0, ALU.add, ALU.mult)
            kb.TT(dv[:, 3 * sub + 0], dv[:, 3 * sub + 0], gp, ALU.mult)
            kb.CP(dv[:, 3 * sub + 1], modv[:, m0:m0 + 8, :])
            kb.TS(dv[:, 3 * sub + 2], modv[:, m0 + 16:m0 + 24, :], 32.0, None, ALU.mult)
            kb.TT(dv[:, 3 * sub + 2], dv[:, 3 * sub + 2], gq, ALU.mult)

    def compute_mod(l):
        for blk in range(16):
            mod_block(l, blk)
        mod_finish(l)

    def dvv(l):
        return V(DVA[l].ap.rearrange("p (w c s) -> p w c s", w=6, c=KC), DVA[l].bufs)

    def mm_chunk(lhs_fn, nk, rhs_p, rhs_s, segs, reads, ksplit=None, reads_hi=None):
        outs = {}
        slp = rslot()
        outs["p"] = slp
        wr = [slp]
        if "s" in segs:
            sls = sslot()
            outs["s"] = sls
            wr.append(sls)

        def fn():
            last = None
            for k in range(nk):
                lh = lhs_fn(k)
                for t in range(NT):
                    last = nc.tensor.matmul(slp.ap[:, t * 512:(t + 1) * 512], lhsT=lh, rhs=rhs_p(k, t),
                                            start=(k == 0), stop=(k == nk - 1))
            if "s" in segs and not DBG_SKIP:
                for k in range(nk):
                    last = nc.tensor.matmul(sls.ap[:, 0:NS], lhsT=lhs_fn(k), rhs=rhs_s(k),
                                            start=(k == 0), stop=(k == nk - 1))
            return last
        if ksplit is not None:
            def fn_lo():
                last = None
                for k in range(ksplit):
                    for t in range(NT):
                        last = nc.tensor.matmul(slp.ap[:, t * 512:(t + 1) * 512], lhsT=lhs_fn(k), rhs=rhs_p(k, t),
                                                start=(k == 0), stop=False)
                return last

            def fn_hi():
                last = None
                for k in range(ksplit, nk):
                    for t in range(NT):
                        last = nc.tensor.matmul(slp.ap[:, t * 512:(t + 1) * 512], lhsT=lhs_fn(k), rhs=rhs_p(k, t),
                                                start=False, stop=(k == nk - 1))
                if "s" in segs:
                    for k in range(nk):
                        last = nc.tensor.matmul(sls.ap[:, 0:NS], lhsT=lhs_fn(k), rhs=rhs_s(k),
                                                start=(k == 0), stop=(k == nk - 1))
                return last
            kb.MM(fn_lo, reads, [slp])
            kb.MM(fn_hi, reads_hi, wr)
            return outs
        kb.MM(fn, reads, wr)
        return outs

    def rms_stats(src, segs, sq_from_psum=None):
        for seg in segs:
            for k in range(KC):
                q = nsq()
                n = NPC if seg == "p" else NS
                qv = q[:, 0:n]
                kb.A(qv, src[seg][k], AF.Square)
                st = STATP if seg == "p" else STATS

                def fn(qv=qv, st=st, n=n, k=k):
                    last = None
                    for t in range(max(1, n // 512)):
                        w = min(512, n)
                        last = nc.tensor.matmul(st.ap[:, t * 512:t * 512 + w], lhsT=ones_b.ap, rhs=qv.ap[:, t * 512:t * 512 + w],
                                                start=(k == 0), stop=(k == KC - 1))
                    return last
                kb.MM(fn, [qv, ones_b], [st])

    def rstd_from_stats(segs):
        for seg in segs:
            st, rs = (STATP, RSp) if seg == "p" else (STATS, RSs)
            kb.A(rs, st, AF.Sqrt, bias=float(D * EPS))
            kb.RC(rs, rs)

    def prenorm(l, sub, segs):
        dv = dvv(l)
        rms_stats(X, segs)
        rstd_from_stats(segs)
        for k in range(KC):
            t = ntmp()
            kb.TT(t[:, 0:NPC], X["p"][k], RSp, ALU.mult)
            kb.A(H["p"][k], t[:, 0:NPC], AF.Identity, bias=dv[:, 3 * sub + 1, k, 0:1], scale=dv[:, 3 * sub + 0, k, 0:1])
        if "s" in segs:
            t = ntmp()
            tv = t[:, 0:KC * NS].re("p (k n) -> p k n", k=KC) if KC * NS <= NCOL else None
            kb.TT(tv, X["sall"], RSs.us(1).bc([128, KC, NS]), ALU.mult)
            t4 = tv.re("p k (b j) -> p k b j", j=4)
            kb.TT(t4, t4, dv[:, 3 * sub + 0, :, 1:17].us(3).bc([128, KC, 16, 4]), ALU.mult)
            kb.TT(H["sall"].re("p k (b j) -> p k b j", j=4), t4,
                  dv[:, 3 * sub + 1, :, 1:17].us(3).bc([128, KC, 16, 4]), ALU.add)

    def postnorm(l, sub, SRC, segs):
        dv = dvv(l)
        rstd_from_stats(segs)
        for k in range(KC):
            t = ntmp()
            kb.TT(t[:, 0:NPC], SRC["p"][k], RSp, ALU.mult)
            kb.TT(X["p"][k], X["p"][k], t[:, 0:NPC], ALU.add)
        if "s" in segs:
            t = ntmp()
            tv = t[:, 0:KC * NS].re("p (k n) -> p k n", k=KC)
            kb.TT(tv, SRC["sall"], RSs.us(1).bc([128, KC, NS]), ALU.mult)
            t4 = tv.re("p k (b j) -> p k b j", j=4)
            kb.TT(t4, t4, dv[:, 3 * sub + 2, :, 1:17].us(3).bc([128, KC, 16, 4]), ALU.mult)
            kb.TT(X["sall"], X["sall"], tv, ALU.add)

    def out_proj(l, sub, wd, nk, rhs_src, DST, segs):
        pend = None
        for mo in range(KC):
            wv = take_w(wd, mo * 128, wl=1)
            if nk > 16 and mo == 0:
                ks = 16
                outs = mm_chunk(lambda k, wv=wv: wv.ap[:, k, :], nk,
                                lambda k, t: rhs_src["p"][k].ap[:, t * 512:(t + 1) * 512],
                                lambda k: rhs_src["s"][k].ap, segs,
                                [wv] + [rhs_src["p"][k] for k in range(ks)], ksplit=ks,
                                reads_hi=[wv] + [rhs_src["p"][k] for k in range(ks, nk)] +
                                         [rhs_src[s][k] for s in segs if s == "s" for k in range(nk)])
            else:
                outs = mm_chunk(lambda k, wv=wv: wv.ap[:, k, :], nk,
                                lambda k, t: rhs_src["p"][k].ap[:, t * 512:(t + 1) * 512],
                                lambda k: rhs_src["s"][k].ap, segs,
                                [wv] + [rhs_src[s][k] for s in segs for k in range(nk)])
            if pend is not None:
                pend()
            qs = {}
            for seg in segs:
                if seg == "p":
                    kb.A(DST[seg][mo], outs[seg], AF.Identity, scale=dvv(l)[:, 3 * sub + 2, mo, 0:1])
                else:
                    kb.A(DST[seg][mo], outs[seg][:, 0:NS], AF.Copy)
                q = nsq()
                n = NPC if seg == "p" else NS
                qs[seg] = q[:, 0:n]
                kb.A(qs[seg], outs[seg] if seg == "p" else outs[seg][:, 0:NS], AF.Square)

            def mk(qs=qs, mo=mo):
                for seg in segs:
                    st = STATP if seg == "p" else STATS
                    n = NPC if seg == "p" else NS
                    qv = qs[seg]

                    def fn(qv=qv, st=st, n=n):
                        last = None
                        for t in range(max(1, n // 512)):
                            w = min(512, n)
                            last = nc.tensor.matmul(st.ap[:, t * 512:t * 512 + w], lhsT=ones_b.ap,
                                                    rhs=qv.ap[:, t * 512:t * 512 + w], start=(mo == 0), stop=(mo == KC - 1))
                        return last
                    kb.MM(fn, [qv, ones_b], [st])
            pend = mk
        pend()
        postnorm(l, sub, DST, segs)

    def store_rows(srcs, ncol, dsts):
        n = len(srcs)
        sl = rslot()
        kb.MM(lambda: [nc.tensor.transpose(out=sl.ap[0:ncol, i * 128:(i + 1) * 128], in_=srcs[i].ap, identity=ident_f.ap)
                       for i in range(n)][-1], list(srcs) + [ident_f], [sl])
        ov = V(OST.ap[0:ncol, 0:n * 128], OST.bufs)
        kb.A(ov, sl[0:ncol, 0:n * 128], AF.Copy)
        for (r0, nr, dfn) in dsts:
            kb.dma(sp, dfn(), OST.ap[r0:r0 + nr, 0:n * 128], reads=[ov], is_out=True)

    def build_diag(l, c, bi):
        dgb = DGS[bi]
        dgv = V(dgb.ap.rearrange("p (k n) -> p k n", k=31), dgb.bufs)
        o = PO["cconv_w"] + c
        wv_ = PTv[:, l, o:o + 93:3].us(2).bc([128, 31, 128])
        kb.TT(dgv, ident_b.us(1).bc([128, 31, 128]), wv_, ALU.mult)
        return dgv

    def mix(l, q, segs):
        last = (q == NPIECE - 1)
        first = (q == 0)
        prenorm(l, 0, segs)
        dgs_built = [build_diag(l, 0, 0)]
        chk(f'mixP{q}_{l}')
        hreads = [H[s][k] for s in segs for k in range(KC)]

        first_grp = [True]

        def win_chunk(wv, c0):
            if first_grp[0]:
                first_grp[0] = False
                return mm_chunk(lambda k: wv.ap[:, k, c0:c0 + 128], KC,
                                lambda k, t: H["p"][k].ap[:, t * 512:(t + 1) * 512],
                                lambda k: H["s"][k].ap, segs, [wv] + [H["p"][k] for k in range(4)], ksplit=4,
                                reads_hi=[wv] + [H["p"][k] for k in range(4, KC)] + [H[s][k] for s in segs if s == "s" for k in range(KC)])
            return mm_chunk(lambda k: wv.ap[:, k, c0:c0 + 128], KC,
                            lambda k, t: H["p"][k].ap[:, t * 512:(t + 1) * 512],
                            lambda k: H["s"][k].ap, segs, [wv] + hreads)

        if last:
            for tI in range(2):
                st = V(STG.ap[0:120, 0:256], STG.bufs)
                kb.dma(sp, st.ap, stp_d[l, tI * 120:(tI + 1) * 120, :], writes=[st])
                sl = rslot()
                kb.MM(lambda st=st, sl=sl: [nc.tensor.transpose(out=sl.ap[:, c * 120:(c + 1) * 120], in_=st.ap[:, c * 128:(c + 1) * 128],
                                                                 identity=ident_f.ap[0:120, 0:120]) for c in range(2)][-1],
                      [st, ident_f], [sl])
                for c in range(2):
                    dstv = UPSP[c][:, tI * 8:(tI + 1) * 8, :]
                    kb.A(dstv, sl[:, c * 120:(c + 1) * 120].re("p (b r) -> p b r", r=HP), AF.Copy)
            chk(f'mixS1{q}_{l}')
            st = V(STG.ap[0:32, 0:384], STG.bufs)
            kb.dma(sp, st.ap, sts_d[l, :, :], writes=[st])
            sl = rslot()
            kb.MM(lambda: [nc.tensor.transpose(out=sl.ap[:, c * 32:(c + 1) * 32], in_=st.ap[:, c * 128:(c + 1) * 128],
                                               identity=ident_f.ap[0:32, 0:32]) for c in range(3)][-1], [st, ident_f], [sl])
            for c in range(3):
                kb.A(ZSP[c], sl[:, c * 32:(c + 1) * 32].re("p (b r) -> p b r", r=HS), AF.Copy)
            chk(f'mixS2{q}_{l}')
            for c in range(3):
                sl = rslot()
                sts_ = []
                for tI in range(4):
                    st = V(STG.ap[0:120, tI * 384:(tI + 1) * 384], STG.bufs)
                    if c == 0:
                        kb.dma(sp, st.ap, stc_d[l, tI * 120:(tI + 1) * 120, :], writes=[st])
                    sts_.append(st)
                kb.MM(lambda sl=sl, c=c, sts_=sts_: [nc.tensor.transpose(out=sl.ap[:, tI * 120:(tI + 1) * 120],
                                                                          in_=sts_[tI].ap[:, c * 128:(c + 1) * 128],
                                                                          identity=ident_f.ap[0:120, 0:120]) for tI in range(4)][-1],
                      sts_ + [ident_f], [sl])
                chk(f'mixS3{q}_{l}_{c}')
                kb.CP(GL["sf"][c][:, :, 0:HC], sl[:, 0:480].re("p (b r) -> p b r", r=HC))
                chk(f'mixS4{q}_{l}_{c}')

        chk(f'mixA{q}_{l}')
        chk(f'mixB{q}_{l}')
        whb = take_w(win_d[l], 256)
        wcg = take_w(win_d[l], 256 + 768)
        wbg = take_w(win_d[l], 256 + 384)
        for c in range(3):
            zi = c % 2
            Z, Zs = ZB["pf"][zi], ZB["sf"][zi]
            o_hb = win_chunk(whb, c * 128)
            hb = ntmp()
            kb.A(hb[:, 0:NPC], o_hb["p"], AF.Copy)
            if "s" in segs:
                kb.A(hb[:, NPC:NCOL], o_hb["s"][:, 0:NS], AF.Copy)
            o_cg = win_chunk(wcg, c * 128)
            if first:
                kb.MS(Z[:, 0:HS], 0.0)
            else:
                kb.CP(Z[:, 0:HS], V(HALO_S[l].ap[:, c * HS:(c + 1) * HS], HALO_S[l].bufs))
            kb.TT(Z[:, HS:HS + NPC], o_cg["p"], hb[:, 0:NPC], ALU.mult)
            if not last:
                kb.CP(V(HALO_S[l].ap[:, c * HS:(c + 1) * HS], HALO_S[l].bufs), Z[:, NPC:NPC + HS])
            if "s" in segs:
                kb.CP(Zs[:, :, 0:HS], ZSP[c])
                kb.TT(Zs[:, :, HS:HS + 4], o_cg["s"][:, 0:NS].re("p (b j) -> p b j", j=4),
                      hb[:, NPC:NCOL].re("p (b j) -> p b j", j=4), ALU.mult)
            ca = ntmp()
            for (zv, cav, is3) in [(Z, ca[:, 0:NPC], False)] + ([(Zs, ca[:, NPC:NCOL].re("p (b j) -> p b j", j=4), True)] if "s" in segs else []):
                n = 4 if is3 else NPC
                for kk in range(3):
                    src = zv[:, :, kk:kk + n] if is3 else zv[:, kk:kk + n]
                    wk = pcol(l, "sconv_w", kk * 3 + c)
                    if kk == 0:
                        kb.TS(cav, src, wk, None, ALU.mult)
                    else:
                        kb.STT(cav, src, wk, cav, ALU.mult, ALU.add)
            if last:
                t = TAILS
                kb.CP(t[:, c * 96:c * 96 + HS], Z[:, NPC:NPC + HS])
                kb.CP(t[:, c * 96 + HS:c * 96 + HS + 32].re("p (j b) -> p j b", j=2), Zs[:, :, HS + 2:HS + 4].re("p b j -> p j b"))
                sconv_tails.append(t[:, c * 96:c * 96 + HS + 32])
            o_bg = win_chunk(wbg, c * 128)
            kb.TT(CAT["p"][2 + c], o_bg["p"], ca[:, 0:NPC], ALU.mult)
            if "s" in segs:
                kb.TT(CAT["s"][2 + c], o_bg["s"][:, 0:NS], ca[:, NPC:NCOL], ALU.mult)
        if last:
            store_rows(sconv_tails, HS + 32,
                       [(0, HS, lambda: osp_d[l, :, :])] +
                       [(HS + 16 * j, 16, (lambda j=j: oss_d[l, :, j, :])) for j in range(2)])
            sconv_tails.clear()

        dgs_built.append(build_diag(l, 1, 1))
        chk(f'mixC{q}_{l}')
        wac = take_w(win_d[l], 1408)
        wbc = take_w(win_d[l], 1792)
        gtv = V(GT.ap[:, 0:288].rearrange("p (c n) -> p c n", c=3), GT.bufs)
        for c in range(3):
            G, Gs = GL["pf"][c], GL["sf"][c]
            o_b = win_chunk(wbc, c * 128)
            sg = ntmp()
            kb.A(sg[:, 0:NPC], o_b["p"], AF.Sigmoid)
            if "s" in segs:
                kb.A(sg[:, NPC:NCOL], o_b["s"][:, 0:NS], AF.Sigmoid)
            o_a = win_chunk(wac, c * 128)
            if first:
                kb.MS(G[:, 0:HC], 0.0)
            else:
                kb.CP(G[:, 0:HC], V(HALO_C[l].ap[:, c * HC:(c + 1) * HC], HALO_C[l].bufs))
            kb.TT(G[:, HC:HC + NPC], o_a["p"], sg[:, 0:NPC], ALU.mult)
            if not last:
                kb.CP(V(HALO_C[l].ap[:, c * HC:(c + 1) * HC], HALO_C[l].bufs), G[:, NPC:NPC + HC])
            if "s" in segs:
                kb.TT(Gs[:, :, HC:HC + 4], o_a["s"][:, 0:NS].re("p (b j) -> p b j", j=4),
                      sg[:, NPC:NCOL].re("p (b j) -> p b j", j=4), ALU.mult)
            if last:
                kb.TT(gtv[:, c, 0:HC], o_a["p"][:, NPC - HC:NPC], sg[:, NPC - HC:NPC], ALU.mult)
                kb.TT(gtv[:, c, HC:HC + NS].re("p (j b) -> p j b", j=4), o_a["s"][:, 0:NS].re("p (b j) -> p j b", j=4),
                      sg[:, NPC:NCOL].re("p (b j) -> p j b", j=4), ALU.mult)
        if last:
            store_rows([gtv[:, c, 0:HC + NS] for c in range(3)], HC + NS,
                       [(0, HC, lambda: ocp_d[l, :, :])] +
                       [(HC + 16 * j, 16, (lambda j=j: ocs_d[l, :, 26 + j, :])) for j in range(4)])
            kb.dma(sp, ocs_d[l, :, 0:26, :], stc_d[l].rearrange("(b r) c -> b r c", r=HC)[:, 4:HC, :], is_out=True)
        wv = take_w(win_d[l], 0)
        for ch in range(2):
            outs = win_chunk(wv, ch * 128)
            E = UP["pf"][0]
            Es = UP["sf"][0]
            if first:
                kb.MS(E[:, 0:HP], 0.0)
            else:
                kb.CP(E[:, 0:HP], V(HALO_P[l].ap[:, ch * HP:(ch + 1) * HP], HALO_P[l].bufs))
            kb.A(E[:, HP:HP + NPC], outs["p"], AF.Copy)
            if not last:
                kb.CP(V(HALO_P[l].ap[:, ch * HP:(ch + 1) * HP], HALO_P[l].bufs), E[:, NPC:NPC + HP])
            if "s" in segs:
                kb.CP(Es[:, :, 0:HP], UPSP[ch])
                kb.A(Es[:, :, HP:HP + 4], outs["s"][:, 0:NS].re("p (b j) -> p b j", j=4), AF.Copy)
            L = HP + NPC
            views = [(E, W1["pf"][0], W2["pf"][0], L, None)]
            if "s" in segs:
                views.append((Es, W1["sf"][0], W2["sf"][0], HP + 4, 1))
            for (e, w1, w2, Ln, is3) in views:
                def sl_(v, a, b):
                    return v[:, :, a:b] if is3 else v[:, a:b]
                kb.TT(sl_(w1, 1, Ln), sl_(e, 1, Ln), sl_(e, 0, Ln - 1), ALU.add)
                kb.TT(sl_(w2, 3, Ln), sl_(w1, 3, Ln), sl_(w1, 1, Ln - 2), ALU.add)
                if ch == 1 and not (is3 and DBG_SKIP):
                    kb.TT(sl_(w1, 7, Ln), sl_(w2, 7, Ln), sl_(w2, 3, Ln - 4), ALU.add)
                    kb.TT(sl_(w2, 15, Ln), sl_(w1, 15, Ln), sl_(w1, 7, Ln - 8), ALU.add)
                for half, wsrc in ((0, w1), (1, w2)):
                    wdw = (2, 4, 8, 16)[2 * ch + half]
                    pr = slice(64 * half, 64 * half + 64)
                    if is3:
                        o_ = PLS[ch][pr, NPC:NCOL].re("p (b j) -> p b j", j=4)
                        kb.STT(o_, wsrc[pr, :, HP:HP + 4], 1.0 / wdw, e[pr, :, HP:HP + 4], ALU.mult, ALU.subtract)
                    else:
                        kb.STT(PLS[ch][pr, 0:NPC], wsrc[pr, HP:Ln], 1.0 / wdw, e[pr, HP:Ln], ALU.mult, ALU.subtract)
                        if first:
                            t = ntmp()
                            kb.TT(t[pr, 0:HP], wsrc[pr, HP:2 * HP], icv[pr, ch, :], ALU.mult)
                            kb.TT(PLS[ch][pr, 0:HP], t[pr, 0:HP], e[pr, HP:2 * HP], ALU.subtract)
            chk(f'mixB1{q}_{l}_{ch}')
            if last:
                t = TAILS
                kb.CP(t[:, ch * 96:ch * 96 + HP], E[:, NPC:NPC + HP])
                kb.CP(t[:, ch * 96 + HP:ch * 96 + HP + NS].re("p (j b) -> p j b", j=4), Es[:, :, HP:HP + 4].re("p b j -> p j b"))
                pool_tails.append(t[:, ch * 96:ch * 96 + HP + NS])
        chk(f'mixB3{q}_{l}')
        if last:
            store_rows(pool_tails, HP + NS,
                       [(0, HP, lambda: opp_d[l, :, :])] +
                       [(HP + 16 * j, 16, (lambda j=j: ops_d[l, :, 11 + j, :])) for j in range(4)])
            pool_tails.clear()
            chk(f'mixB4{q}_{l}')
            kb.dma(sp, ops_d[l, :, 0:11, :], stp_d[l].rearrange("(b r) c -> b r c", r=HP)[:, 4:HP, :], is_out=True)

        chk(f'mixD{q}_{l}')
        units = [("p", t) for t in range(NT)] + ([("s", 0)] if "s" in segs else [])
        NVB = NPC + NS
        vbv = V(VB.ap.rearrange("p (c n) -> p c n", c=3), VB.bufs)
        vqv = V(VQ.ap.rearrange("p (c n) -> p c n", c=3), VQ.bufs)
        for c in range(3):
            dgv = dgs_built[c]
            G, Gs = GL["pf"][c], GL["sf"][c]
            for (seg, t) in units:
                n = 512 if seg == "p" else NS
                o0 = t * 512 if seg == "p" else NPC
                if seg == "p":
                    sl = rslot()
                    kb.MM(lambda sl=sl, G=G, t=t, dgv=dgv: [nc.tensor.matmul(sl.ap[:, 0:512], lhsT=dgv.ap[:, kk, :],
                                                                     rhs=G.ap[:, kk + t * 512:kk + t * 512 + 512],
                                                                     start=(kk == 0), stop=(kk == 30)) for kk in range(31)][-1],
                          [dgv, G], [sl])
                    src = sl[:, 0:512]
                else:
                    sl = sslot()
                    kb.MM(lambda sl=sl, Gs=Gs, dgv=dgv: [nc.tensor.matmul(sl.ap[:, 0:NS], lhsT=dgv.ap[:, kk, :],
                                                                  rhs=Gs.ap[:, :, kk:kk + 4],
                                                                  start=(kk == 0), stop=(kk == 30)) for kk in range(31)][-1],
                          [dgv, Gs], [sl])
                    src = sl[:, 0:NS]
                kb.A(vbv[:, c, o0:o0 + n], src, AF.Identity, bias=pcol(l, "cconv_b", c))
                kb.A(vqv[:, c, o0:o0 + n], vbv[:, c, o0:o0 + n], AF.Square)
            if c == 0:
                dgs_built.append(build_diag(l, 2, 0))
        for ch in range(2):
            chk(f'mixB2{q}_{l}_{ch}')
            o2 = mm_chunk(lambda k: PWv.ap[:, l, ch, :], 1,
                          lambda k, t: PLS[ch].ap[:, t * 512:(t + 1) * 512], lambda k: PLS[ch].ap[:, NPC:NCOL], segs, [PWv, PLS[ch]])
            for seg in segs:
                kb.A(CAT[seg][ch], o2[seg] if seg == "p" else o2[seg][:, 0:NS], AF.Identity, scale=pcol(l, "pool_scale", ch))
            chk(f'mixB5{q}_{l}_{ch}')
        for (seg, t) in units:
            n = 512 if seg == "p" else NS
            o0 = t * 512 if seg == "p" else NPC
            vbu = vbv[:, :, o0:o0 + n]
            vqu = vqv[:, :, o0:o0 + n]
            s1 = rslot()
            s2 = rslot()
            for (sv, srcv) in ((s1, vbu), (s2, vqu)):
                kb.MM(lambda sv=sv, srcv=srcv, n=n: [nc.tensor.matmul(sv.ap[:, 0:n], lhsT=ones_f.ap, rhs=srcv.ap[:, c, :],
                                                                      start=(c == 0), stop=(c == 2)) for c in range(3)][-1],
                      [srcv, ones_f], [sv])
            kb.TS(LNM[:, 0:n], s1[:, 0:n], 1.0 / DC, None, ALU.mult)
            t_ = ntmp()
            kb.TT(t_[:, 0:n], LNM[:, 0:n], LNM[:, 0:n], ALU.mult)
            kb.STT(LNR[:, 0:n], s2[:, 0:n], 1.0 / DC, t_[:, 0:n], ALU.mult, ALU.subtract)
            kb.A(LNR[:, 0:n], LNR[:, 0:n], AF.Sqrt, bias=float(EPS))
            kb.RC(LNR[:, 0:n], LNR[:, 0:n])
            for c in range(3):
                kb.TT(vbu[:, c], vbu[:, c], LNM[:, 0:n], ALU.subtract)
                kb.TT(vbu[:, c], vbu[:, c], LNR[:, 0:n], ALU.mult)
                dst = CAT["p"][5 + c][:, t * 512:t * 512 + 512] if seg == "p" else CAT["s"][5 + c]
                kb.A(dst, vbu[:, c], AF.Silu, bias=pcol(l, "cln_b", c), scale=pcol(l, "cln_g", c))

        chk(f'mixE{q}_{l}')
        out_proj(l, 0, wout_d[l], KC, CAT, MX, segs)

    def ffn(l, q, segs):
        last = (q == NPIECE - 1)
        first = (q == 0)
        prenorm(l, 1, segs)
        hreads = [H[s][k] for s in segs for k in range(KC)]
        upsv = V(UPS.ap.rearrange("p (c b r) -> p c b r", c=NFC, b=16), UPS.bufs)
        fstvs = [V(f_.ap.rearrange("p (c n) -> p c n", c=11), f_.bufs) for f_ in FST]
        hfv = V(HALO_F[l].ap.rearrange("p (c r) -> p c r", c=NFC), HALO_F[l].bufs)
        if last:
            for rd in range(4):
                st = V(STG.ap[0:32, 0:1408], STG.bufs)
                kb.dma(sp, st.ap, stf_d[l, :, rd * 1408:(rd + 1) * 1408], writes=[st])
                sl = rslot()
                kb.MM(lambda st=st, sl=sl: [nc.tensor.transpose(out=sl.ap[:, c * 32:(c + 1) * 32], in_=st.ap[:, c * 128:(c + 1) * 128],
                                                                 identity=ident_f.ap[0:32, 0:32]) for c in range(11)][-1],
                      [st, ident_f], [sl])
                kb.A(upsv[:, rd * 11:(rd + 1) * 11], sl[:, 0:352].re("p (c b r) -> p c b r", c=11, b=16), AF.Copy)
        ub_i = 0
        ac_i = 0
        pt_i = 0
        if first:
            for ub in UPB:
                kb.MS(ub["pf"][0][:, 0:HF], 0.0)
        for j in range(NPAIR):
            wv = take_w(wup_d[l], j * 128, wl=1)
            accs = []
            for gv in range(2):
                ci = j + gv * NPAIR
                if j == 0 and gv == 0:
                    outs = mm_chunk(lambda k, gv=gv: wv[gv].ap[:, k, :], KC,
                                    lambda k, t: H["p"][k].ap[:, t * 512:(t + 1) * 512],
                                    lambda k: H["s"][k].ap, segs, [wv[gv]] + [H["p"][k] for k in range(4)], ksplit=4,
                                    reads_hi=[wv[gv]] + [H["p"][k] for k in range(4, KC)] + [H[s][k] for s in segs if s == "s" for k in range(KC)])
                else:
                    outs = mm_chunk(lambda k, gv=gv: wv[gv].ap[:, k, :], KC,
                                    lambda k, t: H["p"][k].ap[:, t * 512:(t + 1) * 512],
                                    lambda k: H["s"][k].ap, segs, [wv[gv]] + hreads)
                ub = UPB[ub_i % NUPB]; ub_i += 1
                acc = ACC[ac_i % NACC]; ac_i += 1
                U, Us = ub["pf"][0], ub["sf"][0]
                if not first:
                    kb.A(U[:, 0:HF], hfv[:, ci, :], AF.Copy)
                kb.A(U[:, HF:HF + NPC], outs["p"], AF.Copy)
                kb.A(acc[:, 0:NPC], outs["p"], AF.Identity, scale=pcol(l, "ffn_conv_w", 2 * NFC + ci))
                if not last:
                    kb.A(hfv[:, ci, :], U[:, NPC:NPC + HF], AF.Copy)
                if "s" in segs:
                    o4 = outs["s"][:, 0:NS].re("p (b j) -> p b j", j=4)
                    kb.A(Us[:, :, 0:HF], upsv[:, ci], AF.Copy)
                    kb.A(Us[:, :, HF:HF + 4], o4, AF.Copy)
                    kb.A(acc[:, NPC:NCOL], outs["s"][:, 0:NS], AF.Identity, scale=pcol(l, "ffn_conv_w", 2 * NFC + ci))
                if last:
                    rc = ci % 11
                    fstv = fstvs[gv]
                    kb.A(fstv[:, rc, 0:HF], outs["p"][:, NPC - HF:NPC], AF.Copy)
                    kb.A(fstv[:, rc, HF:HF + 32].re("p (j b) -> p j b", j=2), o4[:, :, 2:4].re("p b j -> p j b"), AF.Copy)
                for kk in (1, 0):
                    wk = pcol(l, "ffn_conv_w", kk * NFC + ci)
                    if True:
                        kb.STT(acc[:, 0:NPC], U[:, kk:kk + NPC], wk, acc[:, 0:NPC], ALU.mult, ALU.add)
                        if "s" in segs:
                            a4 = acc[:, NPC:NCOL].re("p (b j) -> p b j", j=4)
                            kb.STT(a4, Us[:, :, kk:kk + 4], wk, a4, ALU.mult, ALU.add)
                    else:
                        pt_ = PTMP[pt_i % 2]; pt_i += 1
                        kb.TS(pt_[:, 0:NPC], U[:, kk:kk + NPC], wk, None, ALU.mult, eng=pool)
                        if "s" in segs:
                            kb.TS(pt_[:, NPC:NCOL].re("p (b j) -> p b j", j=4), Us[:, :, kk:kk + 4], wk, None, ALU.mult, eng=pool)
                        nn_ = NCOL if "s" in segs else NPC
                        kb.TT(acc[:, 0:nn_], acc[:, 0:nn_], pt_[:, 0:nn_], ALU.add, eng=pool)
                accs.append(acc)
                if last and ci % 11 == 10:
                    rd = ci // 11
                    for g0 in range(0, 11, 4):
                        ng = min(4, 11 - g0)
                        cs = (rd * 11 + g0) * 128
                        store_rows([fstvs[gv][:, g0 + i, 0:34] for i in range(ng)], 34,
                                   [(0, HF, (lambda cs=cs, ng=ng: ofp_d[l, :, cs:cs + ng * 128]))] +
                                   [(HF + 16 * jj, 16, (lambda jj=jj, cs=cs, ng=ng: ofs_d[l, :, jj, cs:cs + ng * 128])) for jj in range(2)])
            nn = NCOL if "s" in segs else NPC
            kb.A(accs[0][:, 0:nn], accs[0][:, 0:nn], AF.Silu)
            kb.TT(V(AA["full"][:, j, 0:nn], [AA["bp"][j], AA["bs"][j]]), accs[0][:, 0:nn], accs[1][:, 0:nn], ALU.mult)
            if q == 0 and l + 1 < DEPTH and j < 16:
                mod_block(l + 1, j)
                if j == 15:
                    mod_finish(l + 1)
        out_proj(l, 1, wdn_d[l], NPAIR, AA, FF, segs)

    UPSP = [kb.sbn(f"upsp{c}", 16 * HP) for c in range(2)]
    UPSP = [V(u.ap.rearrange("p (b r) -> p b r", r=HP), u.bufs) for u in UPSP]
    ZSP = [kb.sbn(f"zsp{c}", 16 * HS) for c in range(3)]
    ZSP = [V(u.ap.rearrange("p (b r) -> p b r", r=HS), u.bufs) for u in ZSP]
    pool_tails = []
    sconv_tails = []
    print("SBUF words used (final)", kb.top, "of", kb.nwords)

    try:
        chk('setup')
        compute_mod(0)
        chk('mod0')
        for q in range(NPIECE):
            last = (q == NPIECE - 1)
            segs = ["p", "s"] if last else ["p"]
            blocks = [(xp_d, q * NPC + tb * 128, 128, tb * 128) for tb in range(NPC // 128)]
            if last:
                blocks.append((xs_d, 0, NS, NPC))
            for bi, (src, r0, nr, c0) in enumerate(blocks):
                io = IOX[bi % 2]
                iov = V(io.ap[0:nr, :], io.bufs)
                kb.dma(sp, iov.ap, src[r0:r0 + nr, :], writes=[iov])
                sl = rslot() if NT == 2 else None
                if NT == 2:
                    kb.MM(lambda iov=iov, sl=sl, nr=nr: [nc.tensor.transpose(out=sl.ap[:, k * 128:k * 128 + nr], in_=iov.ap[:, k * 128:(k + 1) * 128],
                                                                             identity=ident_f.ap[0:nr, 0:nr]) for k in range(KC)][-1],
                          [iov, ident_f], [sl])
                    dst = V(X["full"][:, :, c0:c0 + nr], X["bp"] + X["bs"])
                    kb.A(dst, sl[:, 0:KC * 128].re("p (k n) -> p k n", k=KC)[:, :, 0:nr], AF.Copy)
                else:
                    for hk in range(2):
                        sl = rslot()
                        kb.MM(lambda iov=iov, sl=sl, nr=nr, hk=hk: [nc.tensor.transpose(out=sl.ap[:, k * 128:k * 128 + nr],
                                                                                        in_=iov.ap[:, (hk * 4 + k) * 128:(hk * 4 + k + 1) * 128],
                                                                                        identity=ident_f.ap[0:nr, 0:nr]) for k in range(4)][-1],
                              [iov, ident_f], [sl])
                        dst = V(X["full"][:, hk * 4:hk * 4 + 4, c0:c0 + nr], X["bp"] + X["bs"])
                        kb.A(dst, sl[:, 0:512].re("p (k n) -> p k n", k=4)[:, :, 0:nr], AF.Copy)
            chk(f'xload{q}')
            for l in range(DEPTH):
                mix(l, q, segs)
                chk(f'mix{q}_{l}')
                ffn(l, q, segs)
                chk(f'ffn{q}_{l}')
            for bi, (src, r0, nr, c0) in enumerate(blocks):
                io = IOX[bi % 2]
                iov = V(io.ap[0:nr, :], io.bufs)
                for hk in range(2):
                    sl = rslot()
                    xr = [X["p"][hk * 4 + k] if c0 < NPC else X["s"][hk * 4 + k] for k in range(4)]
                    kb.MM(lambda sl=sl, nr=nr, c0=c0, hk=hk: [nc.tensor.transpose(out=sl.ap[0:nr, k * 128:(k + 1) * 128],
                                                                                  in_=X["full"][:, hk * 4 + k, c0:c0 + nr],
                                                                                  identity=ident_f.ap) for k in range(4)][-1],
                          xr + [ident_f], [sl])
                    kb.A(iov[:, hk * 512:(hk + 1) * 512], sl[0:nr, 0:512], AF.Copy)
                dstd = (yp_d if src is xp_d else ys_d)[r0:r0 + nr, :]
                kb.dma(sp, dstd, iov.ap, reads=[iov], is_out=True)
    except _Stop:
        pass

    best = {}
    for (s, v) in kb.out_toks:
        if best.get(s.key, (None, 0))[1] < v:
            best[s.key] = (s, v)
    for k, (s, v) in best.items():
        nc.sync.wait_ge(s.h, v)
    build.marks = kb.marks
    return nc


_NC = None


def kernel(x_prompt, x_sample, c_prompt, c_sample, state_pool, state_sconv, state_cconv, state_ffn,
           w_ada, b_ada, g_mix_pre, g_mix_post, g_ffn_pre, g_ffn_post,
           w_in, pool_w, pool_scale, sconv_w, cconv_w, cconv_b, cln_g, cln_b,
           w_out, w_up, ffn_conv_w, w_down):
    global _NC
    f = lambda a: np.ascontiguousarray(np.asarray(a, dtype=np.float32))
    x_prompt, x_sample, c_prompt, c_sample = map(f, (x_prompt, x_sample, c_prompt, c_sample))
    state_pool, state_sconv, state_cconv, state_ffn = map(f, (state_pool, state_sconv, state_cconv, state_ffn))
    rows = []
    for l in range(DEPTH):
        parts = [f(b_ada)[l].reshape(48, 128), f(g_mix_pre)[l].reshape(8, 128), f(g_mix_post)[l].reshape(8, 128),
                 f(g_ffn_pre)[l].reshape(8, 128), f(g_ffn_post)[l].reshape(8, 128), f(pool_scale)[l].reshape(2, 128),
                 f(sconv_w)[l].reshape(9, 128), f(cconv_w)[l].reshape(93, 128), f(cconv_b)[l].reshape(3, 128),
                 f(cln_g)[l].reshape(3, 128), f(cln_b)[l].reshape(3, 128), f(ffn_conv_w)[l].reshape(132, 128)]
        rows.append(np.concatenate(parts, axis=0))
    ptab = np.ascontiguousarray(np.stack(rows, 0))
    shared = {"w_ada": f(w_ada), "ptab": ptab, "w_in": f(w_in), "pool_w": f(pool_w), "w_out": f(w_out),
              "w_up": f(w_up), "w_down": f(w_down)}
    in_maps = []
    for i in range(NCORES):
        sl = slice(16 * i, 16 * i + 16)
        m = dict(shared)
        m["xp"] = x_prompt[i]
        m["xs"] = np.ascontiguousarray(x_sample[sl].reshape(NS, D))
        m["cc"] = np.ascontiguousarray(np.concatenate([c_prompt[i:i + 1], c_sample[sl]], axis=0))
        m["st_pool"] = np.ascontiguousarray(state_pool[:, sl].reshape(DEPTH, 16 * HP, DP))
        m["st_sconv"] = np.ascontiguousarray(state_sconv[:, sl].reshape(DEPTH, 16 * HS, DS))
        m["st_cconv"] = np.ascontiguousarray(state_cconv[:, sl].reshape(DEPTH, 16 * HC, DC))
        m["st_ffn"] = np.ascontiguousarray(state_ffn[:, sl].reshape(DEPTH, 16 * HF, 2 * DFF))
        in_maps.append(m)
    if _NC is None:
        _NC = build()
    res = run_bass_kernel_spmd(_NC, in_maps, core_ids=list(range(NCORES)))
    R = res.results
    cat = lambda k, ax: np.concatenate([np.asarray(R[i][k]) for i in range(NCORES)], axis=ax)
    yp = np.stack([np.asarray(R[i]["yp"]) for i in range(NCORES)], 0)
    ys = cat("ys", 0).reshape(128, 4, D)
    outs = [yp, ys]
    for k in ("o_pool_p", "o_sconv_p", "o_cconv_p", "o_ffn_p"):
        outs.append(np.stack([np.asarray(R[i][k]) for i in range(NCORES)], 1))
    for k in ("o_pool_s", "o_sconv_s", "o_cconv_s", "o_ffn_s"):
        outs.append(cat(k, 1))
    return tuple(np.ascontiguousarray(o.astype(np.float32)) for o in outs)
```

```python
import numpy as np
import concourse.bass as bass
import concourse.mybir as mybir
from concourse.bass_utils import run_bass_kernel_spmd

F32 = mybir.dt.float32
BF16 = mybir.dt.bfloat16
AF = mybir.ActivationFunctionType
ALU = mybir.AluOpType

NCORES = 8
D = 1024
KC = 8
DEPTH = 4
SEQ = 2048
NSEQ_S = 16
NS = 64
DP, DS, DC, DFF = 256, 384, 384, 2816
DIN = 2176
NFC = 44
NPAIR = 22
EPS = 1e-6
NPC = 512
NPIECE = SEQ // NPC
NT = NPC // 512
NCOL = NPC + NS
HP, HS, HC, HF = 15, 2, 30, 2

PO = {}
_o = 0
for _n, _r in (("b_ada", 48), ("g_mix_pre", 8), ("g_mix_post", 8), ("g_ffn_pre", 8), ("g_ffn_post", 8),
               ("pool_scale", 2), ("sconv_w", 9), ("cconv_w", 93), ("cconv_b", 3), ("cln_g", 3),
               ("cln_b", 3), ("ffn_conv_w", 132)):
    PO[_n] = _o
    _o += _r
NPT = _o
DBG_SKIP = False


class _Stop(Exception):
    pass


class Buf:
    __slots__ = ("name", "space", "lo", "hi", "w", "r", "ov")

    def __init__(self, name, space, lo, hi):
        self.name, self.space, self.lo, self.hi = name, space, lo, hi
        self.w = None
        self.r = {}
        self.ov = None


class V:
    def __init__(self, ap, bufs):
        self.ap = ap
        self.bufs = list(bufs)

    def __getitem__(self, key):
        return V(self.ap[key], self.bufs)

    def re(self, s, **kw):
        return V(self.ap.rearrange(s, **kw), self.bufs)

    def bc(self, shape):
        return V(self.ap.to_broadcast(shape), self.bufs)

    def us(self, axis):
        return V(self.ap.unsqueeze(axis), self.bufs)


class Sem:
    def __init__(self, h, key):
        self.h, self.key, self.total = h, key, 0


class Eng:
    def __init__(self, nc, name, h, is_pe=False):
        self.name, self.h, self.is_pe = name, h, is_pe
        self.sem = Sem(nc.alloc_semaphore("s_" + name), "s_" + name)
        self.count = 0
        self.seen = {}
        self.dsems = []
        self.di = 0


class KB:
    def __init__(self):
        nc = bass.Bass("TRN2", target_bir_lowering=False)
        self.nc = nc
        self.pe = Eng(nc, "pe", nc.tensor, True)
        self.act = Eng(nc, "act", nc.scalar)
        self.dve = Eng(nc, "dve", nc.vector)
        self.pool = Eng(nc, "pool", nc.gpsimd)
        self.sp = Eng(nc, "sp", nc.sync)
        for e, n in ((self.sp, 14), (self.pool, 12)):
            for i in range(n):
                e.dsems.append(Sem(nc.alloc_semaphore(f"d_{e.name}{i}"), f"d_{e.name}{i}"))
        self.bufs = {"sb": [], "ps": []}
        self.nwords = 53100
        self.arena = nc.alloc_sbuf_tensor("arena", [128, self.nwords], F32)
        self.psum = nc.alloc_psum_tensor("psum", [128, 4096], F32)
        self.top = 0
        self.out_toks = []
        self.opn = 0
        self.stop_n = 0
        self.marks = {}

    def mkbuf(self, name, space, lo, hi):
        b = Buf(name, space, lo, hi)
        lst = self.bufs[space]
        b.ov = [b]
        for o in lst:
            if o.lo < hi and lo < o.hi:
                o.ov.append(b)
                b.ov.append(o)
        lst.append(b)
        return b

    def alloc(self, words):
        lo = self.top
        self.top += words
        assert self.top <= self.nwords, f"SBUF arena overflow {self.top}"
        return lo

    def sb(self, name, lo, words, dt=F32, shape=None):
        b = self.mkbuf(name, "sb", lo, lo + words)
        ap = self.arena[:, lo:lo + words]
        if dt != F32:
            ap = ap.bitcast(dt)
        return V(ap, [b])

    def sbn(self, name, words, dt=F32):
        return self.sb(name, self.alloc(words), words, dt)

    def ps(self, name, lo, n):
        b = self.mkbuf(name, "ps", lo, lo + n)
        return V(self.psum[:, lo:lo + n], [b])

    def _deps(self, eng, reads, writes):
        toks = {}
        for b in reads:
            for ob in b.ov:
                if ob.w is not None:
                    s, v = ob.w
                    if toks.get(s.key, (None, 0))[1] < v:
                        toks[s.key] = (s, v)
        for b in writes:
            for ob in b.ov:
                if ob.w is not None:
                    s, v = ob.w
                    if toks.get(s.key, (None, 0))[1] < v:
                        toks[s.key] = (s, v)
                for k, (s, v) in ob.r.items():
                    if toks.get(k, (None, 0))[1] < v:
                        toks[k] = (s, v)
        for k, (s, v) in toks.items():
            if eng.is_pe and s is eng.sem:
                continue
            if eng.seen.get(k, 0) >= v:
                continue
            eng.h.wait_ge(s.h, v)
            eng.seen[k] = v

    def _commit(self, tok, reads, writes):
        s, v = tok
        ws = set(id(b) for b in writes)
        for b in writes:
            b.w = tok
            b.r = {}
        for b in reads:
            if id(b) in ws:
                continue
            if b.r.get(s.key, (None, 0))[1] < v:
                b.r[s.key] = (s, v)

    def emit(self, eng, fn, reads, writes):
        self.opn += 1
        if self.stop_n and self.opn == self.stop_n:
            raise _Stop()
        rb = [b for v in reads for b in v.bufs]
        wb = [b for v in writes for b in v.bufs]
        self._deps(eng, rb, wb)
        ins = fn()
        eng.count += 1
        ins.then_inc(eng.sem.h, 1)
        tok = (eng.sem, eng.count)
        self._commit(tok, rb, wb)
        return tok

    def dma(self, eng, out, in_, reads=(), writes=(), is_out=False):
        self.opn += 1
        if self.stop_n and self.opn == self.stop_n:
            raise _Stop()
        rb = [b for v in reads for b in v.bufs]
        wb = [b for v in writes for b in v.bufs]
        s = eng.dsems[eng.di % len(eng.dsems)]
        eng.di += 1
        if s.total > 0 and eng.seen.get(s.key, 0) < s.total:
            eng.h.wait_ge(s.h, s.total)
            eng.seen[s.key] = s.total
        self._deps(eng, rb, wb)
        ins = eng.h.dma_start(out=out, in_=in_)
        ins.then_inc(s.h, 16)
        s.total += 16
        tok = (s, s.total)
        self._commit(tok, rb, wb)
        if is_out:
            self.out_toks.append(tok)
        return tok

    def A(self, out, in_, func, bias=None, scale=None, eng=None):
        rd = [in_] + [x for x in (bias, scale) if isinstance(x, V)]
        kw = {}
        if bias is not None:
            kw["bias"] = bias.ap if isinstance(bias, V) else bias
        if scale is not None:
            kw["scale"] = scale.ap if isinstance(scale, V) else scale
        return self.emit(self.act, lambda: self.nc.scalar.activation(out=out.ap, in_=in_.ap, func=func, **kw),
                         rd, [out])

    def TS(self, out, in0, s1, s2, op0, op1=None, eng=None):
        eng = eng or self.dve
        rd = [in0] + [x for x in (s1, s2) if isinstance(x, V)]
        a1 = s1.ap if isinstance(s1, V) else s1
        a2 = s2.ap if isinstance(s2, V) else s2
        kw = {}
        if op1 is not None:
            kw["op1"] = op1
        return self.emit(eng, lambda: eng.h.tensor_scalar(out=out.ap, in0=in0.ap, scalar1=a1, scalar2=a2,
                                                          op0=op0, **kw), rd, [out])

    def TT(self, out, in0, in1, op, eng=None):
        eng = eng or self.dve
        return self.emit(eng, lambda: eng.h.tensor_tensor(out=out.ap, in0=in0.ap, in1=in1.ap, op=op),
                         [in0, in1], [out])

    def STT(self, out, in0, sc, in1, op0, op1, eng=None):
        eng = eng or self.dve
        rd = [in0, in1] + ([sc] if isinstance(sc, V) else [])
        a = sc.ap if isinstance(sc, V) else sc
        return self.emit(eng, lambda: eng.h.scalar_tensor_tensor(out=out.ap, in0=in0.ap, scalar=a, in1=in1.ap,
                                                                 op0=op0, op1=op1), rd, [out])

    def RC(self, out, in_):
        return self.emit(self.dve, lambda: self.nc.vector.reciprocal(out=out.ap, in_=in_.ap), [in_], [out])

    def CP(self, out, in_, eng=None):
        eng = eng or self.dve
        return self.emit(eng, lambda: eng.h.tensor_copy(out=out.ap, in_=in_.ap), [in_], [out])

    def MS(self, out, val, eng=None):
        eng = eng or self.dve
        return self.emit(eng, lambda: eng.h.memset(out.ap, val), [], [out])

    def MM(self, fn, reads, writes):
        return self.emit(self.pe, fn, reads, writes)


def build(stop=None):
    kb = KB()

    def chk(tag):
        kb.marks[tag] = kb.opn
        if stop is not None and tag == stop:
            raise _Stop()
    nc = kb.nc
    pe, act, dve, pool, sp = kb.pe, kb.act, kb.dve, kb.pool, kb.sp

    def din(name, shape):
        return nc.dram_tensor(name, list(shape), F32, kind="ExternalInput").ap()

    def dout(name, shape):
        return nc.dram_tensor(name, list(shape), F32, kind="ExternalOutput").ap()

    xp_d = din("xp", [SEQ, D]); xs_d = din("xs", [NS, D]); cc_d = din("cc", [17, D])
    stp_d = din("st_pool", [DEPTH, 16 * HP, DP]); sts_d = din("st_sconv", [DEPTH, 16 * HS, DS])
    stc_d = din("st_cconv", [DEPTH, 16 * HC, DC]); stf_d = din("st_ffn", [DEPTH, 16 * HF, 2 * DFF])
    wada_d = din("w_ada", [DEPTH, D, 6 * D]); pt_d = din("ptab", [DEPTH, NPT, 128])
    win_d = din("w_in", [DEPTH, D, DIN]); pw_d = din("pool_w", [DEPTH, 4, 64, 64])
    wout_d = din("w_out", [DEPTH, D, D]); wup_d = din("w_up", [DEPTH, D, 2 * DFF])
    wdn_d = din("w_down", [DEPTH, DFF, D])
    yp_d = dout("yp", [SEQ, D]); ys_d = dout("ys", [NS, D])
    opp_d = dout("o_pool_p", [DEPTH, HP, DP]); osp_d = dout("o_sconv_p", [DEPTH, HS, DS])
    ocp_d = dout("o_cconv_p", [DEPTH, HC, DC]); ofp_d = dout("o_ffn_p", [DEPTH, HF, 2 * DFF])
    ops_d = dout("o_pool_s", [DEPTH, 16, HP, DP]); oss_d = dout("o_sconv_s", [DEPTH, 16, HS, DS])
    ocs_d = dout("o_cconv_s", [DEPTH, 16, HC, DC]); ofs_d = dout("o_ffn_s", [DEPTH, 16, HF, 2 * DFF])

    wada_d = [wada_d[l] for l in range(DEPTH)]; win_d = [win_d[l] for l in range(DEPTH)]
    wout_d = [wout_d[l] for l in range(DEPTH)]; wup_d = [wup_d[l] for l in range(DEPTH)]; wdn_d = [wdn_d[l] for l in range(DEPTH)]

    W_X = KC * NCOL
    x_lo = kb.alloc(W_X)
    hc_lo = kb.alloc(W_X)
    a_lo = kb.alloc(NPAIR * NCOL // 2)
    A_WORDS = NPAIR * NCOL // 2

    def chunked(name, lo, nch, dt):
        wpc = NCOL if dt == F32 else NCOL // 2
        full = kb.arena[:, lo:lo + nch * wpc]
        if dt != F32:
            full = full.bitcast(dt)
        full = full.rearrange("p (c n) -> p c n", c=nch)
        res = {"full": full, "p": [], "s": [], "bp": [], "bs": []}
        pw = NPC if dt == F32 else NPC // 2
        for c in range(nch):
            bp = kb.mkbuf(f"{name}{c}p", "sb", lo + c * wpc, lo + c * wpc + pw)
            bs = kb.mkbuf(f"{name}{c}s", "sb", lo + c * wpc + pw, lo + (c + 1) * wpc)
            res["p"].append(V(full[:, c, 0:NPC], [bp]))
            res["s"].append(V(full[:, c, NPC:NCOL], [bs]))
            res["bp"].append(bp); res["bs"].append(bs)
        res["sall"] = V(full[:, :, NPC:NCOL], res["bs"])
        return res

    X = chunked("x", x_lo, KC, F32)
    H = chunked("h", hc_lo, KC, BF16)
    CAT = chunked("cat", hc_lo + W_X // 2, KC, BF16)
    FF = chunked("ff", hc_lo, KC, F32)
    MX = chunked("mx", a_lo, KC, F32)
    AA = chunked("a", a_lo, NPAIR, BF16)

    WSLOT = 1536
    NWS = 5
    wring = [kb.sbn(f"wr{i}", WSLOT, BF16) for i in range(NWS)]
    wr_i = [0]
    whalf = [[kb.sb(f"wrh{i}_{g}", w_.bufs[0].lo + g * 512, 512, BF16) for g in range(2)] for i, w_ in enumerate(wring)]

    PT = kb.sbn("pt", DEPTH * NPT)
    PTv = V(PT.ap.rearrange("p (l n) -> p l n", l=DEPTH), PT.bufs)
    DVA = [kb.sbn(f"dv{l}", 6 * KC * 17) for l in range(DEPTH)]
    ident_f = kb.sbn("ident_f", 128); ident_b = kb.sbn("ident_b", 64, BF16)
    ones_b = kb.sbn("ones_b", 64, BF16); ones_f = kb.sbn("ones_f", 128)
    invcnt = kb.sbn("invcnt", 32)
    PW = kb.sbn("pw", DEPTH * 2 * 64, BF16)
    PWv = V(PW.ap.rearrange("p (l c n) -> p l c n", l=DEPTH, c=2), PW.bufs)
    CTt = kb.sbn("ct", 80, BF16)
    CTv = V(CTt.ap[:, 0:KC * 17].rearrange("p (k s) -> p k s", k=KC), CTt.bufs)
    RSp = kb.sbn("rsp", NPC); RSs = kb.sbn("rss", NS)
    SQ = [kb.sbn(f"sq{i}", NCOL // 2, BF16) for i in range(3)]
    TMP = [kb.sbn(f"tmp{i}", NCOL) for i in range(4)]
    sq_i = [0]; tmp_i = [0]
    TB = [kb.sb(f"tb{i}", TMP[2 * i].bufs[0].lo, 2 * NPC) for i in range(2)]
    tb_i = [0]

    def ntb():
        tb_i[0] += 1
        t_ = TB[tb_i[0] % 2]
        return V(t_.ap.rearrange("p (c n) -> p c n", c=2), t_.bufs)

    def nsq():
        sq_i[0] += 1
        return SQ[sq_i[0] % len(SQ)]

    def ntmp():
        tmp_i[0] += 1
        return TMP[tmp_i[0] % len(TMP)]

    HALO_P = [kb.sbn(f"hp{l}", 2 * HP) for l in range(DEPTH)]
    HALO_S = [kb.sbn(f"hs{l}", 3 * HS) for l in range(DEPTH)]
    HALO_C = [kb.sbn(f"hcv{l}", 3 * HC // 2, BF16) for l in range(DEPTH)]
    HALO_F = [kb.sbn(f"hf{l}", NFC * HF // 2, BF16) for l in range(DEPTH)]

    def padded(name, lo, nch, Hh, dt):
        cols = Hh + NPC + 16 * (Hh + 4)
        colsw = cols if dt == F32 else (cols + 1) // 2
        res = {"cols": cols, "H": Hh, "pf": [], "sf": [], "words": nch * colsw}
        for c in range(nch):
            l0 = lo + c * colsw
            ap = kb.arena[:, l0:l0 + colsw]
            if dt != F32:
                ap = ap.bitcast(dt)
            pcw = (Hh + NPC) if dt == F32 else (Hh + NPC) // 2
            bp = kb.mkbuf(f"{name}{c}p", "sb", l0, l0 + pcw)
            bs = kb.mkbuf(f"{name}{c}s", "sb", l0 + pcw, l0 + colsw)
            res["pf"].append(V(ap[:, 0:Hh + NPC], [bp]))
            res["sf"].append(V(ap[:, Hh + NPC:Hh + NPC + 16 * (Hh + 4)].rearrange("p (b j) -> p b j", j=Hh + 4), [bs]))
        return res

    MODT = kb.sbn("modt", 48 * 17 + 16)
    STG = kb.sbn("stg", 1536)
    OST = kb.sbn("ost", 1536)
    IOX = [kb.sb(f"iox{i}", a_lo + i * 1024, 1024) for i in range(2)]
    m_lo = kb.top
    mlo = m_lo
    UP = padded("up", mlo, 1, HP, F32); mlo += UP["words"]
    W1 = padded("w1", mlo, 1, HP, F32); mlo += W1["words"]
    W2 = padded("w2", mlo, 1, HP, F32); mlo += W2["words"]
    PLS = []
    for i in range(2):
        PLS.append(kb.sb(f"pl{i}", mlo, NCOL // 2, BF16)); mlo += NCOL // 2
    ZB = padded("zb", mlo, 2, HS, F32); mlo += ZB["words"]
    GL = padded("gl", mlo, 3, HC, BF16); mlo += GL["words"]
    VB = kb.sb("vb", mlo, 3 * NCOL); mlo += 3 * NCOL
    VQ = kb.sb("vq", mlo, 3 * NCOL); mlo += 3 * NCOL
    LNM = kb.sb("lnm", mlo, 512); mlo += 512
    LNR = kb.sb("lnr", mlo, 512); mlo += 512
    GT = kb.sb("gt", mlo, 3 * 96); mlo += 3 * 96
    TAILS = kb.sb("tails", mlo, 3 * 96); mlo += 3 * 96
    DGS = []
    for i in range(2):
        DGS.append(kb.sb(f"dg{i}", mlo, 31 * 64, BF16)); mlo += 31 * 64
    m_hi = mlo
    flo = m_lo
    UPBw = (HF + NPC + 16 * (HF + 4)) // 2
    UPB = []
    NUPB, NACC = 6, 8
    for i in range(NUPB):
        UPB.append(padded(f"upb{i}", flo, 1, HF, BF16)); flo += UPBw
    ACC = []
    for i in range(NACC):
        ACC.append(kb.sb(f"acc{i}", flo, NCOL)); flo += NCOL
    PTMP = []
    for i in range(2):
        PTMP.append(kb.sb(f"ptmp{i}", flo, NCOL)); flo += NCOL
    FST = []
    for i in range(2):
        FST.append(kb.sb(f"fst{i}", flo, 11 * 34)); flo += 11 * 34
    UPS = kb.sb("ups", flo, NFC * 16 * HF // 2, BF16); flo += NFC * 16 * HF // 2
    kb.top = max(m_hi, flo)
    assert kb.top <= kb.nwords, f"SBUF overflow {kb.top}"
    print("SBUF words used", kb.top, "of", kb.nwords)

    RB = 8 - NT - 2
    nslots = RB // NT
    ring = [kb.ps(f"ring{i}", i * NT * 512, NT * 512) for i in range(nslots)]
    if NT == 1:
        ring.append(kb.ps("ring_b6", 6 * 512, 512))
        nslots += 1
    ring_i = [0]
    STATP = kb.ps("statp", RB * 512, NT * 512)
    STATS = kb.ps("stats", 7 * 512, 64)

    def rslot():
        ring_i[0] += 1
        return ring[ring_i[0] % nslots]

    def sslot():
        return rslot()

    def wslot():
        wr_i[0] += 1
        return wring[wr_i[0] % NWS]

    wplan = []

    def P_mod(l):
        for blk in range(16):
            wplan.append((wada_d[l], D, [(blk * 384, 384, 0)], 384))

    def P_mix(l):
        for (c0, n) in ((256, 384), (256 + 768, 384), (256 + 384, 384), (1408, 384), (1792, 384), (0, 256)):
            wplan.append((win_d[l], D, [(c0, n, 0)], n))
        for mo in range(KC):
            wplan.append((wout_d[l], D, [(mo * 128, 128, 0)], 128))

    def P_ffn(l, modl=None):
        for j in range(NPAIR):
            wplan.append((wup_d[l], D, [(j * 128, 128, "pair")], 256))
            if modl is not None and j < 16:
                wplan.append((wada_d[modl], D, [(j * 384, 384, 0)], 384))
        for mo in range(KC):
            wplan.append((wdn_d[l], DFF, [(mo * 128, 128, 0)], 128))

    P_mod(0)
    for q_ in range(NPIECE):
        for l_ in range(DEPTH):
            P_mix(l_)
            P_ffn(l_, (l_ + 1) if (q_ == 0 and l_ + 1 < DEPTH) else None)
    wst = {"issued": 0, "taken": 0, "views": {}}
    WLIVE = 3

    def w_issue(j):
        dram2d, nrows, parts, tcols = wplan[j]
        nkc = nrows // 128
        slot = wring[j % NWS]
        view = V(slot.ap[:, 0:nkc * tcols].rearrange("p (k c) -> p k c", k=nkc), slot.bufs)
        for (c0, ncols, coff) in parts:
            if coff == "pair":
                hv = []
                for g in range(2):
                    hb_ = whalf[j % NWS][g]
                    hview = V(hb_.ap.rearrange("p (k c) -> p k c", k=nkc), hb_.bufs)
                    src = dram2d[0:nrows, g * DFF + c0:g * DFF + c0 + ncols].rearrange("(k p) c -> p k c", p=128)
                    kb.dma(pool, hview.ap, src, writes=[hview])
                    hv.append(hview)
                view = hv
                continue
            src = dram2d[0:nrows, c0:c0 + ncols].rearrange("(k p) c -> p k c", p=128)
            kb.dma(pool, view.ap[:, :, coff:coff + ncols], src, writes=[view])
        wst["views"][j] = view

    def take_w(dram2d, c0, wl=WLIVE):
        i = wst["taken"]
        assert wplan[i][0] is dram2d and wplan[i][2][0][0] == c0, (i, c0, wplan[i][2])
        lim = min(len(wplan) - 1, i + NWS - wl)
        while wst["issued"] <= lim:
            w_issue(wst["issued"])
            wst["issued"] += 1
        wst["taken"] += 1
        return wst["views"].pop(i)

    iot = V(STG.ap[:, 0:128].bitcast(mybir.dt.int32), STG.bufs)
    kb.emit(pool, lambda: nc.gpsimd.iota(iot.ap, pattern=[[1, 128]], base=0, channel_multiplier=-1), [], [iot])
    kb.CP(ident_f, iot)
    kb.TS(ident_f, ident_f, 0.0, None, ALU.is_equal)
    kb.CP(ident_b, ident_f)
    kb.MS(ones_b, 1.0)
    kb.MS(ones_f, 1.0)
    icv = V(invcnt.ap[:, 0:30].rearrange("p (c t) -> p c t", c=2), invcnt.bufs)
    for ch in range(2):
        for half in range(2):
            w = (2, 4, 8, 16)[2 * ch + half]
            ps_ = slice(64 * half, 64 * half + 64)
            kb.MS(icv[ps_, ch, :], 1.0 / w)
            for t in range(w - 1):
                kb.MS(icv[ps_, ch, t:t + 1], 1.0 / (t + 1))
    for l in range(DEPTH):
        r = 0
        while r < NPT:
            n = min(128, NPT - r)
            st = V(STG.ap[0:n, 0:128], STG.bufs)
            kb.dma(sp, st.ap, pt_d[l, r:r + n, :], writes=[st])
            sl = rslot()
            kb.MM(lambda st=st, sl=sl, n=n: nc.tensor.transpose(out=sl.ap[:, 0:n], in_=st.ap, identity=ident_f.ap[0:n, 0:n]),
                  [st, ident_f], [sl])
            kb.A(PTv[:, l, r:r + n], sl[:, 0:n], AF.Copy)
            r += n
    kb.MS(PW, 0.0)
    for l in range(DEPTH):
        for g in range(4):
            ch, half = g // 2, g % 2
            dst = PWv[64 * half:64 * half + 64, l, ch, 64 * half:64 * half + 64]
            kb.dma(pool, dst.ap, pw_d[l, g, :, :], writes=[PWv])
    cst = V(STG.ap[0:17, 0:1024], STG.bufs)
    kb.dma(sp, cst.ap, cc_d[:, :], writes=[cst])
    kb.A(cst, cst, AF.Silu)
    sl = rslot()
    kb.MM(lambda: [nc.tensor.transpose(out=sl.ap[:, k * 17:(k + 1) * 17], in_=cst.ap[:, k * 128:(k + 1) * 128],
                                       identity=ident_f.ap[0:17, 0:17]) for k in range(KC)][-1],
          [cst, ident_f], [sl])
    kb.A(CTv, sl[:, 0:KC * 17].re("p (k s) -> p k s", k=KC), AF.Copy)

    def pcol(l, name, idx):
        o = PO[name] + idx
        return PTv[:, l, o:o + 1]

    def mod_block(l, blk):
        modv = V(MODT.ap[:, 0:48 * 17].rearrange("p (c s) -> p c s", c=48), MODT.bufs)
        wv = take_w(wada_d[l], blk * 384, wl=1)
        sl = rslot()

        def fn(wv=wv, sl=sl):
            last = None
            for j in range(3):
                for k in range(KC):
                    last = nc.tensor.matmul(sl.ap[:, j * 17:(j + 1) * 17], lhsT=wv.ap[:, k, j * 128:(j + 1) * 128],
                                            rhs=CTv.ap[:, k, :], start=(k == 0), stop=(k == KC - 1))
            return last
        kb.MM(fn, [wv, CTv], [sl])
        o = PO["b_ada"] + blk * 3
        kb.TT(modv[:, blk * 3:blk * 3 + 3, :], sl[:, 0:51].re("p (c s) -> p c s", c=3),
              PTv[:, l, o:o + 3].us(2).bc([128, 3, 17]), ALU.add)

    def mod_finish(l):
        modv = V(MODT.ap[:, 0:48 * 17].rearrange("p (c s) -> p c s", c=48), MODT.bufs)
        dv = V(DVA[l].ap.rearrange("p (w c s) -> p w c s", w=6, c=KC), DVA[l].bufs)
        for sub, (gpre, gpost) in enumerate((("g_mix_pre", "g_mix_post"), ("g_ffn_pre", "g_ffn_post"))):
            m0 = sub * 24
            gp = PTv[:, l, PO[gpre]:PO[gpre] + 8].us(2).bc([128, 8, 17])
            gq = PTv[:, l, PO[gpost]:PO[gpost] + 8].us(2).bc([128, 8, 17])
            kb.TS(dv[:, 3 * sub + 0], modv[:, m0 + 8:m0 + 16, :], 1.0, 32.0, ALU.add, ALU.mult)
            kb.TT(dv[:, 3 * sub + 0], dv[:, 3 * sub + 0], gp, ALU.mult)
            kb.CP(dv[:, 3 * sub + 1], modv[:, m0:m0 + 8, :])
            kb.TS(dv[:, 3 * sub + 2], modv[:, m0 + 16:m0 + 24, :], 32.0, None, ALU.mult)
            kb.TT(dv[:, 3 * sub + 2], dv[:, 3 * sub + 2], gq, ALU.mult)

    def compute_mod(l):
        for blk in range(16):
            mod_block(l, blk)
        mod_finish(l)

    def dvv(l):
        return V(DVA[l].ap.rearrange("p (w c s) -> p w c s", w=6, c=KC), DVA[l].bufs)

    def mm_chunk(lhs_fn, nk, rhs_p, rhs_s, segs, reads, ksplit=None, reads_hi=None):
        outs = {}
        slp = rslot()
        outs["p"] = slp
        wr = [slp]
        if "s" in segs:
            sls = sslot()
            outs["s"] = sls
            wr.append(sls)

        def fn():
            last = None
            for k in range(nk):
                lh = lhs_fn(k)
                for t in range(NT):
                    last = nc.tensor.matmul(slp.ap[:, t * 512:(t + 1) * 512], lhsT=lh, rhs=rhs_p(k, t),
                                            start=(k == 0), stop=(k == nk - 1))
            if "s" in segs and not DBG_SKIP:
                for k in range(nk):
                    last = nc.tensor.matmul(sls.ap[:, 0:NS], lhsT=lhs_fn(k), rhs=rhs_s(k),
                                            start=(k == 0), stop=(k == nk - 1))
            return last
        if ksplit is not None:
            def fn_lo():
                last = None
                for k in range(ksplit):
                    for t in range(NT):
                        last = nc.tensor.matmul(slp.ap[:, t * 512:(t + 1) * 512], lhsT=lhs_fn(k), rhs=rhs_p(k, t),
                                                start=(k == 0), stop=False)
                return last

            def fn_hi():
                last = None
                for k in range(ksplit, nk):
                    for t in range(NT):
                        last = nc.tensor.matmul(slp.ap[:, t * 512:(t + 1) * 512], lhsT=lhs_fn(k), rhs=rhs_p(k, t),
                                                start=False, stop=(k == nk - 1))
                if "s" in segs:
                    for k in range(nk):
                        last = nc.tensor.matmul(sls.ap[:, 0:NS], lhsT=lhs_fn(k), rhs=rhs_s(k),
                                                start=(k == 0), stop=(k == nk - 1))
                return last
            kb.MM(fn_lo, reads, [slp])
            kb.MM(fn_hi, reads_hi, wr)
            return outs
        kb.MM(fn, reads, wr)
        return outs

    def rms_stats(src, segs, sq_from_psum=None):
        for seg in segs:
            for k in range(KC):
                q = nsq()
                n = NPC if seg == "p" else NS
                qv = q[:, 0:n]
                kb.A(qv, src[seg][k], AF.Square)
                st = STATP if seg == "p" else STATS

                def fn(qv=qv, st=st, n=n, k=k):
                    last = None
                    for t in range(max(1, n // 512)):
                        w = min(512, n)
                        last = nc.tensor.matmul(st.ap[:, t * 512:t * 512 + w], lhsT=ones_b.ap, rhs=qv.ap[:, t * 512:t * 512 + w],
                                                start=(k == 0), stop=(k == KC - 1))
                    return last
                kb.MM(fn, [qv, ones_b], [st])

    def rstd_from_stats(segs):
        for seg in segs:
            st, rs = (STATP, RSp) if seg == "p" else (STATS, RSs)
            kb.A(rs, st, AF.Sqrt, bias=float(D * EPS))
            kb.RC(rs, rs)

    def prenorm(l, sub, segs):
        dv = dvv(l)
        rms_stats(X, segs)
        rstd_from_stats(segs)
        rbc = RSp.us(1).bc([128, 2, NPC])
        for k in range(0, KC, 2):
            t = ntb()
            xpair = V(X["full"][:, k:k + 2, 0:NPC], [X["bp"][k], X["bp"][k + 1]])
            kb.TT(t, xpair, rbc, ALU.mult)
            for i in range(2):
                kb.A(H["p"][k + i], t[:, i, :], AF.Identity, bias=dv[:, 3 * sub + 1, k + i, 0:1], scale=dv[:, 3 * sub + 0, k + i, 0:1])
        if "s" in segs:
            t = ntmp()
            tv = t[:, 0:KC * NS].re("p (k n) -> p k n", k=KC) if KC * NS <= NCOL else None
            kb.TT(tv, X["sall"], RSs.us(1).bc([128, KC, NS]), ALU.mult)
            t4 = tv.re("p k (b j) -> p k b j", j=4)
            kb.TT(t4, t4, dv[:, 3 * sub + 0, :, 1:17].us(3).bc([128, KC, 16, 4]), ALU.mult)
            kb.TT(H["sall"].re("p k (b j) -> p k b j", j=4), t4,
                  dv[:, 3 * sub + 1, :, 1:17].us(3).bc([128, KC, 16, 4]), ALU.add)

    def postnorm(l, sub, SRC, segs):
        dv = dvv(l)
        rstd_from_stats(segs)
        rbc = RSp.us(1).bc([128, 2, NPC])
        for k in range(0, KC, 2):
            t = ntb()
            spair = V(SRC["full"][:, k:k + 2, 0:NPC], [SRC["bp"][k], SRC["bp"][k + 1]])
            xpair = V(X["full"][:, k:k + 2, 0:NPC], [X["bp"][k], X["bp"][k + 1]])
            kb.TT(t, spair, rbc, ALU.mult)
            kb.TT(xpair, xpair, t, ALU.add)
        if "s" in segs:
            t = ntmp()
            tv = t[:, 0:KC * NS].re("p (k n) -> p k n", k=KC)
            kb.TT(tv, SRC["sall"], RSs.us(1).bc([128, KC, NS]), ALU.mult)
            t4 = tv.re("p k (b j) -> p k b j", j=4)
            kb.TT(t4, t4, dv[:, 3 * sub + 2, :, 1:17].us(3).bc([128, KC, 16, 4]), ALU.mult)
            kb.TT(X["sall"], X["sall"], tv, ALU.add)

    def out_proj(l, sub, wd, nk, rhs_src, DST, segs):
        pend = None
        for mo in range(KC):
            wv = take_w(wd, mo * 128, wl=1)
            if nk > 16 and mo == 0:
                ks = 16
                outs = mm_chunk(lambda k, wv=wv: wv.ap[:, k, :], nk,
                                lambda k, t: rhs_src["p"][k].ap[:, t * 512:(t + 1) * 512],
                                lambda k: rhs_src["s"][k].ap, segs,
                                [wv] + [rhs_src["p"][k] for k in range(ks)], ksplit=ks,
                                reads_hi=[wv] + [rhs_src["p"][k] for k in range(ks, nk)] +
                                         [rhs_src[s][k] for s in segs if s == "s" for k in range(nk)])
            else:
                outs = mm_chunk(lambda k, wv=wv: wv.ap[:, k, :], nk,
                                lambda k, t: rhs_src["p"][k].ap[:, t * 512:(t + 1) * 512],
                                lambda k: rhs_src["s"][k].ap, segs,
                                [wv] + [rhs_src[s][k] for s in segs for k in range(nk)])
            if pend is not None:
                pend()
            qs = {}
            for seg in segs:
                if seg == "p":
                    kb.A(DST[seg][mo], outs[seg], AF.Identity, scale=dvv(l)[:, 3 * sub + 2, mo, 0:1])
                else:
                    kb.A(DST[seg][mo], outs[seg][:, 0:NS], AF.Copy)
                q = nsq()
                n = NPC if seg == "p" else NS
                qs[seg] = q[:, 0:n]
                kb.A(qs[seg], outs[seg] if seg == "p" else outs[seg][:, 0:NS], AF.Square)

            def mk(qs=qs, mo=mo):
                for seg in segs:
                    st = STATP if seg == "p" else STATS
                    n = NPC if seg == "p" else NS
                    qv = qs[seg]

                    def fn(qv=qv, st=st, n=n):
                        last = None
                        for t in range(max(1, n // 512)):
                            w = min(512, n)
                            last = nc.tensor.matmul(st.ap[:, t * 512:t * 512 + w], lhsT=ones_b.ap,
                                                    rhs=qv.ap[:, t * 512:t * 512 + w], start=(mo == 0), stop=(mo == KC - 1))
                        return last
                    kb.MM(fn, [qv, ones_b], [st])
            pend = mk
        pend()
        postnorm(l, sub, DST, segs)

    def store_rows(srcs, ncol, dsts):
        n = len(srcs)
        sl = rslot()
        kb.MM(lambda: [nc.tensor.transpose(out=sl.ap[0:ncol, i * 128:(i + 1) * 128], in_=srcs[i].ap, identity=ident_f.ap)
                       for i in range(n)][-1], list(srcs) + [ident_f], [sl])
        ov = V(OST.ap[0:ncol, 0:n * 128], OST.bufs)
        kb.A(ov, sl[0:ncol, 0:n * 128], AF.Copy)
        for (r0, nr, dfn) in dsts:
            kb.dma(sp, dfn(), OST.ap[r0:r0 + nr, 0:n * 128], reads=[ov], is_out=True)

    def build_diag(l, c, bi):
        dgb = DGS[bi]
        dgv = V(dgb.ap.rearrange("p (k n) -> p k n", k=31), dgb.bufs)
        o = PO["cconv_w"] + c
        wv_ = PTv[:, l, o:o + 93:3].us(2).bc([128, 31, 128])
        kb.TT(dgv, ident_b.us(1).bc([128, 31, 128]), wv_, ALU.mult)
        return dgv

    def mix(l, q, segs):
        last = (q == NPIECE - 1)
        first = (q == 0)
        prenorm(l, 0, segs)
        dgs_built = [build_diag(l, 0, 0)]
        chk(f'mixP{q}_{l}')
        hreads = [H[s][k] for s in segs for k in range(KC)]

        def win_chunk(wv, c0):
            return mm_chunk(lambda k: wv.ap[:, k, c0:c0 + 128], KC,
                            lambda k, t: H["p"][k].ap[:, t * 512:(t + 1) * 512],
                            lambda k: H["s"][k].ap, segs, [wv] + hreads)

        if last:
            for tI in range(2):
                st = V(STG.ap[0:120, 0:256], STG.bufs)
                kb.dma(sp, st.ap, stp_d[l, tI * 120:(tI + 1) * 120, :], writes=[st])
                sl = rslot()
                kb.MM(lambda st=st, sl=sl: [nc.tensor.transpose(out=sl.ap[:, c * 120:(c + 1) * 120], in_=st.ap[:, c * 128:(c + 1) * 128],
                                                                 identity=ident_f.ap[0:120, 0:120]) for c in range(2)][-1],
                      [st, ident_f], [sl])
                for c in range(2):
                    dstv = UPSP[c][:, tI * 8:(tI + 1) * 8, :]
                    kb.A(dstv, sl[:, c * 120:(c + 1) * 120].re("p (b r) -> p b r", r=HP), AF.Copy)
            chk(f'mixS1{q}_{l}')
            st = V(STG.ap[0:32, 0:384], STG.bufs)
            kb.dma(sp, st.ap, sts_d[l, :, :], writes=[st])
            sl = rslot()
            kb.MM(lambda: [nc.tensor.transpose(out=sl.ap[:, c * 32:(c + 1) * 32], in_=st.ap[:, c * 128:(c + 1) * 128],
                                               identity=ident_f.ap[0:32, 0:32]) for c in range(3)][-1], [st, ident_f], [sl])
            for c in range(3):
                kb.A(ZSP[c], sl[:, c * 32:(c + 1) * 32].re("p (b r) -> p b r", r=HS), AF.Copy)
            chk(f'mixS2{q}_{l}')
            for c in range(3):
                sl = rslot()
                sts_ = []
                for tI in range(4):
                    st = V(STG.ap[0:120, tI * 384:(tI + 1) * 384], STG.bufs)
                    if c == 0:
                        kb.dma(sp, st.ap, stc_d[l, tI * 120:(tI + 1) * 120, :], writes=[st])
                    sts_.append(st)
                kb.MM(lambda sl=sl, c=c, sts_=sts_: [nc.tensor.transpose(out=sl.ap[:, tI * 120:(tI + 1) * 120],
                                                                          in_=sts_[tI].ap[:, c * 128:(c + 1) * 128],
                                                                          identity=ident_f.ap[0:120, 0:120]) for tI in range(4)][-1],
                      sts_ + [ident_f], [sl])
                chk(f'mixS3{q}_{l}_{c}')
                kb.CP(GL["sf"][c][:, :, 0:HC], sl[:, 0:480].re("p (b r) -> p b r", r=HC))
                chk(f'mixS4{q}_{l}_{c}')

        chk(f'mixA{q}_{l}')
        chk(f'mixB{q}_{l}')
        whb = take_w(win_d[l], 256)
        wcg = take_w(win_d[l], 256 + 768)
        wbg = take_w(win_d[l], 256 + 384)
        for c in range(3):
            zi = c % 2
            Z, Zs = ZB["pf"][zi], ZB["sf"][zi]
            o_hb = win_chunk(whb, c * 128)
            hb = ntmp()
            kb.A(hb[:, 0:NPC], o_hb["p"], AF.Copy)
            if "s" in segs:
                kb.A(hb[:, NPC:NCOL], o_hb["s"][:, 0:NS], AF.Copy)
            o_cg = win_chunk(wcg, c * 128)
            if first:
                kb.MS(Z[:, 0:HS], 0.0)
            else:
                kb.CP(Z[:, 0:HS], V(HALO_S[l].ap[:, c * HS:(c + 1) * HS], HALO_S[l].bufs))
            kb.TT(Z[:, HS:HS + NPC], o_cg["p"], hb[:, 0:NPC], ALU.mult)
            if not last:
                kb.CP(V(HALO_S[l].ap[:, c * HS:(c + 1) * HS], HALO_S[l].bufs), Z[:, NPC:NPC + HS])
            if "s" in segs:
                kb.CP(Zs[:, :, 0:HS], ZSP[c])
                kb.TT(Zs[:, :, HS:HS + 4], o_cg["s"][:, 0:NS].re("p (b j) -> p b j", j=4),
                      hb[:, NPC:NCOL].re("p (b j) -> p b j", j=4), ALU.mult)
            ca = ntmp()
            for (zv, cav, is3) in [(Z, ca[:, 0:NPC], False)] + ([(Zs, ca[:, NPC:NCOL].re("p (b j) -> p b j", j=4), True)] if "s" in segs else []):
                n = 4 if is3 else NPC
                for kk in range(3):
                    src = zv[:, :, kk:kk + n] if is3 else zv[:, kk:kk + n]
                    wk = pcol(l, "sconv_w", kk * 3 + c)
                    if kk == 0:
                        kb.TS(cav, src, wk, None, ALU.mult)
                    else:
                        kb.STT(cav, src, wk, cav, ALU.mult, ALU.add)
            if last:
                t = TAILS
                kb.CP(t[:, c * 96:c * 96 + HS], Z[:, NPC:NPC + HS])
                kb.CP(t[:, c * 96 + HS:c * 96 + HS + 32].re("p (j b) -> p j b", j=2), Zs[:, :, HS + 2:HS + 4].re("p b j -> p j b"))
                sconv_tails.append(t[:, c * 96:c * 96 + HS + 32])
            o_bg = win_chunk(wbg, c * 128)
            kb.TT(CAT["p"][2 + c], o_bg["p"], ca[:, 0:NPC], ALU.mult)
            if "s" in segs:
                kb.TT(CAT["s"][2 + c], o_bg["s"][:, 0:NS], ca[:, NPC:NCOL], ALU.mult)
        if last:
            store_rows(sconv_tails, HS + 32,
                       [(0, HS, lambda: osp_d[l, :, :])] +
                       [(HS + 16 * j, 16, (lambda j=j: oss_d[l, :, j, :])) for j in range(2)])
            sconv_tails.clear()

        dgs_built.append(build_diag(l, 1, 1))
        chk(f'mixC{q}_{l}')
        wac = take_w(win_d[l], 1408)
        wbc = take_w(win_d[l], 1792)
        gtv = V(GT.ap[:, 0:288].rearrange("p (c n) -> p c n", c=3), GT.bufs)
        for c in range(3):
            G, Gs = GL["pf"][c], GL["sf"][c]
            o_b = win_chunk(wbc, c * 128)
            sg = ntmp()
            kb.A(sg[:, 0:NPC], o_b["p"], AF.Sigmoid)
            if "s" in segs:
                kb.A(sg[:, NPC:NCOL], o_b["s"][:, 0:NS], AF.Sigmoid)
            o_a = win_chunk(wac, c * 128)
            if first:
                kb.MS(G[:, 0:HC], 0.0)
            else:
                kb.CP(G[:, 0:HC], V(HALO_C[l].ap[:, c * HC:(c + 1) * HC], HALO_C[l].bufs))
            kb.TT(G[:, HC:HC + NPC], o_a["p"], sg[:, 0:NPC], ALU.mult)
            if not last:
                kb.CP(V(HALO_C[l].ap[:, c * HC:(c + 1) * HC], HALO_C[l].bufs), G[:, NPC:NPC + HC])
            if "s" in segs:
                kb.TT(Gs[:, :, HC:HC + 4], o_a["s"][:, 0:NS].re("p (b j) -> p b j", j=4),
                      sg[:, NPC:NCOL].re("p (b j) -> p b j", j=4), ALU.mult)
            if last:
                kb.TT(gtv[:, c, 0:HC], o_a["p"][:, NPC - HC:NPC], sg[:, NPC - HC:NPC], ALU.mult)
                kb.TT(gtv[:, c, HC:HC + NS].re("p (j b) -> p j b", j=4), o_a["s"][:, 0:NS].re("p (b j) -> p j b", j=4),
                      sg[:, NPC:NCOL].re("p (b j) -> p j b", j=4), ALU.mult)
        if last:
            store_rows([gtv[:, c, 0:HC + NS] for c in range(3)], HC + NS,
                       [(0, HC, lambda: ocp_d[l, :, :])] +
                       [(HC + 16 * j, 16, (lambda j=j: ocs_d[l, :, 26 + j, :])) for j in range(4)])
            kb.dma(sp, ocs_d[l, :, 0:26, :], stc_d[l].rearrange("(b r) c -> b r c", r=HC)[:, 4:HC, :], is_out=True)
        wv = take_w(win_d[l], 0)
        for ch in range(2):
            outs = win_chunk(wv, ch * 128)
            E = UP["pf"][0]
            Es = UP["sf"][0]
            if first:
                kb.MS(E[:, 0:HP], 0.0)
            else:
                kb.CP(E[:, 0:HP], V(HALO_P[l].ap[:, ch * HP:(ch + 1) * HP], HALO_P[l].bufs))
            kb.A(E[:, HP:HP + NPC], outs["p"], AF.Copy)
            if not last:
                kb.CP(V(HALO_P[l].ap[:, ch * HP:(ch + 1) * HP], HALO_P[l].bufs), E[:, NPC:NPC + HP])
            if "s" in segs:
                kb.CP(Es[:, :, 0:HP], UPSP[ch])
                kb.A(Es[:, :, HP:HP + 4], outs["s"][:, 0:NS].re("p (b j) -> p b j", j=4), AF.Copy)
            L = HP + NPC
            views = [(E, W1["pf"][0], W2["pf"][0], L, None)]
            if "s" in segs:
                views.append((Es, W1["sf"][0], W2["sf"][0], HP + 4, 1))
            for (e, w1, w2, Ln, is3) in views:
                def sl_(v, a, b):
                    return v[:, :, a:b] if is3 else v[:, a:b]
                kb.TT(sl_(w1, 1, Ln), sl_(e, 1, Ln), sl_(e, 0, Ln - 1), ALU.add)
                kb.TT(sl_(w2, 3, Ln), sl_(w1, 3, Ln), sl_(w1, 1, Ln - 2), ALU.add)
                if ch == 1 and not (is3 and DBG_SKIP):
                    kb.TT(sl_(w1, 7, Ln), sl_(w2, 7, Ln), sl_(w2, 3, Ln - 4), ALU.add)
                    kb.TT(sl_(w2, 15, Ln), sl_(w1, 15, Ln), sl_(w1, 7, Ln - 8), ALU.add)
                for half, wsrc in ((0, w1), (1, w2)):
                    wdw = (2, 4, 8, 16)[2 * ch + half]
                    pr = slice(64 * half, 64 * half + 64)
                    if is3:
                        o_ = PLS[ch][pr, NPC:NCOL].re("p (b j) -> p b j", j=4)
                        kb.STT(o_, wsrc[pr, :, HP:HP + 4], 1.0 / wdw, e[pr, :, HP:HP + 4], ALU.mult, ALU.subtract)
                    else:
                        kb.STT(PLS[ch][pr, 0:NPC], wsrc[pr, HP:Ln], 1.0 / wdw, e[pr, HP:Ln], ALU.mult, ALU.subtract)
                        if first:
                            t = ntmp()
                            kb.TT(t[pr, 0:HP], wsrc[pr, HP:2 * HP], icv[pr, ch, :], ALU.mult)
                            kb.TT(PLS[ch][pr, 0:HP], t[pr, 0:HP], e[pr, HP:2 * HP], ALU.subtract)
            chk(f'mixB1{q}_{l}_{ch}')
            if last:
                t = TAILS
                kb.CP(t[:, ch * 96:ch * 96 + HP], E[:, NPC:NPC + HP])
                kb.CP(t[:, ch * 96 + HP:ch * 96 + HP + NS].re("p (j b) -> p j b", j=4), Es[:, :, HP:HP + 4].re("p b j -> p j b"))
                pool_tails.append(t[:, ch * 96:ch * 96 + HP + NS])
        chk(f'mixB3{q}_{l}')
        if last:
            store_rows(pool_tails, HP + NS,
                       [(0, HP, lambda: opp_d[l, :, :])] +
                       [(HP + 16 * j, 16, (lambda j=j: ops_d[l, :, 11 + j, :])) for j in range(4)])
            pool_tails.clear()
            chk(f'mixB4{q}_{l}')
            kb.dma(sp, ops_d[l, :, 0:11, :], stp_d[l].rearrange("(b r) c -> b r c", r=HP)[:, 4:HP, :], is_out=True)

        chk(f'mixD{q}_{l}')
        units = [("p", t) for t in range(NT)] + ([("s", 0)] if "s" in segs else [])
        NVB = NPC + NS
        vbv = V(VB.ap.rearrange("p (c n) -> p c n", c=3), VB.bufs)
        vqv = V(VQ.ap.rearrange("p (c n) -> p c n", c=3), VQ.bufs)
        for c in range(3):
            dgv = dgs_built[c]
            G, Gs = GL["pf"][c], GL["sf"][c]
            for (seg, t) in units:
                n = 512 if seg == "p" else NS
                o0 = t * 512 if seg == "p" else NPC
                if seg == "p":
                    sl = rslot()
                    kb.MM(lambda sl=sl, G=G, t=t, dgv=dgv: [nc.tensor.matmul(sl.ap[:, 0:512], lhsT=dgv.ap[:, kk, :],
                                                                     rhs=G.ap[:, kk + t * 512:kk + t * 512 + 512],
                                                                     start=(kk == 0), stop=(kk == 30)) for kk in range(31)][-1],
                          [dgv, G], [sl])
                    src = sl[:, 0:512]
                else:
                    sl = sslot()
                    kb.MM(lambda sl=sl, Gs=Gs, dgv=dgv: [nc.tensor.matmul(sl.ap[:, 0:NS], lhsT=dgv.ap[:, kk, :],
                                                                  rhs=Gs.ap[:, :, kk:kk + 4],
                                                                  start=(kk == 0), stop=(kk == 30)) for kk in range(31)][-1],
                          [dgv, Gs], [sl])
                    src = sl[:, 0:NS]
                kb.A(vbv[:, c, o0:o0 + n], src, AF.Identity, bias=pcol(l, "cconv_b", c))
                kb.A(vqv[:, c, o0:o0 + n], vbv[:, c, o0:o0 + n], AF.Square)
            if c == 0:
                dgs_built.append(build_diag(l, 2, 0))
        for ch in range(2):
            chk(f'mixB2{q}_{l}_{ch}')
            o2 = mm_chunk(lambda k: PWv.ap[:, l, ch, :], 1,
                          lambda k, t: PLS[ch].ap[:, t * 512:(t + 1) * 512], lambda k: PLS[ch].ap[:, NPC:NCOL], segs, [PWv, PLS[ch]])
            for seg in segs:
                kb.A(CAT[seg][ch], o2[seg] if seg == "p" else o2[seg][:, 0:NS], AF.Identity, scale=pcol(l, "pool_scale", ch))
            chk(f'mixB5{q}_{l}_{ch}')
        for (seg, t) in units:
            n = 512 if seg == "p" else NS
            o0 = t * 512 if seg == "p" else NPC
            vbu = vbv[:, :, o0:o0 + n]
            vqu = vqv[:, :, o0:o0 + n]
            s1 = rslot()
            s2 = rslot()
            for (sv, srcv) in ((s1, vbu), (s2, vqu)):
                kb.MM(lambda sv=sv, srcv=srcv, n=n: [nc.tensor.matmul(sv.ap[:, 0:n], lhsT=ones_f.ap, rhs=srcv.ap[:, c, :],
                                                                      start=(c == 0), stop=(c == 2)) for c in range(3)][-1],
                      [srcv, ones_f], [sv])
            kb.TS(LNM[:, 0:n], s1[:, 0:n], 1.0 / DC, None, ALU.mult)
            t_ = ntmp()
            kb.TT(t_[:, 0:n], LNM[:, 0:n], LNM[:, 0:n], ALU.mult)
            kb.STT(LNR[:, 0:n], s2[:, 0:n], 1.0 / DC, t_[:, 0:n], ALU.mult, ALU.subtract)
            kb.A(LNR[:, 0:n], LNR[:, 0:n], AF.Sqrt, bias=float(EPS))
            kb.RC(LNR[:, 0:n], LNR[:, 0:n])
            for c in range(3):
                kb.TT(vbu[:, c], vbu[:, c], LNM[:, 0:n], ALU.subtract)
                kb.TT(vbu[:, c], vbu[:, c], LNR[:, 0:n], ALU.mult)
                dst = CAT["p"][5 + c][:, t * 512:t * 512 + 512] if seg == "p" else CAT["s"][5 + c]
                kb.A(dst, vbu[:, c], AF.Silu, bias=pcol(l, "cln_b", c), scale=pcol(l, "cln_g", c))

        chk(f'mixE{q}_{l}')
        out_proj(l, 0, wout_d[l], KC, CAT, MX, segs)

    def ffn(l, q, segs):
        last = (q == NPIECE - 1)
        first = (q == 0)
        prenorm(l, 1, segs)
        hreads = [H[s][k] for s in segs for k in range(KC)]
        upsv = V(UPS.ap.rearrange("p (c b r) -> p c b r", c=NFC, b=16), UPS.bufs)
        fstvs = [V(f_.ap.rearrange("p (c n) -> p c n", c=11), f_.bufs) for f_ in FST]
        hfv = V(HALO_F[l].ap.rearrange("p (c r) -> p c r", c=NFC), HALO_F[l].bufs)
        if last:
            for rd in range(4):
                st = V(STG.ap[0:32, 0:1408], STG.bufs)
                kb.dma(sp, st.ap, stf_d[l, :, rd * 1408:(rd + 1) * 1408], writes=[st])
                sl = rslot()
                kb.MM(lambda st=st, sl=sl: [nc.tensor.transpose(out=sl.ap[:, c * 32:(c + 1) * 32], in_=st.ap[:, c * 128:(c + 1) * 128],
                                                                 identity=ident_f.ap[0:32, 0:32]) for c in range(11)][-1],
                      [st, ident_f], [sl])
                kb.A(upsv[:, rd * 11:(rd + 1) * 11], sl[:, 0:352].re("p (c b r) -> p c b r", c=11, b=16), AF.Copy)
        ub_i = 0
        ac_i = 0
        pt_i = 0
        if first:
            for ub in UPB:
                kb.MS(ub["pf"][0][:, 0:HF], 0.0)
        for j in range(NPAIR):
            wv = take_w(wup_d[l], j * 128, wl=1)
            accs = []
            for gv in range(2):
                ci = j + gv * NPAIR
                outs = mm_chunk(lambda k, gv=gv: wv[gv].ap[:, k, :], KC,
                                lambda k, t: H["p"][k].ap[:, t * 512:(t + 1) * 512],
                                lambda k: H["s"][k].ap, segs, [wv[gv]] + hreads)
                ub = UPB[ub_i % NUPB]; ub_i += 1
                acc = ACC[ac_i % NACC]; ac_i += 1
                U, Us = ub["pf"][0], ub["sf"][0]
                if not first:
                    kb.A(U[:, 0:HF], hfv[:, ci, :], AF.Copy)
                kb.A(U[:, HF:HF + NPC], outs["p"], AF.Copy)
                kb.A(acc[:, 0:NPC], outs["p"], AF.Identity, scale=pcol(l, "ffn_conv_w", 2 * NFC + ci))
                if not last:
                    kb.A(hfv[:, ci, :], U[:, NPC:NPC + HF], AF.Copy)
                if "s" in segs:
                    o4 = outs["s"][:, 0:NS].re("p (b j) -> p b j", j=4)
                    kb.A(Us[:, :, 0:HF], upsv[:, ci], AF.Copy)
                    kb.A(Us[:, :, HF:HF + 4], o4, AF.Copy)
                    kb.A(acc[:, NPC:NCOL], outs["s"][:, 0:NS], AF.Identity, scale=pcol(l, "ffn_conv_w", 2 * NFC + ci))
                if last:
                    rc = ci % 11
                    fstv = fstvs[gv]
                    kb.A(fstv[:, rc, 0:HF], outs["p"][:, NPC - HF:NPC], AF.Copy)
                    kb.A(fstv[:, rc, HF:HF + 32].re("p (j b) -> p j b", j=2), o4[:, :, 2:4].re("p b j -> p j b"), AF.Copy)
                for kk in (1, 0):
                    wk = pcol(l, "ffn_conv_w", kk * NFC + ci)
                    if True:
                        kb.STT(acc[:, 0:NPC], U[:, kk:kk + NPC], wk, acc[:, 0:NPC], ALU.mult, ALU.add)
                        if "s" in segs:
                            a4 = acc[:, NPC:NCOL].re("p (b j) -> p b j", j=4)
                            kb.STT(a4, Us[:, :, kk:kk + 4], wk, a4, ALU.mult, ALU.add)
                    else:
                        pt_ = PTMP[pt_i % 2]; pt_i += 1
                        kb.TS(pt_[:, 0:NPC], U[:, kk:kk + NPC], wk, None, ALU.mult, eng=pool)
                        if "s" in segs:
                            kb.TS(pt_[:, NPC:NCOL].re("p (b j) -> p b j", j=4), Us[:, :, kk:kk + 4], wk, None, ALU.mult, eng=pool)
                        nn_ = NCOL if "s" in segs else NPC
                        kb.TT(acc[:, 0:nn_], acc[:, 0:nn_], pt_[:, 0:nn_], ALU.add, eng=pool)
                accs.append(acc)
                if last and ci % 11 == 10:
                    rd = ci // 11
                    for g0 in range(0, 11, 4):
                        ng = min(4, 11 - g0)
                        cs = (rd * 11 + g0) * 128
                        store_rows([fstvs[gv][:, g0 + i, 0:34] for i in range(ng)], 34,
                                   [(0, HF, (lambda cs=cs, ng=ng: ofp_d[l, :, cs:cs + ng * 128]))] +
                                   [(HF + 16 * jj, 16, (lambda jj=jj, cs=cs, ng=ng: ofs_d[l, :, jj, cs:cs + ng * 128])) for jj in range(2)])
            nn = NCOL if "s" in segs else NPC
            kb.A(accs[0][:, 0:nn], accs[0][:, 0:nn], AF.Silu)
            kb.TT(V(AA["full"][:, j, 0:nn], [AA["bp"][j], AA["bs"][j]]), accs[0][:, 0:nn], accs[1][:, 0:nn], ALU.mult)
            if q == 0 and l + 1 < DEPTH and j < 16:
                mod_block(l + 1, j)
                if j == 15:
                    mod_finish(l + 1)
        out_proj(l, 1, wdn_d[l], NPAIR, AA, FF, segs)

    UPSP = [kb.sbn(f"upsp{c}", 16 * HP) for c in range(2)]
    UPSP = [V(u.ap.rearrange("p (b r) -> p b r", r=HP), u.bufs) for u in UPSP]
    ZSP = [kb.sbn(f"zsp{c}", 16 * HS) for c in range(3)]
    ZSP = [V(u.ap.rearrange("p (b r) -> p b r", r=HS), u.bufs) for u in ZSP]
    pool_tails = []
    sconv_tails = []
    print("SBUF words used (final)", kb.top, "of", kb.nwords)

    try:
        chk('setup')
        compute_mod(0)
        chk('mod0')
        for q in range(NPIECE):
            last = (q == NPIECE - 1)
            segs = ["p", "s"] if last else ["p"]
            blocks = [(xp_d, q * NPC + tb * 128, 128, tb * 128) for tb in range(NPC // 128)]
            if last:
                blocks.append((xs_d, 0, NS, NPC))
            for bi, (src, r0, nr, c0) in enumerate(blocks):
                io = IOX[bi % 2]
                iov = V(io.ap[0:nr, :], io.bufs)
                kb.dma(sp, iov.ap, src[r0:r0 + nr, :], writes=[iov])
                sl = rslot() if NT == 2 else None
                if NT == 2:
                    kb.MM(lambda iov=iov, sl=sl, nr=nr: [nc.tensor.transpose(out=sl.ap[:, k * 128:k * 128 + nr], in_=iov.ap[:, k * 128:(k + 1) * 128],
                                                                             identity=ident_f.ap[0:nr, 0:nr]) for k in range(KC)][-1],
                          [iov, ident_f], [sl])
                    dst = V(X["full"][:, :, c0:c0 + nr], X["bp"] + X["bs"])
                    kb.A(dst, sl[:, 0:KC * 128].re("p (k n) -> p k n", k=KC)[:, :, 0:nr], AF.Copy)
                else:
                    for hk in range(2):
                        sl = rslot()
                        kb.MM(lambda iov=iov, sl=sl, nr=nr, hk=hk: [nc.tensor.transpose(out=sl.ap[:, k * 128:k * 128 + nr],
                                                                                        in_=iov.ap[:, (hk * 4 + k) * 128:(hk * 4 + k + 1) * 128],
                                                                                        identity=ident_f.ap[0:nr, 0:nr]) for k in range(4)][-1],
                              [iov, ident_f], [sl])
                        dst = V(X["full"][:, hk * 4:hk * 4 + 4, c0:c0 + nr], X["bp"] + X["bs"])
                        kb.A(dst, sl[:, 0:512].re("p (k n) -> p k n", k=4)[:, :, 0:nr], AF.Copy)
            chk(f'xload{q}')
            for l in range(DEPTH):
                mix(l, q, segs)
                chk(f'mix{q}_{l}')
                ffn(l, q, segs)
                chk(f'ffn{q}_{l}')
            for bi, (src, r0, nr, c0) in enumerate(blocks):
                io = IOX[bi % 2]
                iov = V(io.ap[0:nr, :], io.bufs)
                for hk in range(2):
                    sl = rslot()
                    xr = [X["p"][hk * 4 + k] if c0 < NPC else X["s"][hk * 4 + k] for k in range(4)]
                    kb.MM(lambda sl=sl, nr=nr, c0=c0, hk=hk: [nc.tensor.transpose(out=sl.ap[0:nr, k * 128:(k + 1) * 128],
                                                                                  in_=X["full"][:, hk * 4 + k, c0:c0 + nr],
                                                                                  identity=ident_f.ap) for k in range(4)][-1],
                          xr + [ident_f], [sl])
                    kb.A(iov[:, hk * 512:(hk + 1) * 512], sl[0:nr, 0:512], AF.Copy)
                dstd = (yp_d if src is xp_d else ys_d)[r0:r0 + nr, :]
                kb.dma(sp, dstd, iov.ap, reads=[iov], is_out=True)
    except _Stop:
        pass

    best = {}
    for (s, v) in kb.out_toks:
        if best.get(s.key, (None, 0))[1] < v:
            best[s.key] = (s, v)
    for k, (s, v) in best.items():
        nc.sync.wait_ge(s.h, v)
    build.marks = kb.marks
    return nc


_NC = None


def kernel(x_prompt, x_sample, c_prompt, c_sample, state_pool, state_sconv, state_cconv, state_ffn,
           w_ada, b_ada, g_mix_pre, g_mix_post, g_ffn_pre, g_ffn_post,
           w_in, pool_w, pool_scale, sconv_w, cconv_w, cconv_b, cln_g, cln_b,
           w_out, w_up, ffn_conv_w, w_down):
    global _NC
    f = lambda a: np.ascontiguousarray(np.asarray(a, dtype=np.float32))
    x_prompt, x_sample, c_prompt, c_sample = map(f, (x_prompt, x_sample, c_prompt, c_sample))
    state_pool, state_sconv, state_cconv, state_ffn = map(f, (state_pool, state_sconv, state_cconv, state_ffn))
    rows = []
    for l in range(DEPTH):
        parts = [f(b_ada)[l].reshape(48, 128), f(g_mix_pre)[l].reshape(8, 128), f(g_mix_post)[l].reshape(8, 128),
                 f(g_ffn_pre)[l].reshape(8, 128), f(g_ffn_post)[l].reshape(8, 128), f(pool_scale)[l].reshape(2, 128),
                 f(sconv_w)[l].reshape(9, 128), f(cconv_w)[l].reshape(93, 128), f(cconv_b)[l].reshape(3, 128),
                 f(cln_g)[l].reshape(3, 128), f(cln_b)[l].reshape(3, 128), f(ffn_conv_w)[l].reshape(132, 128)]
        rows.append(np.concatenate(parts, axis=0))
    ptab = np.ascontiguousarray(np.stack(rows, 0))
    shared = {"w_ada": f(w_ada), "ptab": ptab, "w_in": f(w_in), "pool_w": f(pool_w), "w_out": f(w_out),
              "w_up": f(w_up), "w_down": f(w_down)}
    in_maps = []
    for i in range(NCORES):
        sl = slice(16 * i, 16 * i + 16)
        m = dict(shared)
        m["xp"] = x_prompt[i]
        m["xs"] = np.ascontiguousarray(x_sample[sl].reshape(NS, D))
        m["cc"] = np.ascontiguousarray(np.concatenate([c_prompt[i:i + 1], c_sample[sl]], axis=0))
        m["st_pool"] = np.ascontiguousarray(state_pool[:, sl].reshape(DEPTH, 16 * HP, DP))
        m["st_sconv"] = np.ascontiguousarray(state_sconv[:, sl].reshape(DEPTH, 16 * HS, DS))
        m["st_cconv"] = np.ascontiguousarray(state_cconv[:, sl].reshape(DEPTH, 16 * HC, DC))
        m["st_ffn"] = np.ascontiguousarray(state_ffn[:, sl].reshape(DEPTH, 16 * HF, 2 * DFF))
        in_maps.append(m)
    if _NC is None:
        _NC = build()
    res = run_bass_kernel_spmd(_NC, in_maps, core_ids=list(range(NCORES)))
    R = res.results
    cat = lambda k, ax: np.concatenate([np.asarray(R[i][k]) for i in range(NCORES)], axis=ax)
    yp = np.stack([np.asarray(R[i]["yp"]) for i in range(NCORES)], 0)
    ys = cat("ys", 0).reshape(128, 4, D)
    outs = [yp, ys]
    for k in ("o_pool_p", "o_sconv_p", "o_cconv_p", "o_ffn_p"):
        outs.append(np.stack([np.asarray(R[i][k]) for i in range(NCORES)], 1))
    for k in ("o_pool_s", "o_sconv_s", "o_cconv_s", "o_ffn_s"):
        outs.append(cat(k, 1))
    return tuple(np.ascontiguousarray(o.astype(np.float32)) for o in outs)
```

```python
import numpy as np
import concourse.bass as bass
import concourse.mybir as mybir
from concourse.bass_utils import run_bass_kernel_spmd

F32 = mybir.dt.float32
BF16 = mybir.dt.bfloat16
AF = mybir.ActivationFunctionType
ALU = mybir.AluOpType

NCORES = 8
D = 1024
KC = 8
DEPTH = 4
SEQ = 2048
NSEQ_S = 16
NS = 64
DP, DS, DC, DFF = 256, 384, 384, 2816
DIN = 2176
NFC = 44
NPAIR = 22
EPS = 1e-6
NPC = 512
NPIECE = SEQ // NPC
NT = NPC // 512
NCOL = NPC + NS
HP, HS, HC, HF = 15, 2, 30, 2

PO = {}
_o = 0
for _n, _r in (("b_ada", 48), ("g_mix_pre", 8), ("g_mix_post", 8), ("g_ffn_pre", 8), ("g_ffn_post", 8),
               ("pool_scale", 2), ("sconv_w", 9), ("cconv_w", 93), ("cconv_b", 3), ("cln_g", 3),
               ("cln_b", 3), ("ffn_conv_w", 132)):
    PO[_n] = _o
    _o += _r
NPT = _o
DBG_SKIP = False


class _Stop(Exception):
    pass


class Buf:
    __slots__ = ("name", "space", "lo", "hi", "w", "r", "ov")

    def __init__(self, name, space, lo, hi):
        self.name, self.space, self.lo, self.hi = name, space, lo, hi
        self.w = None
        self.r = {}
        self.ov = None


class V:
    def __init__(self, ap, bufs):
        self.ap = ap
        self.bufs = list(bufs)

    def __getitem__(self, key):
        return V(self.ap[key], self.bufs)

    def re(self, s, **kw):
        return V(self.ap.rearrange(s, **kw), self.bufs)

    def bc(self, shape):
        return V(self.ap.to_broadcast(shape), self.bufs)

    def us(self, axis):
        return V(self.ap.unsqueeze(axis), self.bufs)


class Sem:
    def __init__(self, h, key):
        self.h, self.key, self.total = h, key, 0


class Eng:
    def __init__(self, nc, name, h, is_pe=False):
        self.name, self.h, self.is_pe = name, h, is_pe
        self.sem = Sem(nc.alloc_semaphore("s_" + name), "s_" + name)
        self.count = 0
        self.seen = {}
        self.dsems = []
        self.di = 0


class KB:
    def __init__(self):
        nc = bass.Bass("TRN2", target_bir_lowering=False)
        self.nc = nc
        self.pe = Eng(nc, "pe", nc.tensor, True)
        self.act = Eng(nc, "act", nc.scalar)
        self.dve = Eng(nc, "dve", nc.vector)
        self.pool = Eng(nc, "pool", nc.gpsimd)
        self.sp = Eng(nc, "sp", nc.sync)
        for e, n in ((self.sp, 14), (self.pool, 12)):
            for i in range(n):
                e.dsems.append(Sem(nc.alloc_semaphore(f"d_{e.name}{i}"), f"d_{e.name}{i}"))
        self.bufs = {"sb": [], "ps": []}
        self.nwords = 53100
        self.arena = nc.alloc_sbuf_tensor("arena", [128, self.nwords], F32)
        self.psum = nc.alloc_psum_tensor("psum", [128, 4096], F32)
        self.top = 0
        self.out_toks = []
        self.opn = 0
        self.stop_n = 0
        self.marks = {}

    def mkbuf(self, name, space, lo, hi):
        b = Buf(name, space, lo, hi)
        lst = self.bufs[space]
        b.ov = [b]
        for o in lst:
            if o.lo < hi and lo < o.hi:
                o.ov.append(b)
                b.ov.append(o)
        lst.append(b)
        return b

    def alloc(self, words):
        lo = self.top
        self.top += words
        assert self.top <= self.nwords, f"SBUF arena overflow {self.top}"
        return lo

    def sb(self, name, lo, words, dt=F32, shape=None):
        b = self.mkbuf(name, "sb", lo, lo + words)
        ap = self.arena[:, lo:lo + words]
        if dt != F32:
            ap = ap.bitcast(dt)
        return V(ap, [b])

    def sbn(self, name, words, dt=F32):
        return self.sb(name, self.alloc(words), words, dt)

    def ps(self, name, lo, n):
        b = self.mkbuf(name, "ps", lo, lo + n)
        return V(self.psum[:, lo:lo + n], [b])

    def _deps(self, eng, reads, writes):
        toks = {}
        for b in reads:
            for ob in b.ov:
                if ob.w is not None:
                    s, v = ob.w
                    if toks.get(s.key, (None, 0))[1] < v:
                        toks[s.key] = (s, v)
        for b in writes:
            for ob in b.ov:
                if ob.w is not None:
                    s, v = ob.w
                    if toks.get(s.key, (None, 0))[1] < v:
                        toks[s.key] = (s, v)
                for k, (s, v) in ob.r.items():
                    if toks.get(k, (None, 0))[1] < v:
                        toks[k] = (s, v)
        for k, (s, v) in toks.items():
            if eng.is_pe and s is eng.sem:
                continue
            if eng.seen.get(k, 0) >= v:
                continue
            eng.h.wait_ge(s.h, v)
            eng.seen[k] = v

    def _commit(self, tok, reads, writes):
        s, v = tok
        ws = set(id(b) for b in writes)
        for b in writes:
            b.w = tok
            b.r = {}
        for b in reads:
            if id(b) in ws:
                continue
            if b.r.get(s.key, (None, 0))[1] < v:
                b.r[s.key] = (s, v)

    def emit(self, eng, fn, reads, writes):
        self.opn += 1
        if self.stop_n and self.opn == self.stop_n:
            raise _Stop()
        rb = [b for v in reads for b in v.bufs]
        wb = [b for v in writes for b in v.bufs]
        self._deps(eng, rb, wb)
        ins = fn()
        eng.count += 1
        ins.then_inc(eng.sem.h, 1)
        tok = (eng.sem, eng.count)
        self._commit(tok, rb, wb)
        return tok

    def dma(self, eng, out, in_, reads=(), writes=(), is_out=False):
        self.opn += 1
        if self.stop_n and self.opn == self.stop_n:
            raise _Stop()
        rb = [b for v in reads for b in v.bufs]
        wb = [b for v in writes for b in v.bufs]
        s = eng.dsems[eng.di % len(eng.dsems)]
        eng.di += 1
        if s.total > 0 and eng.seen.get(s.key, 0) < s.total:
            eng.h.wait_ge(s.h, s.total)
            eng.seen[s.key] = s.total
        self._deps(eng, rb, wb)
        ins = eng.h.dma_start(out=out, in_=in_)
        ins.then_inc(s.h, 16)
        s.total += 16
        tok = (s, s.total)
        self._commit(tok, rb, wb)
        if is_out:
            self.out_toks.append(tok)
        return tok

    def A(self, out, in_, func, bias=None, scale=None, eng=None):
        rd = [in_] + [x for x in (bias, scale) if isinstance(x, V)]
        kw = {}
        if bias is not None:
            kw["bias"] = bias.ap if isinstance(bias, V) else bias
        if scale is not None:
            kw["scale"] = scale.ap if isinstance(scale, V) else scale
        return self.emit(self.act, lambda: self.nc.scalar.activation(out=out.ap, in_=in_.ap, func=func, **kw),
                         rd, [out])

    def TS(self, out, in0, s1, s2, op0, op1=None, eng=None):
        eng = eng or self.dve
        rd = [in0] + [x for x in (s1, s2) if isinstance(x, V)]
        a1 = s1.ap if isinstance(s1, V) else s1
        a2 = s2.ap if isinstance(s2, V) else s2
        kw = {}
        if op1 is not None:
            kw["op1"] = op1
        return self.emit(eng, lambda: eng.h.tensor_scalar(out=out.ap, in0=in0.ap, scalar1=a1, scalar2=a2,
                                                          op0=op0, **kw), rd, [out])

    def TT(self, out, in0, in1, op, eng=None):
        eng = eng or self.dve
        return self.emit(eng, lambda: eng.h.tensor_tensor(out=out.ap, in0=in0.ap, in1=in1.ap, op=op),
                         [in0, in1], [out])

    def STT(self, out, in0, sc, in1, op0, op1, eng=None):
        eng = eng or self.dve
        rd = [in0, in1] + ([sc] if isinstance(sc, V) else [])
        a = sc.ap if isinstance(sc, V) else sc
        return self.emit(eng, lambda: eng.h.scalar_tensor_tensor(out=out.ap, in0=in0.ap, scalar=a, in1=in1.ap,
                                                                 op0=op0, op1=op1), rd, [out])

    def RC(self, out, in_):
        return self.emit(self.dve, lambda: self.nc.vector.reciprocal(out=out.ap, in_=in_.ap), [in_], [out])

    def CP(self, out, in_, eng=None):
        eng = eng or self.dve
        return self.emit(eng, lambda: eng.h.tensor_copy(out=out.ap, in_=in_.ap), [in_], [out])

    def MS(self, out, val, eng=None):
        eng = eng or self.dve
        return self.emit(eng, lambda: eng.h.memset(out.ap, val), [], [out])

    def MM(self, fn, reads, writes):
        return self.emit(self.pe, fn, reads, writes)


def build(stop=None):
    kb = KB()

    def chk(tag):
        kb.marks[tag] = kb.opn
        if stop is not None and tag == stop:
            raise _Stop()
    nc = kb.nc
    pe, act, dve, pool, sp = kb.pe, kb.act, kb.dve, kb.pool, kb.sp

    def din(name, shape):
        return nc.dram_tensor(name, list(shape), F32, kind="ExternalInput").ap()

    def dout(name, shape):
        return nc.dram_tensor(name, list(shape), F32, kind="ExternalOutput").ap()

    xp_d = din("xp", [SEQ, D]); xs_d = din("xs", [NS, D]); cc_d = din("cc", [17, D])
    stp_d = din("st_pool", [DEPTH, 16 * HP, DP]); sts_d = din("st_sconv", [DEPTH, 16 * HS, DS])
    stc_d = din("st_cconv", [DEPTH, 16 * HC, DC]); stf_d = din("st_ffn", [DEPTH, 16 * HF, 2 * DFF])
    wada_d = din("w_ada", [DEPTH, D, 6 * D]); pt_d = din("ptab", [DEPTH, NPT, 128])
    win_d = din("w_in", [DEPTH, D, DIN]); pw_d = din("pool_w", [DEPTH, 4, 64, 64])
    wout_d = din("w_out", [DEPTH, D, D]); wup_d = din("w_up", [DEPTH, D, 2 * DFF])
    wdn_d = din("w_down", [DEPTH, DFF, D])
    yp_d = dout("yp", [SEQ, D]); ys_d = dout("ys", [NS, D])
    opp_d = dout("o_pool_p", [DEPTH, HP, DP]); osp_d = dout("o_sconv_p", [DEPTH, HS, DS])
    ocp_d = dout("o_cconv_p", [DEPTH, HC, DC]); ofp_d = dout("o_ffn_p", [DEPTH, HF, 2 * DFF])
    ops_d = dout("o_pool_s", [DEPTH, 16, HP, DP]); oss_d = dout("o_sconv_s", [DEPTH, 16, HS, DS])
    ocs_d = dout("o_cconv_s", [DEPTH, 16, HC, DC]); ofs_d = dout("o_ffn_s", [DEPTH, 16, HF, 2 * DFF])

    wada_d = [wada_d[l] for l in range(DEPTH)]; win_d = [win_d[l] for l in range(DEPTH)]
    wout_d = [wout_d[l] for l in range(DEPTH)]; wup_d = [wup_d[l] for l in range(DEPTH)]; wdn_d = [wdn_d[l] for l in range(DEPTH)]

    W_X = KC * NCOL
    x_lo = kb.alloc(W_X)
    hc_lo = kb.alloc(W_X)
    a_lo = kb.alloc(NPAIR * NCOL // 2)
    A_WORDS = NPAIR * NCOL // 2

    def chunked(name, lo, nch, dt):
        wpc = NCOL if dt == F32 else NCOL // 2
        full = kb.arena[:, lo:lo + nch * wpc]
        if dt != F32:
            full = full.bitcast(dt)
        full = full.rearrange("p (c n) -> p c n", c=nch)
        res = {"full": full, "p": [], "s": [], "bp": [], "bs": []}
        pw = NPC if dt == F32 else NPC // 2
        for c in range(nch):
            bp = kb.mkbuf(f"{name}{c}p", "sb", lo + c * wpc, lo + c * wpc + pw)
            bs = kb.mkbuf(f"{name}{c}s", "sb", lo + c * wpc + pw, lo + (c + 1) * wpc)
            res["p"].append(V(full[:, c, 0:NPC], [bp]))
            res["s"].append(V(full[:, c, NPC:NCOL], [bs]))
            res["bp"].append(bp); res["bs"].append(bs)
        res["sall"] = V(full[:, :, NPC:NCOL], res["bs"])
        return res

    X = chunked("x", x_lo, KC, F32)
    H = chunked("h", hc_lo, KC, BF16)
    CAT = chunked("cat", hc_lo + W_X // 2, KC, BF16)
    FF = chunked("ff", hc_lo, KC, F32)
    MX = chunked("mx", a_lo, KC, F32)
    AA = chunked("a", a_lo, NPAIR, BF16)

    WSLOT = 1536
    NWS = 5
    wring = [kb.sbn(f"wr{i}", WSLOT, BF16) for i in range(NWS)]
    wr_i = [0]
    whalf = [[kb.sb(f"wrh{i}_{g}", w_.bufs[0].lo + g * 512, 512, BF16) for g in range(2)] for i, w_ in enumerate(wring)]

    PT = kb.sbn("pt", DEPTH * NPT)
    PTv = V(PT.ap.rearrange("p (l n) -> p l n", l=DEPTH), PT.bufs)
    DVA = [kb.sbn(f"dv{l}", 6 * KC * 17) for l in range(DEPTH)]
    ident_f = kb.sbn("ident_f", 128); ident_b = kb.sbn("ident_b", 64, BF16)
    ones_b = kb.sbn("ones_b", 64, BF16); ones_f = kb.sbn("ones_f", 128)
    invcnt = kb.sbn("invcnt", 32)
    PW = kb.sbn("pw", DEPTH * 2 * 64, BF16)
    PWv = V(PW.ap.rearrange("p (l c n) -> p l c n", l=DEPTH, c=2), PW.bufs)
    CTt = kb.sbn("ct", 80, BF16)
    CTv = V(CTt.ap[:, 0:KC * 17].rearrange("p (k s) -> p k s", k=KC), CTt.bufs)
    RSp = kb.sbn("rsp", NPC); RSs = kb.sbn("rss", NS)
    SQ = [kb.sbn(f"sq{i}", NCOL // 2, BF16) for i in range(3)]
    TMP = [kb.sbn(f"tmp{i}", NCOL) for i in range(4)]
    sq_i = [0]; tmp_i = [0]

    def nsq():
        sq_i[0] += 1
        return SQ[sq_i[0] % len(SQ)]

    def ntmp():
        tmp_i[0] += 1
        return TMP[tmp_i[0] % len(TMP)]

    HALO_P = [kb.sbn(f"hp{l}", 2 * HP) for l in range(DEPTH)]
    HALO_S = [kb.sbn(f"hs{l}", 3 * HS) for l in range(DEPTH)]
    HALO_C = [kb.sbn(f"hcv{l}", 3 * HC // 2, BF16) for l in range(DEPTH)]
    HALO_F = [kb.sbn(f"hf{l}", NFC * HF // 2, BF16) for l in range(DEPTH)]

    def padded(name, lo, nch, Hh, dt):
        cols = Hh + NPC + 16 * (Hh + 4)
        colsw = cols if dt == F32 else (cols + 1) // 2
        res = {"cols": cols, "H": Hh, "pf": [], "sf": [], "words": nch * colsw}
        for c in range(nch):
            l0 = lo + c * colsw
            ap = kb.arena[:, l0:l0 + colsw]
            if dt != F32:
                ap = ap.bitcast(dt)
            pcw = (Hh + NPC) if dt == F32 else (Hh + NPC) // 2
            bp = kb.mkbuf(f"{name}{c}p", "sb", l0, l0 + pcw)
            bs = kb.mkbuf(f"{name}{c}s", "sb", l0 + pcw, l0 + colsw)
            res["pf"].append(V(ap[:, 0:Hh + NPC], [bp]))
            res["sf"].append(V(ap[:, Hh + NPC:Hh + NPC + 16 * (Hh + 4)].rearrange("p (b j) -> p b j", j=Hh + 4), [bs]))
        return res

    MODT = kb.sbn("modt", 48 * 17 + 16)
    STG = kb.sbn("stg", 1536)
    OST = kb.sbn("ost", 1536)
    IOX = [kb.sb(f"iox{i}", a_lo + i * 1024, 1024) for i in range(2)]
    m_lo = kb.top
    mlo = m_lo
    UP = padded("up", mlo, 1, HP, F32); mlo += UP["words"]
    W1 = padded("w1", mlo, 1, HP, F32); mlo += W1["words"]
    W2 = padded("w2", mlo, 1, HP, F32); mlo += W2["words"]
    PLS = []
    for i in range(2):
        PLS.append(kb.sb(f"pl{i}", mlo, NCOL // 2, BF16)); mlo += NCOL // 2
    ZB = padded("zb", mlo, 2, HS, F32); mlo += ZB["words"]
    GL = padded("gl", mlo, 3, HC, BF16); mlo += GL["words"]
    VB = kb.sb("vb", mlo, 3 * NCOL); mlo += 3 * NCOL
    VQ = kb.sb("vq", mlo, 3 * NCOL); mlo += 3 * NCOL
    LNM = kb.sb("lnm", mlo, 512); mlo += 512
    LNR = kb.sb("lnr", mlo, 512); mlo += 512
    GT = kb.sb("gt", mlo, 3 * 96); mlo += 3 * 96
    TAILS = kb.sb("tails", mlo, 3 * 96); mlo += 3 * 96
    DGS = []
    for i in range(2):
        DGS.append(kb.sb(f"dg{i}", mlo, 31 * 64, BF16)); mlo += 31 * 64
    m_hi = mlo
    flo = m_lo
    UPBw = (HF + NPC + 16 * (HF + 4)) // 2
    UPB = []
    NUPB, NACC = 6, 8
    for i in range(NUPB):
        UPB.append(padded(f"upb{i}", flo, 1, HF, BF16)); flo += UPBw
    ACC = []
    for i in range(NACC):
        ACC.append(kb.sb(f"acc{i}", flo, NCOL)); flo += NCOL
    PTMP = []
    for i in range(2):
        PTMP.append(kb.sb(f"ptmp{i}", flo, NCOL)); flo += NCOL
    FST = []
    for i in range(2):
        FST.append(kb.sb(f"fst{i}", flo, 11 * 34)); flo += 11 * 34
    UPS = kb.sb("ups", flo, NFC * 16 * HF // 2, BF16); flo += NFC * 16 * HF // 2
    kb.top = max(m_hi, flo)
    assert kb.top <= kb.nwords, f"SBUF overflow {kb.top}"
    print("SBUF words used", kb.top, "of", kb.nwords)

    RB = 8 - NT - 2
    nslots = RB // NT
    ring = [kb.ps(f"ring{i}", i * NT * 512, NT * 512) for i in range(nslots)]
    if NT == 1:
        ring.append(kb.ps("ring_b6", 6 * 512, 512))
        nslots += 1
    ring_i = [0]
    STATP = kb.ps("statp", RB * 512, NT * 512)
    STATS = kb.ps("stats", 7 * 512, 64)

    def rslot():
        ring_i[0] += 1
        return ring[ring_i[0] % nslots]

    def sslot():
        return rslot()

    def wslot():
        wr_i[0] += 1
        return wring[wr_i[0] % NWS]

    wplan = []

    def P_mod(l):
        for blk in range(16):
            wplan.append((wada_d[l], D, [(blk * 384, 384, 0)], 384))

    NMODMIX = 4

    def P_mix(l, modl=None):
        for (c0, n) in ((256, 384), (256 + 768, 384), (256 + 384, 384), (1408, 384), (1792, 384), (0, 256)):
            wplan.append((win_d[l], D, [(c0, n, 0)], n))
        if modl is not None:
            for blk in range(NMODMIX):
                wplan.append((wada_d[modl], D, [(blk * 384, 384, 0)], 384))
        for mo in range(KC):
            wplan.append((wout_d[l], D, [(mo * 128, 128, 0)], 128))

    def P_ffn(l, modl=None):
        for j in range(NPAIR):
            wplan.append((wup_d[l], D, [(j * 128, 128, "pair")], 256))
            if modl is not None and j < 16 - NMODMIX:
                wplan.append((wada_d[modl], D, [((NMODMIX + j) * 384, 384, 0)], 384))
        for mo in range(KC):
            wplan.append((wdn_d[l], DFF, [(mo * 128, 128, 0)], 128))

    P_mod(0)
    for q_ in range(NPIECE):
        for l_ in range(DEPTH):
            P_mix(l_, (l_ + 1) if (q_ == 0 and l_ + 1 < DEPTH) else None)
            P_ffn(l_, (l_ + 1) if (q_ == 0 and l_ + 1 < DEPTH) else None)
    wst = {"issued": 0, "taken": 0, "views": {}}
    WLIVE = 3

    def w_issue(j):
        dram2d, nrows, parts, tcols = wplan[j]
        nkc = nrows // 128
        slot = wring[j % NWS]
        view = V(slot.ap[:, 0:nkc * tcols].rearrange("p (k c) -> p k c", k=nkc), slot.bufs)
        for (c0, ncols, coff) in parts:
            if coff == "pair":
                hv = []
                for g in range(2):
                    hb_ = whalf[j % NWS][g]
                    hview = V(hb_.ap.rearrange("p (k c) -> p k c", k=nkc), hb_.bufs)
                    src = dram2d[0:nrows, g * DFF + c0:g * DFF + c0 + ncols].rearrange("(k p) c -> p k c", p=128)
                    kb.dma(pool, hview.ap, src, writes=[hview])
                    hv.append(hview)
                view = hv
                continue
            src = dram2d[0:nrows, c0:c0 + ncols].rearrange("(k p) c -> p k c", p=128)
            kb.dma(pool, view.ap[:, :, coff:coff + ncols], src, writes=[view])
        wst["views"][j] = view

    def take_w(dram2d, c0, wl=WLIVE):
        i = wst["taken"]
        assert wplan[i][0] is dram2d and wplan[i][2][0][0] == c0, (i, c0, wplan[i][2])
        lim = min(len(wplan) - 1, i + NWS - wl)
        while wst["issued"] <= lim:
            w_issue(wst["issued"])
            wst["issued"] += 1
        wst["taken"] += 1
        return wst["views"].pop(i)

    iot = V(STG.ap[:, 0:128].bitcast(mybir.dt.int32), STG.bufs)
    kb.emit(pool, lambda: nc.gpsimd.iota(iot.ap, pattern=[[1, 128]], base=0, channel_multiplier=-1), [], [iot])
    kb.CP(ident_f, iot)
    kb.TS(ident_f, ident_f, 0.0, None, ALU.is_equal)
    kb.CP(ident_b, ident_f)
    kb.MS(ones_b, 1.0)
    kb.MS(ones_f, 1.0)
    icv = V(invcnt.ap[:, 0:30].rearrange("p (c t) -> p c t", c=2), invcnt.bufs)
    for ch in range(2):
        for half in range(2):
            w = (2, 4, 8, 16)[2 * ch + half]
            ps_ = slice(64 * half, 64 * half + 64)
            kb.MS(icv[ps_, ch, :], 1.0 / w)
            for t in range(w - 1):
                kb.MS(icv[ps_, ch, t:t + 1], 1.0 / (t + 1))
    for l in range(DEPTH):
        r = 0
        while r < NPT:
            n = min(128, NPT - r)
            st = V(STG.ap[0:n, 0:128], STG.bufs)
            kb.dma(sp, st.ap, pt_d[l, r:r + n, :], writes=[st])
            sl = rslot()
            kb.MM(lambda st=st, sl=sl, n=n: nc.tensor.transpose(out=sl.ap[:, 0:n], in_=st.ap, identity=ident_f.ap[0:n, 0:n]),
                  [st, ident_f], [sl])
            kb.A(PTv[:, l, r:r + n], sl[:, 0:n], AF.Copy)
            r += n
    kb.MS(PW, 0.0)
    for l in range(DEPTH):
        for g in range(4):
            ch, half = g // 2, g % 2
            dst = PWv[64 * half:64 * half + 64, l, ch, 64 * half:64 * half + 64]
            kb.dma(pool, dst.ap, pw_d[l, g, :, :], writes=[PWv])
    cst = V(STG.ap[0:17, 0:1024], STG.bufs)
    kb.dma(sp, cst.ap, cc_d[:, :], writes=[cst])
    kb.A(cst, cst, AF.Silu)
    sl = rslot()
    kb.MM(lambda: [nc.tensor.transpose(out=sl.ap[:, k * 17:(k + 1) * 17], in_=cst.ap[:, k * 128:(k + 1) * 128],
                                       identity=ident_f.ap[0:17, 0:17]) for k in range(KC)][-1],
          [cst, ident_f], [sl])
    kb.A(CTv, sl[:, 0:KC * 17].re("p (k s) -> p k s", k=KC), AF.Copy)

    def pcol(l, name, idx):
        o = PO[name] + idx
        return PTv[:, l, o:o + 1]

    def mod_block(l, blk):
        modv = V(MODT.ap[:, 0:48 * 17].rearrange("p (c s) -> p c s", c=48), MODT.bufs)
        wv = take_w(wada_d[l], blk * 384, wl=1)
        sl = rslot()

        def fn(wv=wv, sl=sl):
            last = None
            for j in range(3):
                for k in range(KC):
                    last = nc.tensor.matmul(sl.ap[:, j * 17:(j + 1) * 17], lhsT=wv.ap[:, k, j * 128:(j + 1) * 128],
                                            rhs=CTv.ap[:, k, :], start=(k == 0), stop=(k == KC - 1))
            return last
        kb.MM(fn, [wv, CTv], [sl])
        o = PO["b_ada"] + blk * 3
        kb.TT(modv[:, blk * 3:blk * 3 + 3, :], sl[:, 0:51].re("p (c s) -> p c s", c=3),
              PTv[:, l, o:o + 3].us(2).bc([128, 3, 17]), ALU.add)

    def mod_finish(l):
        modv = V(MODT.ap[:, 0:48 * 17].rearrange("p (c s) -> p c s", c=48), MODT.bufs)
        dv = V(DVA[l].ap.rearrange("p (w c s) -> p w c s", w=6, c=KC), DVA[l].bufs)
        for sub, (gpre, gpost) in enumerate((("g_mix_pre", "g_mix_post"), ("g_ffn_pre", "g_ffn_post"))):
            m0 = sub * 24
            gp = PTv[:, l, PO[gpre]:PO[gpre] + 8].us(2).bc([128, 8, 17])
            gq = PTv[:, l, PO[gpost]:PO[gpost] + 8].us(2).bc([128, 8, 17])
            kb.TS(dv[:, 3 * sub + 0], modv[:, m0 + 8:m0 + 16, :], 1.0, 32.0, ALU.add, ALU.mult)
            kb.TT(dv[:, 3 * sub + 0], dv[:, 3 * sub + 0], gp, ALU.mult)
            kb.CP(dv[:, 3 * sub + 1], modv[:, m0:m0 + 8, :])
            kb.TS(dv[:, 3 * sub + 2], modv[:, m0 + 16:m0 + 24, :], 32.0, None, ALU.mult)
            kb.TT(dv[:, 3 * sub + 2], dv[:, 3 * sub + 2], gq, ALU.mult)

    def compute_mod(l):
        for blk in range(16):
            mod_block(l, blk)
        mod_finish(l)

    def dvv(l):
        return V(DVA[l].ap.rearrange("p (w c s) -> p w c s", w=6, c=KC), DVA[l].bufs)

    def mm_chunk(lhs_fn, nk, rhs_p, rhs_s, segs, reads, ksplit=None, reads_hi=None):
        outs = {}
        slp = rslot()
        outs["p"] = slp
        wr = [slp]
        if "s" in segs:
            sls = sslot()
            outs["s"] = sls
            wr.append(sls)

        def fn():
            last = None
            for k in range(nk):
                lh = lhs_fn(k)
                for t in range(NT):
                    last = nc.tensor.matmul(slp.ap[:, t * 512:(t + 1) * 512], lhsT=lh, rhs=rhs_p(k, t),
                                            start=(k == 0), stop=(k == nk - 1))
            if "s" in segs and not DBG_SKIP:
                for k in range(nk):
                    last = nc.tensor.matmul(sls.ap[:, 0:NS], lhsT=lhs_fn(k), rhs=rhs_s(k),
                                            start=(k == 0), stop=(k == nk - 1))
            return last
        if ksplit is not None:
            def fn_lo():
                last = None
                for k in range(ksplit):
                    for t in range(NT):
                        last = nc.tensor.matmul(slp.ap[:, t * 512:(t + 1) * 512], lhsT=lhs_fn(k), rhs=rhs_p(k, t),
                                                start=(k == 0), stop=False)
                return last

            def fn_hi():
                last = None
                for k in range(ksplit, nk):
                    for t in range(NT):
                        last = nc.tensor.matmul(slp.ap[:, t * 512:(t + 1) * 512], lhsT=lhs_fn(k), rhs=rhs_p(k, t),
                                                start=False, stop=(k == nk - 1))
                if "s" in segs:
                    for k in range(nk):
                        last = nc.tensor.matmul(sls.ap[:, 0:NS], lhsT=lhs_fn(k), rhs=rhs_s(k),
                                                start=(k == 0), stop=(k == nk - 1))
                return last
            kb.MM(fn_lo, reads, [slp])
            kb.MM(fn_hi, reads_hi, wr)
            return outs
        kb.MM(fn, reads, wr)
        return outs

    def rms_stats(src, segs, sq_from_psum=None):
        for seg in segs:
            for k in range(KC):
                q = nsq()
                n = NPC if seg == "p" else NS
                qv = q[:, 0:n]
                kb.A(qv, src[seg][k], AF.Square)
                st = STATP if seg == "p" else STATS

                def fn(qv=qv, st=st, n=n, k=k):
                    last = None
                    for t in range(max(1, n // 512)):
                        w = min(512, n)
                        last = nc.tensor.matmul(st.ap[:, t * 512:t * 512 + w], lhsT=ones_b.ap, rhs=qv.ap[:, t * 512:t * 512 + w],
                                                start=(k == 0), stop=(k == KC - 1))
                    return last
                kb.MM(fn, [qv, ones_b], [st])

    def rstd_from_stats(segs):
        for seg in segs:
            st, rs = (STATP, RSp) if seg == "p" else (STATS, RSs)
            kb.A(rs, st, AF.Sqrt, bias=float(D * EPS))
            kb.RC(rs, rs)

    def prenorm(l, sub, segs):
        dv = dvv(l)
        rms_stats(X, segs)
        rstd_from_stats(segs)
        for k in range(KC):
            t = ntmp()
            kb.TT(t[:, 0:NPC], X["p"][k], RSp, ALU.mult)
            kb.A(H["p"][k], t[:, 0:NPC], AF.Identity, bias=dv[:, 3 * sub + 1, k, 0:1], scale=dv[:, 3 * sub + 0, k, 0:1])
        if "s" in segs:
            t = ntmp()
            tv = t[:, 0:KC * NS].re("p (k n) -> p k n", k=KC) if KC * NS <= NCOL else None
            kb.TT(tv, X["sall"], RSs.us(1).bc([128, KC, NS]), ALU.mult)
            t4 = tv.re("p k (b j) -> p k b j", j=4)
            kb.TT(t4, t4, dv[:, 3 * sub + 0, :, 1:17].us(3).bc([128, KC, 16, 4]), ALU.mult)
            kb.TT(H["sall"].re("p k (b j) -> p k b j", j=4), t4,
                  dv[:, 3 * sub + 1, :, 1:17].us(3).bc([128, KC, 16, 4]), ALU.add)

    def postnorm(l, sub, SRC, segs):
        dv = dvv(l)
        rstd_from_stats(segs)
        for k in range(KC):
            t = ntmp()
            kb.TT(t[:, 0:NPC], SRC["p"][k], RSp, ALU.mult)
            kb.TT(X["p"][k], X["p"][k], t[:, 0:NPC], ALU.add)
        if "s" in segs:
            t = ntmp()
            tv = t[:, 0:KC * NS].re("p (k n) -> p k n", k=KC)
            kb.TT(tv, SRC["sall"], RSs.us(1).bc([128, KC, NS]), ALU.mult)
            t4 = tv.re("p k (b j) -> p k b j", j=4)
            kb.TT(t4, t4, dv[:, 3 * sub + 2, :, 1:17].us(3).bc([128, KC, 16, 4]), ALU.mult)
            kb.TT(X["sall"], X["sall"], tv, ALU.add)

    def out_proj(l, sub, wd, nk, rhs_src, DST, segs):
        pend = None
        for mo in range(KC):
            wv = take_w(wd, mo * 128, wl=1)
            if nk > 16 and mo == 0:
                ks = 16
                outs = mm_chunk(lambda k, wv=wv: wv.ap[:, k, :], nk,
                                lambda k, t: rhs_src["p"][k].ap[:, t * 512:(t + 1) * 512],
                                lambda k: rhs_src["s"][k].ap, segs,
                                [wv] + [rhs_src["p"][k] for k in range(ks)], ksplit=ks,
                                reads_hi=[wv] + [rhs_src["p"][k] for k in range(ks, nk)] +
                                         [rhs_src[s][k] for s in segs if s == "s" for k in range(nk)])
            else:
                outs = mm_chunk(lambda k, wv=wv: wv.ap[:, k, :], nk,
                                lambda k, t: rhs_src["p"][k].ap[:, t * 512:(t + 1) * 512],
                                lambda k: rhs_src["s"][k].ap, segs,
                                [wv] + [rhs_src[s][k] for s in segs for k in range(nk)])
            if pend is not None:
                pend()
            qs = {}
            for seg in segs:
                if seg == "p":
                    kb.A(DST[seg][mo], outs[seg], AF.Identity, scale=dvv(l)[:, 3 * sub + 2, mo, 0:1])
                else:
                    kb.A(DST[seg][mo], outs[seg][:, 0:NS], AF.Copy)
                q = nsq()
                n = NPC if seg == "p" else NS
                qs[seg] = q[:, 0:n]
                kb.A(qs[seg], outs[seg] if seg == "p" else outs[seg][:, 0:NS], AF.Square)

            def mk(qs=qs, mo=mo):
                for seg in segs:
                    st = STATP if seg == "p" else STATS
                    n = NPC if seg == "p" else NS
                    qv = qs[seg]

                    def fn(qv=qv, st=st, n=n):
                        last = None
                        for t in range(max(1, n // 512)):
                            w = min(512, n)
                            last = nc.tensor.matmul(st.ap[:, t * 512:t * 512 + w], lhsT=ones_b.ap,
                                                    rhs=qv.ap[:, t * 512:t * 512 + w], start=(mo == 0), stop=(mo == KC - 1))
                        return last
                    kb.MM(fn, [qv, ones_b], [st])
            pend = mk
        pend()
        postnorm(l, sub, DST, segs)

    def store_rows(srcs, ncol, dsts):
        n = len(srcs)
        sl = rslot()
        kb.MM(lambda: [nc.tensor.transpose(out=sl.ap[0:ncol, i * 128:(i + 1) * 128], in_=srcs[i].ap, identity=ident_f.ap)
                       for i in range(n)][-1], list(srcs) + [ident_f], [sl])
        ov = V(OST.ap[0:ncol, 0:n * 128], OST.bufs)
        kb.A(ov, sl[0:ncol, 0:n * 128], AF.Copy)
        for (r0, nr, dfn) in dsts:
            kb.dma(sp, dfn(), OST.ap[r0:r0 + nr, 0:n * 128], reads=[ov], is_out=True)

    def build_diag(l, c, bi):
        dgb = DGS[bi]
        dgv = V(dgb.ap.rearrange("p (k n) -> p k n", k=31), dgb.bufs)
        o = PO["cconv_w"] + c
        wv_ = PTv[:, l, o:o + 93:3].us(2).bc([128, 31, 128])
        kb.TT(dgv, ident_b.us(1).bc([128, 31, 128]), wv_, ALU.mult)
        return dgv

    def mix(l, q, segs):
        last = (q == NPIECE - 1)
        first = (q == 0)
        prenorm(l, 0, segs)
        dgs_built = [build_diag(l, 0, 0)]
        chk(f'mixP{q}_{l}')
        hreads = [H[s][k] for s in segs for k in range(KC)]

        def win_chunk(wv, c0):
            return mm_chunk(lambda k: wv.ap[:, k, c0:c0 + 128], KC,
                            lambda k, t: H["p"][k].ap[:, t * 512:(t + 1) * 512],
                            lambda k: H["s"][k].ap, segs, [wv] + hreads)

        if last:
            for tI in range(2):
                st = V(STG.ap[0:120, 0:256], STG.bufs)
                kb.dma(sp, st.ap, stp_d[l, tI * 120:(tI + 1) * 120, :], writes=[st])
                sl = rslot()
                kb.MM(lambda st=st, sl=sl: [nc.tensor.transpose(out=sl.ap[:, c * 120:(c + 1) * 120], in_=st.ap[:, c * 128:(c + 1) * 128],
                                                                 identity=ident_f.ap[0:120, 0:120]) for c in range(2)][-1],
                      [st, ident_f], [sl])
                for c in range(2):
                    dstv = UPSP[c][:, tI * 8:(tI + 1) * 8, :]
                    kb.A(dstv, sl[:, c * 120:(c + 1) * 120].re("p (b r) -> p b r", r=HP), AF.Copy)
            chk(f'mixS1{q}_{l}')
            st = V(STG.ap[0:32, 0:384], STG.bufs)
            kb.dma(sp, st.ap, sts_d[l, :, :], writes=[st])
            sl = rslot()
            kb.MM(lambda: [nc.tensor.transpose(out=sl.ap[:, c * 32:(c + 1) * 32], in_=st.ap[:, c * 128:(c + 1) * 128],
                                               identity=ident_f.ap[0:32, 0:32]) for c in range(3)][-1], [st, ident_f], [sl])
            for c in range(3):
                kb.A(ZSP[c], sl[:, c * 32:(c + 1) * 32].re("p (b r) -> p b r", r=HS), AF.Copy)
            chk(f'mixS2{q}_{l}')
            for c in range(3):
                sl = rslot()
                sts_ = []
                for tI in range(4):
                    st = V(STG.ap[0:120, tI * 384:(tI + 1) * 384], STG.bufs)
                    if c == 0:
                        kb.dma(sp, st.ap, stc_d[l, tI * 120:(tI + 1) * 120, :], writes=[st])
                    sts_.append(st)
                kb.MM(lambda sl=sl, c=c, sts_=sts_: [nc.tensor.transpose(out=sl.ap[:, tI * 120:(tI + 1) * 120],
                                                                          in_=sts_[tI].ap[:, c * 128:(c + 1) * 128],
                                                                          identity=ident_f.ap[0:120, 0:120]) for tI in range(4)][-1],
                      sts_ + [ident_f], [sl])
                chk(f'mixS3{q}_{l}_{c}')
                kb.CP(GL["sf"][c][:, :, 0:HC], sl[:, 0:480].re("p (b r) -> p b r", r=HC))
                chk(f'mixS4{q}_{l}_{c}')

        chk(f'mixA{q}_{l}')
        chk(f'mixB{q}_{l}')
        whb = take_w(win_d[l], 256)
        wcg = take_w(win_d[l], 256 + 768)
        wbg = take_w(win_d[l], 256 + 384)
        for c in range(3):
            zi = c % 2
            Z, Zs = ZB["pf"][zi], ZB["sf"][zi]
            o_hb = win_chunk(whb, c * 128)
            hb = ntmp()
            kb.A(hb[:, 0:NPC], o_hb["p"], AF.Copy)
            if "s" in segs:
                kb.A(hb[:, NPC:NCOL], o_hb["s"][:, 0:NS], AF.Copy)
            o_cg = win_chunk(wcg, c * 128)
            if first:
                kb.MS(Z[:, 0:HS], 0.0)
            else:
                kb.CP(Z[:, 0:HS], V(HALO_S[l].ap[:, c * HS:(c + 1) * HS], HALO_S[l].bufs))
            kb.TT(Z[:, HS:HS + NPC], o_cg["p"], hb[:, 0:NPC], ALU.mult)
            if not last:
                kb.CP(V(HALO_S[l].ap[:, c * HS:(c + 1) * HS], HALO_S[l].bufs), Z[:, NPC:NPC + HS])
            if "s" in segs:
                kb.CP(Zs[:, :, 0:HS], ZSP[c])
                kb.TT(Zs[:, :, HS:HS + 4], o_cg["s"][:, 0:NS].re("p (b j) -> p b j", j=4),
                      hb[:, NPC:NCOL].re("p (b j) -> p b j", j=4), ALU.mult)
            ca = ntmp()
            for (zv, cav, is3) in [(Z, ca[:, 0:NPC], False)] + ([(Zs, ca[:, NPC:NCOL].re("p (b j) -> p b j", j=4), True)] if "s" in segs else []):
                n = 4 if is3 else NPC
                for kk in range(3):
                    src = zv[:, :, kk:kk + n] if is3 else zv[:, kk:kk + n]
                    wk = pcol(l, "sconv_w", kk * 3 + c)
                    if kk == 0:
                        kb.TS(cav, src, wk, None, ALU.mult)
                    else:
                        kb.STT(cav, src, wk, cav, ALU.mult, ALU.add)
            if last:
                t = TAILS
                kb.CP(t[:, c * 96:c * 96 + HS], Z[:, NPC:NPC + HS])
                kb.CP(t[:, c * 96 + HS:c * 96 + HS + 32].re("p (j b) -> p j b", j=2), Zs[:, :, HS + 2:HS + 4].re("p b j -> p j b"))
                sconv_tails.append(t[:, c * 96:c * 96 + HS + 32])
            o_bg = win_chunk(wbg, c * 128)
            kb.TT(CAT["p"][2 + c], o_bg["p"], ca[:, 0:NPC], ALU.mult)
            if "s" in segs:
                kb.TT(CAT["s"][2 + c], o_bg["s"][:, 0:NS], ca[:, NPC:NCOL], ALU.mult)
        if last:
            store_rows(sconv_tails, HS + 32,
                       [(0, HS, lambda: osp_d[l, :, :])] +
                       [(HS + 16 * j, 16, (lambda j=j: oss_d[l, :, j, :])) for j in range(2)])
            sconv_tails.clear()

        dgs_built.append(build_diag(l, 1, 1))
        chk(f'mixC{q}_{l}')
        wac = take_w(win_d[l], 1408)
        wbc = take_w(win_d[l], 1792)
        gtv = V(GT.ap[:, 0:288].rearrange("p (c n) -> p c n", c=3), GT.bufs)
        for c in range(3):
            G, Gs = GL["pf"][c], GL["sf"][c]
            o_b = win_chunk(wbc, c * 128)
            sg = ntmp()
            kb.A(sg[:, 0:NPC], o_b["p"], AF.Sigmoid)
            if "s" in segs:
                kb.A(sg[:, NPC:NCOL], o_b["s"][:, 0:NS], AF.Sigmoid)
            o_a = win_chunk(wac, c * 128)
            if first:
                kb.MS(G[:, 0:HC], 0.0)
            else:
                kb.CP(G[:, 0:HC], V(HALO_C[l].ap[:, c * HC:(c + 1) * HC], HALO_C[l].bufs))
            kb.TT(G[:, HC:HC + NPC], o_a["p"], sg[:, 0:NPC], ALU.mult)
            if not last:
                kb.CP(V(HALO_C[l].ap[:, c * HC:(c + 1) * HC], HALO_C[l].bufs), G[:, NPC:NPC + HC])
            if "s" in segs:
                kb.TT(Gs[:, :, HC:HC + 4], o_a["s"][:, 0:NS].re("p (b j) -> p b j", j=4),
                      sg[:, NPC:NCOL].re("p (b j) -> p b j", j=4), ALU.mult)
            if last:
                kb.TT(gtv[:, c, 0:HC], o_a["p"][:, NPC - HC:NPC], sg[:, NPC - HC:NPC], ALU.mult)
                kb.TT(gtv[:, c, HC:HC + NS].re("p (j b) -> p j b", j=4), o_a["s"][:, 0:NS].re("p (b j) -> p j b", j=4),
                      sg[:, NPC:NCOL].re("p (b j) -> p j b", j=4), ALU.mult)
        if last:
            store_rows([gtv[:, c, 0:HC + NS] for c in range(3)], HC + NS,
                       [(0, HC, lambda: ocp_d[l, :, :])] +
                       [(HC + 16 * j, 16, (lambda j=j: ocs_d[l, :, 26 + j, :])) for j in range(4)])
            kb.dma(sp, ocs_d[l, :, 0:26, :], stc_d[l].rearrange("(b r) c -> b r c", r=HC)[:, 4:HC, :], is_out=True)
        wv = take_w(win_d[l], 0)
        for ch in range(2):
            outs = win_chunk(wv, ch * 128)
            E = UP["pf"][0]
            Es = UP["sf"][0]
            if first:
                kb.MS(E[:, 0:HP], 0.0)
            else:
                kb.CP(E[:, 0:HP], V(HALO_P[l].ap[:, ch * HP:(ch + 1) * HP], HALO_P[l].bufs))
            kb.A(E[:, HP:HP + NPC], outs["p"], AF.Copy)
            if not last:
                kb.CP(V(HALO_P[l].ap[:, ch * HP:(ch + 1) * HP], HALO_P[l].bufs), E[:, NPC:NPC + HP])
            if "s" in segs:
                kb.CP(Es[:, :, 0:HP], UPSP[ch])
                kb.A(Es[:, :, HP:HP + 4], outs["s"][:, 0:NS].re("p (b j) -> p b j", j=4), AF.Copy)
            L = HP + NPC
            views = [(E, W1["pf"][0], W2["pf"][0], L, None)]
            if "s" in segs:
                views.append((Es, W1["sf"][0], W2["sf"][0], HP + 4, 1))
            for (e, w1, w2, Ln, is3) in views:
                def sl_(v, a, b):
                    return v[:, :, a:b] if is3 else v[:, a:b]
                kb.TT(sl_(w1, 1, Ln), sl_(e, 1, Ln), sl_(e, 0, Ln - 1), ALU.add)
                kb.TT(sl_(w2, 3, Ln), sl_(w1, 3, Ln), sl_(w1, 1, Ln - 2), ALU.add)
                if ch == 1 and not (is3 and DBG_SKIP):
                    kb.TT(sl_(w1, 7, Ln), sl_(w2, 7, Ln), sl_(w2, 3, Ln - 4), ALU.add)
                    kb.TT(sl_(w2, 15, Ln), sl_(w1, 15, Ln), sl_(w1, 7, Ln - 8), ALU.add)
                for half, wsrc in ((0, w1), (1, w2)):
                    wdw = (2, 4, 8, 16)[2 * ch + half]
                    pr = slice(64 * half, 64 * half + 64)
                    if is3:
                        o_ = PLS[ch][pr, NPC:NCOL].re("p (b j) -> p b j", j=4)
                        kb.STT(o_, wsrc[pr, :, HP:HP + 4], 1.0 / wdw, e[pr, :, HP:HP + 4], ALU.mult, ALU.subtract)
                    else:
                        kb.STT(PLS[ch][pr, 0:NPC], wsrc[pr, HP:Ln], 1.0 / wdw, e[pr, HP:Ln], ALU.mult, ALU.subtract)
                        if first:
                            t = ntmp()
                            kb.TT(t[pr, 0:HP], wsrc[pr, HP:2 * HP], icv[pr, ch, :], ALU.mult)
                            kb.TT(PLS[ch][pr, 0:HP], t[pr, 0:HP], e[pr, HP:2 * HP], ALU.subtract)
            chk(f'mixB1{q}_{l}_{ch}')
            if last:
                t = TAILS
                kb.CP(t[:, ch * 96:ch * 96 + HP], E[:, NPC:NPC + HP])
                kb.CP(t[:, ch * 96 + HP:ch * 96 + HP + NS].re("p (j b) -> p j b", j=4), Es[:, :, HP:HP + 4].re("p b j -> p j b"))
                pool_tails.append(t[:, ch * 96:ch * 96 + HP + NS])
        chk(f'mixB3{q}_{l}')
        if last:
            store_rows(pool_tails, HP + NS,
                       [(0, HP, lambda: opp_d[l, :, :])] +
                       [(HP + 16 * j, 16, (lambda j=j: ops_d[l, :, 11 + j, :])) for j in range(4)])
            pool_tails.clear()
            chk(f'mixB4{q}_{l}')
            kb.dma(sp, ops_d[l, :, 0:11, :], stp_d[l].rearrange("(b r) c -> b r c", r=HP)[:, 4:HP, :], is_out=True)

        chk(f'mixD{q}_{l}')
        units = [("p", t) for t in range(NT)] + ([("s", 0)] if "s" in segs else [])
        NVB = NPC + NS
        vbv = V(VB.ap.rearrange("p (c n) -> p c n", c=3), VB.bufs)
        vqv = V(VQ.ap.rearrange("p (c n) -> p c n", c=3), VQ.bufs)
        for c in range(3):
            dgv = dgs_built[c]
            G, Gs = GL["pf"][c], GL["sf"][c]
            for (seg, t) in units:
                n = 512 if seg == "p" else NS
                o0 = t * 512 if seg == "p" else NPC
                if seg == "p":
                    sl = rslot()
                    kb.MM(lambda sl=sl, G=G, t=t, dgv=dgv: [nc.tensor.matmul(sl.ap[:, 0:512], lhsT=dgv.ap[:, kk, :],
                                                                     rhs=G.ap[:, kk + t * 512:kk + t * 512 + 512],
                                                                     start=(kk == 0), stop=(kk == 30)) for kk in range(31)][-1],
                          [dgv, G], [sl])
                    src = sl[:, 0:512]
                else:
                    sl = sslot()
                    kb.MM(lambda sl=sl, Gs=Gs, dgv=dgv: [nc.tensor.matmul(sl.ap[:, 0:NS], lhsT=dgv.ap[:, kk, :],
                                                                  rhs=Gs.ap[:, :, kk:kk + 4],
                                                                  start=(kk == 0), stop=(kk == 30)) for kk in range(31)][-1],
                          [dgv, Gs], [sl])
                    src = sl[:, 0:NS]
                kb.A(vbv[:, c, o0:o0 + n], src, AF.Identity, bias=pcol(l, "cconv_b", c))
                kb.A(vqv[:, c, o0:o0 + n], vbv[:, c, o0:o0 + n], AF.Square)
            if c == 0:
                dgs_built.append(build_diag(l, 2, 0))
        if q == 0 and l + 1 < DEPTH:
            for blk in range(NMODMIX):
                mod_block(l + 1, blk)
        for ch in range(2):
            chk(f'mixB2{q}_{l}_{ch}')
            o2 = mm_chunk(lambda k: PWv.ap[:, l, ch, :], 1,
                          lambda k, t: PLS[ch].ap[:, t * 512:(t + 1) * 512], lambda k: PLS[ch].ap[:, NPC:NCOL], segs, [PWv, PLS[ch]])
            for seg in segs:
                kb.A(CAT[seg][ch], o2[seg] if seg == "p" else o2[seg][:, 0:NS], AF.Identity, scale=pcol(l, "pool_scale", ch))
            chk(f'mixB5{q}_{l}_{ch}')
        for (seg, t) in units:
            n = 512 if seg == "p" else NS
            o0 = t * 512 if seg == "p" else NPC
            vbu = vbv[:, :, o0:o0 + n]
            vqu = vqv[:, :, o0:o0 + n]
            s1 = rslot()
            s2 = rslot()
            for (sv, srcv) in ((s1, vbu), (s2, vqu)):
                kb.MM(lambda sv=sv, srcv=srcv, n=n: [nc.tensor.matmul(sv.ap[:, 0:n], lhsT=ones_f.ap, rhs=srcv.ap[:, c, :],
                                                                      start=(c == 0), stop=(c == 2)) for c in range(3)][-1],
                      [srcv, ones_f], [sv])
            kb.TS(LNM[:, 0:n], s1[:, 0:n], 1.0 / DC, None, ALU.mult)
            t_ = ntmp()
            kb.TT(t_[:, 0:n], LNM[:, 0:n], LNM[:, 0:n], ALU.mult)
            kb.STT(LNR[:, 0:n], s2[:, 0:n], 1.0 / DC, t_[:, 0:n], ALU.mult, ALU.subtract)
            kb.A(LNR[:, 0:n], LNR[:, 0:n], AF.Sqrt, bias=float(EPS))
            kb.RC(LNR[:, 0:n], LNR[:, 0:n])
            for c in range(3):
                kb.TT(vbu[:, c], vbu[:, c], LNM[:, 0:n], ALU.subtract)
                kb.TT(vbu[:, c], vbu[:, c], LNR[:, 0:n], ALU.mult)
                dst = CAT["p"][5 + c][:, t * 512:t * 512 + 512] if seg == "p" else CAT["s"][5 + c]
                kb.A(dst, vbu[:, c], AF.Silu, bias=pcol(l, "cln_b", c), scale=pcol(l, "cln_g", c))

        chk(f'mixE{q}_{l}')
        out_proj(l, 0, wout_d[l], KC, CAT, MX, segs)

    def ffn(l, q, segs):
        last = (q == NPIECE - 1)
        first = (q == 0)
        prenorm(l, 1, segs)
        hreads = [H[s][k] for s in segs for k in range(KC)]
        upsv = V(UPS.ap.rearrange("p (c b r) -> p c b r", c=NFC, b=16), UPS.bufs)
        fstvs = [V(f_.ap.rearrange("p (c n) -> p c n", c=11), f_.bufs) for f_ in FST]
        hfv = V(HALO_F[l].ap.rearrange("p (c r) -> p c r", c=NFC), HALO_F[l].bufs)
        if last:
            for rd in range(4):
                st = V(STG.ap[0:32, 0:1408], STG.bufs)
                kb.dma(sp, st.ap, stf_d[l, :, rd * 1408:(rd + 1) * 1408], writes=[st])
                sl = rslot()
                kb.MM(lambda st=st, sl=sl: [nc.tensor.transpose(out=sl.ap[:, c * 32:(c + 1) * 32], in_=st.ap[:, c * 128:(c + 1) * 128],
                                                                 identity=ident_f.ap[0:32, 0:32]) for c in range(11)][-1],
                      [st, ident_f], [sl])
                kb.A(upsv[:, rd * 11:(rd + 1) * 11], sl[:, 0:352].re("p (c b r) -> p c b r", c=11, b=16), AF.Copy)
        ub_i = 0
        ac_i = 0
        pt_i = 0
        if first:
            for ub in UPB:
                kb.MS(ub["pf"][0][:, 0:HF], 0.0)
        for j in range(NPAIR):
            wv = take_w(wup_d[l], j * 128, wl=1)
            accs = []
            for gv in range(2):
                ci = j + gv * NPAIR
                outs = mm_chunk(lambda k, gv=gv: wv[gv].ap[:, k, :], KC,
                                lambda k, t: H["p"][k].ap[:, t * 512:(t + 1) * 512],
                                lambda k: H["s"][k].ap, segs, [wv[gv]] + hreads)
                ub = UPB[ub_i % NUPB]; ub_i += 1
                acc = ACC[ac_i % NACC]; ac_i += 1
                U, Us = ub["pf"][0], ub["sf"][0]
                if not first:
                    kb.A(U[:, 0:HF], hfv[:, ci, :], AF.Copy)
                kb.A(U[:, HF:HF + NPC], outs["p"], AF.Copy)
                kb.A(acc[:, 0:NPC], outs["p"], AF.Identity, scale=pcol(l, "ffn_conv_w", 2 * NFC + ci))
                if not last:
                    kb.A(hfv[:, ci, :], U[:, NPC:NPC + HF], AF.Copy)
                if "s" in segs:
                    o4 = outs["s"][:, 0:NS].re("p (b j) -> p b j", j=4)
                    kb.A(Us[:, :, 0:HF], upsv[:, ci], AF.Copy)
                    kb.A(Us[:, :, HF:HF + 4], o4, AF.Copy)
                    kb.A(acc[:, NPC:NCOL], outs["s"][:, 0:NS], AF.Identity, scale=pcol(l, "ffn_conv_w", 2 * NFC + ci))
                if last:
                    rc = ci % 11
                    fstv = fstvs[gv]
                    kb.A(fstv[:, rc, 0:HF], outs["p"][:, NPC - HF:NPC], AF.Copy)
                    kb.A(fstv[:, rc, HF:HF + 32].re("p (j b) -> p j b", j=2), o4[:, :, 2:4].re("p b j -> p j b"), AF.Copy)
                for kk in (1, 0):
                    wk = pcol(l, "ffn_conv_w", kk * NFC + ci)
                    if True:
                        kb.STT(acc[:, 0:NPC], U[:, kk:kk + NPC], wk, acc[:, 0:NPC], ALU.mult, ALU.add)
                        if "s" in segs:
                            a4 = acc[:, NPC:NCOL].re("p (b j) -> p b j", j=4)
                            kb.STT(a4, Us[:, :, kk:kk + 4], wk, a4, ALU.mult, ALU.add)
                    else:
                        pt_ = PTMP[pt_i % 2]; pt_i += 1
                        kb.TS(pt_[:, 0:NPC], U[:, kk:kk + NPC], wk, None, ALU.mult, eng=pool)
                        if "s" in segs:
                            kb.TS(pt_[:, NPC:NCOL].re("p (b j) -> p b j", j=4), Us[:, :, kk:kk + 4], wk, None, ALU.mult, eng=pool)
                        nn_ = NCOL if "s" in segs else NPC
                        kb.TT(acc[:, 0:nn_], acc[:, 0:nn_], pt_[:, 0:nn_], ALU.add, eng=pool)
                accs.append(acc)
                if last and ci % 11 == 10:
                    rd = ci // 11
                    for g0 in range(0, 11, 4):
                        ng = min(4, 11 - g0)
                        cs = (rd * 11 + g0) * 128
                        store_rows([fstvs[gv][:, g0 + i, 0:34] for i in range(ng)], 34,
                                   [(0, HF, (lambda cs=cs, ng=ng: ofp_d[l, :, cs:cs + ng * 128]))] +
                                   [(HF + 16 * jj, 16, (lambda jj=jj, cs=cs, ng=ng: ofs_d[l, :, jj, cs:cs + ng * 128])) for jj in range(2)])
            nn = NCOL if "s" in segs else NPC
            kb.A(accs[0][:, 0:nn], accs[0][:, 0:nn], AF.Silu)
            kb.TT(V(AA["full"][:, j, 0:nn], [AA["bp"][j], AA["bs"][j]]), accs[0][:, 0:nn], accs[1][:, 0:nn], ALU.mult)
            if q == 0 and l + 1 < DEPTH and j < 16 - NMODMIX:
                mod_block(l + 1, NMODMIX + j)
                if j == 16 - NMODMIX - 1:
                    mod_finish(l + 1)
        out_proj(l, 1, wdn_d[l], NPAIR, AA, FF, segs)

    UPSP = [kb.sbn(f"upsp{c}", 16 * HP) for c in range(2)]
    UPSP = [V(u.ap.rearrange("p (b r) -> p b r", r=HP), u.bufs) for u in UPSP]
    ZSP = [kb.sbn(f"zsp{c}", 16 * HS) for c in range(3)]
    ZSP = [V(u.ap.rearrange("p (b r) -> p b r", r=HS), u.bufs) for u in ZSP]
    pool_tails = []
    sconv_tails = []
    print("SBUF words used (final)", kb.top, "of", kb.nwords)

    try:
        chk('setup')
        compute_mod(0)
        chk('mod0')
        for q in range(NPIECE):
            last = (q == NPIECE - 1)
            segs = ["p", "s"] if last else ["p"]
            blocks = [(xp_d, q * NPC + tb * 128, 128, tb * 128) for tb in range(NPC // 128)]
            if last:
                blocks.append((xs_d, 0, NS, NPC))
            for bi, (src, r0, nr, c0) in enumerate(blocks):
                io = IOX[bi % 2]
                iov = V(io.ap[0:nr, :], io.bufs)
                kb.dma(sp, iov.ap, src[r0:r0 + nr, :], writes=[iov])
                sl = rslot() if NT == 2 else None
                if NT == 2:
                    kb.MM(lambda iov=iov, sl=sl, nr=nr: [nc.tensor.transpose(out=sl.ap[:, k * 128:k * 128 + nr], in_=iov.ap[:, k * 128:(k + 1) * 128],
                                                                             identity=ident_f.ap[0:nr, 0:nr]) for k in range(KC)][-1],
                          [iov, ident_f], [sl])
                    dst = V(X["full"][:, :, c0:c0 + nr], X["bp"] + X["bs"])
                    kb.A(dst, sl[:, 0:KC * 128].re("p (k n) -> p k n", k=KC)[:, :, 0:nr], AF.Copy)
                else:
                    for hk in range(2):
                        sl = rslot()
                        kb.MM(lambda iov=iov, sl=sl, nr=nr, hk=hk: [nc.tensor.transpose(out=sl.ap[:, k * 128:k * 128 + nr],
                                                                                        in_=iov.ap[:, (hk * 4 + k) * 128:(hk * 4 + k + 1) * 128],
                                                                                        identity=ident_f.ap[0:nr, 0:nr]) for k in range(4)][-1],
                              [iov, ident_f], [sl])
                        dst = V(X["full"][:, hk * 4:hk * 4 + 4, c0:c0 + nr], X["bp"] + X["bs"])
                        kb.A(dst, sl[:, 0:512].re("p (k n) -> p k n", k=4)[:, :, 0:nr], AF.Copy)
            chk(f'xload{q}')
            for l in range(DEPTH):
                mix(l, q, segs)
                chk(f'mix{q}_{l}')
                ffn(l, q, segs)
                chk(f'ffn{q}_{l}')
            for bi, (src, r0, nr, c0) in enumerate(blocks):
                io = IOX[bi % 2]
                iov = V(io.ap[0:nr, :], io.bufs)
                for hk in range(2):
                    sl = rslot()
                    xr = [X["p"][hk * 4 + k] if c0 < NPC else X["s"][hk * 4 + k] for k in range(4)]
                    kb.MM(lambda sl=sl, nr=nr, c0=c0, hk=hk: [nc.tensor.transpose(out=sl.ap[0:nr, k * 128:(k + 1) * 128],
                                                                                  in_=X["full"][:, hk * 4 + k, c0:c0 + nr],
                                                                                  identity=ident_f.ap) for k in range(4)][-1],
                          xr + [ident_f], [sl])
                    kb.A(iov[:, hk * 512:(hk + 1) * 512], sl[0:nr, 0:512], AF.Copy)
                dstd = (yp_d if src is xp_d else ys_d)[r0:r0 + nr, :]
                kb.dma(sp, dstd, iov.ap, reads=[iov], is_out=True)
    except _Stop:
        pass

    best = {}
    for (s, v) in kb.out_toks:
        if best.get(s.key, (None, 0))[1] < v:
            best[s.key] = (s, v)
    for k, (s, v) in best.items():
        nc.sync.wait_ge(s.h, v)
    build.marks = kb.marks
    return nc


_NC = None


def kernel(x_prompt, x_sample, c_prompt, c_sample, state_pool, state_sconv, state_cconv, state_ffn,
           w_ada, b_ada, g_mix_pre, g_mix_post, g_ffn_pre, g_ffn_post,
           w_in, pool_w, pool_scale, sconv_w, cconv_w, cconv_b, cln_g, cln_b,
           w_out, w_up, ffn_conv_w, w_down):
    global _NC
    f = lambda a: np.ascontiguousarray(np.asarray(a, dtype=np.float32))
    x_prompt, x_sample, c_prompt, c_sample = map(f, (x_prompt, x_sample, c_prompt, c_sample))
    state_pool, state_sconv, state_cconv, state_ffn = map(f, (state_pool, state_sconv, state_cconv, state_ffn))
    rows = []
    for l in range(DEPTH):
        parts = [f(b_ada)[l].reshape(48, 128), f(g_mix_pre)[l].reshape(8, 128), f(g_mix_post)[l].reshape(8, 128),
                 f(g_ffn_pre)[l].reshape(8, 128), f(g_ffn_post)[l].reshape(8, 128), f(pool_scale)[l].reshape(2, 128),
                 f(sconv_w)[l].reshape(9, 128), f(cconv_w)[l].reshape(93, 128), f(cconv_b)[l].reshape(3, 128),
                 f(cln_g)[l].reshape(3, 128), f(cln_b)[l].reshape(3, 128), f(ffn_conv_w)[l].reshape(132, 128)]
        rows.append(np.concatenate(parts, axis=0))
    ptab = np.ascontiguousarray(np.stack(rows, 0))
    shared = {"w_ada": f(w_ada), "ptab": ptab, "w_in": f(w_in), "pool_w": f(pool_w), "w_out": f(w_out),
              "w_up": f(w_up), "w_down": f(w_down)}
    in_maps = []
    for i in range(NCORES):
        sl = slice(16 * i, 16 * i + 16)
        m = dict(shared)
        m["xp"] = x_prompt[i]
        m["xs"] = np.ascontiguousarray(x_sample[sl].reshape(NS, D))
        m["cc"] = np.ascontiguousarray(np.concatenate([c_prompt[i:i + 1], c_sample[sl]], axis=0))
        m["st_pool"] = np.ascontiguousarray(state_pool[:, sl].reshape(DEPTH, 16 * HP, DP))
        m["st_sconv"] = np.ascontiguousarray(state_sconv[:, sl].reshape(DEPTH, 16 * HS, DS))
        m["st_cconv"] = np.ascontiguousarray(state_cconv[:, sl].reshape(DEPTH, 16 * HC, DC))
        m["st_ffn"] = np.ascontiguousarray(state_ffn[:, sl].reshape(DEPTH, 16 * HF, 2 * DFF))
        in_maps.append(m)
    if _NC is None:
        _NC = build()
    res = run_bass_kernel_spmd(_NC, in_maps, core_ids=list(range(NCORES)))
    R = res.results
    cat = lambda k, ax: np.concatenate([np.asarray(R[i][k]) for i in range(NCORES)], axis=ax)
    yp = np.stack([np.asarray(R[i]["yp"]) for i in range(NCORES)], 0)
    ys = cat("ys", 0).reshape(128, 4, D)
    outs = [yp, ys]
    for k in ("o_pool_p", "o_sconv_p", "o_cconv_p", "o_ffn_p"):
        outs.append(np.stack([np.asarray(R[i][k]) for i in range(NCORES)], 1))
    for k in ("o_pool_s", "o_sconv_s", "o_cconv_s", "o_ffn_s"):
        outs.append(cat(k, 1))
    return tuple(np.ascontiguousarray(o.astype(np.float32)) for o in outs)
```
